# Optimizing a Trainium2 kernel written in Bass

```python
import math
import jax, jax.numpy as jnp
from jax import lax
import numpy as np

D_MODEL = 2048
BATCH = 1
SEQ = 8192
DEPTH = 4

BRANCH_WIDTH = D_MODEL // 2
N_BRANCHES = 3
MLSTM_HEADS = 4
MLSTM_HEAD_DIM = BRANCH_WIDTH // MLSTM_HEADS
MLSTM_CHUNK = 128
N_MLSTM_GATES = 2 * 2 * MLSTM_HEADS
HYENA_WIDTH = BRANCH_WIDTH
HYENA_BANDS = 16
HYENA_EMB = 2 * HYENA_BANDS + 1
HYENA_FILTER_HIDDEN = 64
S5_WIDTH = BRANCH_WIDTH
S5_GROUP = 16
S5_GROUPS = S5_WIDTH // S5_GROUP
S5_STATE = 64
PEER_HEADS = 8
PEER_KEYS = 128
PEER_TOPK = 16
PEER_QDIM = 256
PEER_HALF = PEER_QDIM // 2
N_EXPERTS = PEER_KEYS * PEER_KEYS
PEER_BLOCK = 128
IN_COLS = 4 * BRANCH_WIDTH + N_MLSTM_GATES + 3 * HYENA_WIDTH + S5_WIDTH + N_BRANCHES * D_MODEL
ALPHA = (2 * DEPTH) ** 0.25
BETA = (8 * DEPTH) ** -0.25
LN_EPS = 1e-5
F32 = jnp.float32

kernel_name = 'hybrid_mlstm_hyena_s5_peer_encoder'


def _layer_norm(x, g, b):
    xf = x.astype(F32)
    mu = xf.mean(-1, keepdims=True)
    var = jnp.square(xf - mu).mean(-1, keepdims=True)
    y = (xf - mu) * lax.rsqrt(var + LN_EPS)
    return (y * g.astype(F32) + b.astype(F32)).astype(x.dtype)


def _split_in(proj):
    sizes = (BRANCH_WIDTH,) * 4 + (N_MLSTM_GATES, 3 * HYENA_WIDTH, S5_WIDTH, N_BRANCHES * D_MODEL)
    idx = np.cumsum(sizes)[:-1].tolist()
    return jnp.split(proj, idx, axis=-1)


def _mlstm_chunk_step(carry, inp):
    c_state, n_state, m_state = carry
    q, k, v, log_i, log_f = inp
    L = q.shape[2]
    b = jnp.cumsum(log_f, axis=-1)
    tri = jnp.tril(jnp.ones((L, L), dtype=bool))
    log_d = jnp.where(tri, b[..., :, None] - b[..., None, :] + log_i[..., None, :], -jnp.inf)
    m_inter = b + m_state[..., None]
    m_t = jnp.maximum(m_inter, log_d.max(-1))
    inter = jnp.exp(m_inter - m_t)
    s = jnp.einsum('bhtd,bhsd->bhts', q, k) * jnp.exp(log_d - m_t[..., None])
    num = jnp.einsum('bhts,bhse->bhte', s, v) + inter[..., None] * jnp.einsum('bhtd,bhde->bhte', q, c_state)
    den = s.sum(-1) + inter * jnp.einsum('bhtd,bhd->bht', q, n_state)
    h = num / jnp.maximum(jnp.abs(den), jnp.exp(-m_t))[..., None]
    b_last = b[..., -1]
    log_w = b_last[..., None] - b + log_i
    m_new = jnp.maximum(b_last + m_state, log_w.max(-1))
    w = jnp.exp(log_w - m_new[..., None])
    decay = jnp.exp(b_last + m_state - m_new)
    c_new = decay[..., None, None] * c_state + jnp.einsum('bhs,bhsd,bhse->bhde', w, k, v)
    n_new = decay[..., None] * n_state + jnp.einsum('bhs,bhsd->bhd', w, k)
    return (c_new, n_new, m_new), h


def _mlstm_scan(q, k, v, log_i, log_f):
    bsz, nh, s, d = q.shape
    nc = s // MLSTM_CHUNK

    def chunks(t):
        t = t.reshape(bsz, nh, nc, MLSTM_CHUNK, *t.shape[3:])
        return jnp.moveaxis(t, 2, 0)

    init = (jnp.zeros((bsz, nh, d, d), F32), jnp.zeros((bsz, nh, d), F32), jnp.zeros((bsz, nh), F32))
    _, h = lax.scan(_mlstm_chunk_step, init, (chunks(q), chunks(k), chunks(v), chunks(log_i), chunks(log_f)))
    return jnp.moveaxis(h, 0, 2).reshape(bsz, nh, s, d)


def _mlstm_branch(q, k, v, o, gates, gate_bias, norm_gain):
    bsz, s, _ = q.shape

    def heads(t):
        return t.astype(F32).reshape(bsz, s, MLSTM_HEADS, MLSTM_HEAD_DIM).transpose(0, 2, 1, 3)

    qh, kh, vh = heads(q), heads(k) * (MLSTM_HEAD_DIM ** -0.5), heads(v)
    g = gates.astype(F32).reshape(bsz, s, 2, 2, MLSTM_HEADS) + gate_bias.astype(F32)
    g = jnp.moveaxis(g, 1, -1)
    log_i = g[:, :, 0]
    log_f = jax.nn.log_sigmoid(g[:, :, 1])
    h_fwd = _mlstm_scan(qh, kh, vh, log_i[:, 0], log_f[:, 0])

    def flip(t):
        return jnp.flip(t, axis=2)

    h_bwd = flip(_mlstm_scan(flip(qh), flip(kh), flip(vh), flip(log_i[:, 1]), flip(log_f[:, 1])))
    h = (h_fwd + h_bwd).transpose(0, 2, 1, 3)
    h = jax.nn.sigmoid(o.astype(F32)).reshape(bsz, s, MLSTM_HEADS, MLSTM_HEAD_DIM) * h
    mu = h.mean(-1, keepdims=True)
    var = jnp.square(h - mu).mean(-1, keepdims=True)
    h = ((h - mu) * lax.rsqrt(var + LN_EPS)).reshape(bsz, s, BRANCH_WIDTH) * norm_gain.astype(F32)
    return h.astype(q.dtype)


def _centred_short_conv(x, w, b):
    s = x.shape[1]
    xp = jnp.pad(x, ((0, 0), (1, 1), (0, 0)))
    return xp[:, :s] * w[0] + xp[:, 1:s + 1] * w[1] + xp[:, 2:] * w[2] + b


def _hyena_filters(seq_len, w1, b1, w2, b2, freq, w3, decay):
    pos = jnp.arange(seq_len, dtype=F32)
    t = pos / (seq_len - 1)
    bands = jnp.linspace(1e-4, HYENA_BANDS - 1, HYENA_BANDS, dtype=F32)
    ang = 2.0 * math.pi * pos[:, None] * bands[None, :] / seq_len
    feat = jnp.concatenate([t[:, None], jnp.cos(ang), -jnp.sin(ang)], axis=-1)
    freq = freq.astype(F32)
    h = jnp.sin(freq[0] * (feat @ w1.astype(F32) + b1.astype(F32)))
    h = jnp.sin(freq[1] * (h @ w2.astype(F32) + b2.astype(F32)))
    filt = (h @ w3.astype(F32)).reshape(seq_len, 2, HYENA_WIDTH)
    filt = filt * jnp.exp(-t[:, None, None] * jnp.abs(decay.astype(F32))[None])
    return filt / (jnp.sum(jnp.abs(filt), axis=0, keepdims=True) + 1e-6)


def _hyena_branch(p, conv_w, conv_b, w1, b1, w2, b2, freq, w3, decay, skip):
    seq_len = p.shape[1]
    u = _centred_short_conv(p, conv_w, conv_b)
    x0, x1, v = jnp.split(u, 3, axis=-1)
    z = (x1 * v).astype(F32)
    filt = _hyena_filters(seq_len, w1, b1, w2, b2, freq, w3, decay)
    two_sided = jnp.concatenate([filt[:, 0], jnp.zeros_like(filt[:1, 0]), jnp.flip(filt[1:, 1], axis=0)], axis=0)
    z_f = jnp.fft.rfft(z, n=2 * seq_len, axis=1)
    h_f = jnp.fft.rfft(two_sided, axis=0)
    y = jnp.fft.irfft(z_f * h_f[None], n=2 * seq_len, axis=1)[:, :seq_len]
    y = y + skip.astype(F32) * z
    return x0 * y.astype(x0.dtype)


def _ssm_combine(e1, e2):
    a1, b1 = e1
    a2, b2 = e2
    return a2 * a1, a2 * b1 + b2


def _s5_branch(u, lam_re, lam_im, log_step, b_re, b_im, c_re, c_im, skip):
    bsz, s, _ = u.shape
    ug = u.astype(F32).reshape(bsz, s, S5_GROUPS, S5_GROUP)
    lam = lax.complex(lam_re.astype(F32), lam_im.astype(F32))
    step = jnp.exp(log_step.astype(F32))[..., None]
    a_bar = jnp.exp(lam * step)
    b_mat = lax.complex(b_re.astype(F32), b_im.astype(F32))
    b_bar = ((a_bar - 1.0) / lam)[..., None] * b_mat
    c_mat = lax.complex(c_re.astype(F32), c_im.astype(F32))

    def run(d, reverse):
        bu = jnp.einsum('gpn,bsgn->bsgp', b_bar[d], ug)
        a = jnp.broadcast_to(a_bar[d], bu.shape)
        _, states = lax.associative_scan(_ssm_combine, (a, bu), reverse=reverse, axis=1)
        return jnp.einsum('gnp,bsgp->bsgn', c_mat[d], states).real

    y = run(0, False) + run(1, True) + skip.astype(F32).reshape(S5_GROUPS, S5_GROUP) * ug
    return y.reshape(bsz, s, S5_WIDTH).astype(u.dtype)


def _mixer(x, w_in, mlstm_gate_bias, mlstm_norm_gain, w_mlstm_out,
           hyena_conv_w, hyena_conv_b, hyena_w1, hyena_b1, hyena_w2, hyena_b2, hyena_freq, hyena_w3,
           hyena_decay, hyena_skip, w_hyena_out,
           s5_lambda_re, s5_lambda_im, s5_log_step, s5_b_re, s5_b_im, s5_c_re, s5_c_im, s5_skip, w_s5_glu,
           w_out):
    bsz, s, d = x.shape
    q, k, v, o, mg, hp, su, gate_pre = _split_in(x @ w_in)
    out_a = _mlstm_branch(q, k, v, o, mg, mlstm_gate_bias, mlstm_norm_gain) @ w_mlstm_out
    out_b = _hyena_branch(hp, hyena_conv_w, hyena_conv_b, hyena_w1, hyena_b1, hyena_w2, hyena_b2,
                          hyena_freq, hyena_w3, hyena_decay, hyena_skip) @ w_hyena_out
    s5y = _s5_branch(su, s5_lambda_re, s5_lambda_im, s5_log_step, s5_b_re, s5_b_im, s5_c_re, s5_c_im, s5_skip)
    glu_a, glu_g = jnp.split(s5y @ w_s5_glu, 2, axis=-1)
    out_c = glu_a * jax.nn.sigmoid(glu_g)
    g = jax.nn.sigmoid(gate_pre).reshape(bsz, s, N_BRANCHES, d)
    merged = g[:, :, 0] * out_a + g[:, :, 1] * out_b + g[:, :, 2] * out_c
    return merged @ w_out


def _peer(x, w_q, subkeys, u_tab, v_tab):
    bsz, s, d = x.shape
    t = x.reshape(bsz * s, d)
    n_tok = t.shape[0]
    q = (t @ w_q).astype(F32).reshape(n_tok, PEER_HEADS, 2, PEER_HALF)
    scores = jnp.einsum('thcd,hckd->thck', q, subkeys.astype(F32))
    top_v, top_i = lax.top_k(scores, PEER_TOPK)
    cand = top_v[:, :, 0, :, None] + top_v[:, :, 1, None, :]
    best_v, best_c = lax.top_k(cand.reshape(n_tok, PEER_HEADS, PEER_TOPK * PEER_TOPK), PEER_TOPK)
    i1 = jnp.take_along_axis(top_i[:, :, 0], best_c // PEER_TOPK, axis=-1)
    i2 = jnp.take_along_axis(top_i[:, :, 1], best_c % PEER_TOPK, axis=-1)
    expert = (i1 * PEER_KEYS + i2).reshape(n_tok, PEER_HEADS * PEER_TOPK)
    gate = jax.nn.softmax(best_v, axis=-1).reshape(n_tok, PEER_HEADS * PEER_TOPK)

    def block(args):
        xb, eb, gb = args
        act = jax.nn.gelu(jnp.einsum('td,ted->te', xb, u_tab[eb]).astype(F32))
        return jnp.einsum('te,ted->td', (act * gb).astype(xb.dtype), v_tab[eb])

    nb = n_tok // PEER_BLOCK
    out = lax.map(block, (t.reshape(nb, PEER_BLOCK, d),
                          expert.reshape(nb, PEER_BLOCK, -1),
                          gate.reshape(nb, PEER_BLOCK, -1)))
    return out.reshape(bsz, s, d)


def setup_inputs(seed: int = 0) -> dict:
    key = jax.random.key(seed)
    ks = iter(jax.random.split(key, 48))

    def nrm(shape, scale):
        return jax.random.normal(next(ks), shape, F32) * scale

    L = DEPTH
    ig = nrm((L, 2, MLSTM_HEADS), 0.1)
    fg = jnp.linspace(3.0, 6.0, MLSTM_HEADS, dtype=F32) + nrm((L, 2, MLSTM_HEADS), 0.1)
    min_decay = math.log(1e-2) / 1.5
    max_decay = math.log(1e-2) / 0.3
    deltas = jnp.linspace(min_decay, max_decay, HYENA_WIDTH, dtype=F32)
    n_idx = jnp.arange(S5_STATE, dtype=F32)
    return {
        'x': nrm((BATCH, SEQ, D_MODEL), 1.0),
        'w_in': nrm((L, D_MODEL, IN_COLS), D_MODEL ** -0.5),
        'mlstm_gate_bias': jnp.stack([ig, fg], axis=2),
        'mlstm_norm_gain': 1.0 + nrm((L, BRANCH_WIDTH), 0.02),
        'w_mlstm_out': nrm((L, BRANCH_WIDTH, D_MODEL), BRANCH_WIDTH ** -0.5),
        'hyena_conv_w': nrm((L, 3, 3 * HYENA_WIDTH), 3 ** -0.5),
        'hyena_conv_b': nrm((L, 3 * HYENA_WIDTH), 0.02),
        'hyena_w1': nrm((L, HYENA_EMB, HYENA_FILTER_HIDDEN), HYENA_EMB ** -0.5),
        'hyena_b1': nrm((L, HYENA_FILTER_HIDDEN), 0.02),
        'hyena_w2': nrm((L, HYENA_FILTER_HIDDEN, HYENA_FILTER_HIDDEN), HYENA_FILTER_HIDDEN ** -0.5),
        'hyena_b2': nrm((L, HYENA_FILTER_HIDDEN), 0.02),
        'hyena_freq': 1.0 + nrm((L, 2, HYENA_FILTER_HIDDEN), 0.02),
        'hyena_w3': nrm((L, HYENA_FILTER_HIDDEN, 2 * HYENA_WIDTH), HYENA_FILTER_HIDDEN ** -0.5),
        'hyena_decay': deltas[None, None] * (1.0 + nrm((L, 2, HYENA_WIDTH), 0.02)),
        'hyena_skip': nrm((L, HYENA_WIDTH), 1.0),
        'w_hyena_out': nrm((L, HYENA_WIDTH, D_MODEL), HYENA_WIDTH ** -0.5),
        's5_lambda_re': -0.5 + nrm((L, 2, S5_GROUPS, S5_STATE), 0.01),
        's5_lambda_im': math.pi * n_idx + nrm((L, 2, S5_GROUPS, S5_STATE), 0.01),
        's5_log_step': jax.random.uniform(next(ks), (L, 2, S5_GROUPS), F32, math.log(1e-3), math.log(1e-1)),
        's5_b_re': nrm((L, 2, S5_GROUPS, S5_STATE, S5_GROUP), (2 * S5_GROUP) ** -0.5),
        's5_b_im': nrm((L, 2, S5_GROUPS, S5_STATE, S5_GROUP), (2 * S5_GROUP) ** -0.5),
        's5_c_re': nrm((L, 2, S5_GROUPS, S5_GROUP, S5_STATE), (2 * S5_STATE) ** -0.5),
        's5_c_im': nrm((L, 2, S5_GROUPS, S5_GROUP, S5_STATE), (2 * S5_STATE) ** -0.5),
        's5_skip': nrm((L, S5_WIDTH), 1.0),
        'w_s5_glu': nrm((L, S5_WIDTH, 2 * D_MODEL), S5_WIDTH ** -0.5),
        'w_out': nrm((L, D_MODEL, D_MODEL), D_MODEL ** -0.5 * BETA),
        'ln1_g': 1.0 + nrm((L, D_MODEL), 0.02),
        'ln1_b': nrm((L, D_MODEL), 0.02),
        'peer_w_q': nrm((L, D_MODEL, PEER_HEADS * PEER_QDIM), D_MODEL ** -0.5),
        'peer_subkeys': nrm((L, PEER_HEADS, 2, PEER_KEYS, PEER_HALF), PEER_HALF ** -0.5),
        'peer_u': nrm((L, N_EXPERTS, D_MODEL), D_MODEL ** -0.5),
        'peer_v': nrm((L, N_EXPERTS, D_MODEL), (PEER_HEADS * PEER_TOPK) ** -0.5 * BETA),
        'ln2_g': 1.0 + nrm((L, D_MODEL), 0.02),
        'ln2_b': nrm((L, D_MODEL), 0.02),
    }


def reference(x, w_in, mlstm_gate_bias, mlstm_norm_gain, w_mlstm_out,
              hyena_conv_w, hyena_conv_b, hyena_w1, hyena_b1, hyena_w2, hyena_b2, hyena_freq, hyena_w3,
              hyena_decay, hyena_skip, w_hyena_out,
              s5_lambda_re, s5_lambda_im, s5_log_step, s5_b_re, s5_b_im, s5_c_re, s5_c_im, s5_skip, w_s5_glu,
              w_out, ln1_g, ln1_b, peer_w_q, peer_subkeys, peer_u, peer_v, ln2_g, ln2_b):
    for l in range(DEPTH):
        mix = _mixer(x, w_in[l], mlstm_gate_bias[l], mlstm_norm_gain[l], w_mlstm_out[l],
                     hyena_conv_w[l], hyena_conv_b[l], hyena_w1[l], hyena_b1[l], hyena_w2[l], hyena_b2[l],
                     hyena_freq[l], hyena_w3[l], hyena_decay[l], hyena_skip[l], w_hyena_out[l],
                     s5_lambda_re[l], s5_lambda_im[l], s5_log_step[l], s5_b_re[l], s5_b_im[l],
                     s5_c_re[l], s5_c_im[l], s5_skip[l], w_s5_glu[l], w_out[l])
        x = _layer_norm(ALPHA * x + mix, ln1_g[l], ln1_b[l])
        ffn = _peer(x, peer_w_q[l], peer_subkeys[l], peer_u[l], peer_v[l])
        x = _layer_norm(ALPHA * x + ffn, ln2_g[l], ln2_b[l])
    return x
```

```python
import math
from contextlib import ExitStack
import numpy as np
import concourse.bass as bass
import concourse.mybir as mybir
from concourse.bass_utils import run_bass_kernel_spmd

F32 = mybir.dt.float32
BF16 = mybir.dt.bfloat16
U32 = mybir.dt.uint32
AF = mybir.ActivationFunctionType
ALU = mybir.AluOpType
AX = mybir.AxisListType

NCORES = 8
D = 2048
S = 8192
DEPTH = 4
TPC = S // NCORES
NCB = 2
NBLK = S // TPC // NCB
BW = 1024
NWF = 1028
NWT = 768
ALPHA = (2 * DEPTH) ** 0.25
LN_EPS = 1e-5
TWO_PI = 2.0 * math.pi


class T:
    def __init__(self, t, name):
        self.t = t
        self.name = name
        self.w = None
        self.r = {}

    def __getitem__(self, idx):
        return self.t[idx]


class K:
    NDS = 24
    EPOCH = 30000

    def __init__(self, nc, es):
        self.nc = nc
        self.es = es
        self.eng = {'pe': nc.tensor, 'dve': nc.vector, 'act': nc.scalar, 'pool': nc.gpsimd, 'sp': nc.sync}
        self.nsem = 0
        self.sem = {k: self._newsem() for k in self.eng}
        self.semid = {k: (k, 0) for k in self.eng}
        self.cnt = {k: 0 for k in self.eng}
        self.waited = {k: {} for k in self.eng}
        self.dsem = [self._newsem() for _ in range(self.NDS)]
        self.dcnt = [0] * self.NDS
        self.dnext = 0
        self.last = {}
        self.ninst = 0

    def _newsem(self):
        self.nsem += 1
        return self.es.enter_context(self.nc.semaphore(f"s{self.nsem}"))

    def _wait(self, e, ev):
        if ev is None:
            return
        sem, val, key = ev
        if key == ('pe', self.semid['pe'][1]) and e == 'pe':
            return
        if self.waited[e].get(key, 0) >= val:
            return
        self.eng[e].wait_ge(sem, val)
        self.waited[e][key] = val

    def deps(self, e, R, W):
        for b in R:
            self._wait(e, b.w)
        for b in W:
            self._wait(e, b.w)
            for ev in b.r.values():
                self._wait(e, ev)

    def commit(self, ev, R, W):
        self.last[ev[2]] = ev
        for b in R:
            b.r[ev[2]] = ev
        for b in W:
            b.w = ev
            b.r = {}

    def op(self, e, fn, R=(), W=()):
        self.deps(e, R, W)
        if self.cnt[e] >= self.EPOCH:
            self.sem[e] = self._newsem()
            self.semid[e] = (e, self.semid[e][1] + 1)
            self.cnt[e] = 0
        ins = fn(self.eng[e])
        self.cnt[e] += 1
        ins.then_inc(self.sem[e], 1)
        ev = (self.sem[e], self.cnt[e], self.semid[e])
        self.commit(ev, R, W)
        self.ninst += 1
        return ev

    def dma(self, q, out, in_, R=(), W=(), **kw):
        i = self.dnext
        self.dnext = (i + 1) % self.NDS
        if self.dcnt[i] > 0:
            self._wait(q, (self.dsem[i], 16 * self.dcnt[i], ('d', i)))
        self.deps(q, R, W)
        self.eng[q].dma_start(out=out, in_=in_, **kw).then_inc(self.dsem[i], 16)
        self.dcnt[i] += 1
        ev = (self.dsem[i], 16 * self.dcnt[i], ('d', i))
        self.commit(ev, R, W)
        self.ninst += 1
        return ev

    def allgather(self, out, in_, R=(), W=()):
        e = 'pool'
        self.deps(e, R, W)
        sem = self._newsem()
        self.nc.gpsimd.collective_compute("AllGather", ALU.bypass, replica_groups=[list(range(NCORES))],
                                          ins=[in_], outs=[out]).then_inc(sem, 1)
        ev = (sem, 1, ('cc', self.nsem))
        self.commit(ev, R, W)
        return ev

    def barrier(self):
        for e in self.eng:
            for ev in list(self.last.values()):
                self._wait(e, ev)

    def sb(self, es, name, shape, dt=F32):
        self.nsem += 1
        name = f"{name}_{self.nsem}"
        return T(es.enter_context(self.nc.sbuf_tensor(name, list(shape), dt)), name)

    def ps(self, es, name, shape, dt=F32):
        return T(es.enter_context(self.nc.psum_tensor(name, list(shape), dt)), name)

    def dram(self, name, shape, dt=F32, kind="Internal"):
        return T(self.nc.dram_tensor(name, list(shape), dt, kind=kind), name)


def _inputs_A(inp, l, c, xT):
    h = c % 4
    m = {}
    m["xT"] = xT
    w = inp["w_in"][l]
    q0 = h * 256
    colsF = np.concatenate([
        np.arange(q0, q0 + 256), 1024 + np.arange(q0, q0 + 256),
        4112 + 0 * 1024 + c * 128 + np.arange(128), 4112 + 1 * 1024 + c * 128 + np.arange(128),
        4112 + 2 * 1024 + c * 128 + np.arange(128), 7184 + c * 128 + np.arange(128),
        4096 + np.array([0 * 8 + 0 * 4 + h, 0 * 8 + 1 * 4 + h, 1 * 8 + 0 * 4 + h, 1 * 8 + 1 * 4 + h]),
    ])
    colsT = np.concatenate([2048 + np.arange(q0, q0 + 256), 1024 + np.arange(q0, q0 + 256), 3072 + np.arange(q0, q0 + 256)])
    m["wf"] = np.ascontiguousarray(w[:, colsF])
    m["wt"] = np.ascontiguousarray(w[:, colsT])
    gb = inp["mlstm_gate_bias"][l][:, :, h].reshape(4)
    m["gb"] = np.ascontiguousarray(np.broadcast_to(gb[None, :], (128, 4))).astype(np.float32)
    m["mg"] = np.ascontiguousarray(np.broadcast_to(inp["mlstm_norm_gain"][l][None, q0:q0 + 256], (128, 256))).astype(np.float32)
    p3 = np.zeros((128, 8, 3), np.float32); sB = np.zeros((128, 8, 2, 16), np.float32); sC = np.zeros((128, 8, 2, 16), np.float32)
    for d_ in range(2):
        for pr in range(4):
            for g2 in range(2):
                g = 8 * c + 2 * pr + g2
                rs = slice(g2 * 64, (g2 + 1) * 64)
                t_ = d_ * 4 + pr
                p3[rs, t_, 0] = inp["s5_lambda_re"][l][d_, g]
                p3[rs, t_, 1] = inp["s5_lambda_im"][l][d_, g]
                p3[rs, t_, 2] = inp["s5_log_step"][l][d_, g]
                sB[rs, t_, 0] = inp["s5_b_re"][l][d_, g]
                sB[rs, t_, 1] = inp["s5_b_im"][l][d_, g]
                sC[rs, t_, 0] = inp["s5_c_re"][l][d_, g].T
                sC[rs, t_, 1] = inp["s5_c_im"][l][d_, g].T
    m["s5p"] = p3; m["s5B"] = sB; m["s5C"] = sC
    m["s5k"] = np.ascontiguousarray(inp["s5_skip"][l][c * 128:(c + 1) * 128].reshape(128, 1))
    w1 = inp["hyena_w1"][l]
    w1p = np.zeros((65, 64), np.float32); w1p[0:16] = w1[1:17]; w1p[32:48] = w1[17:33]; w1p[64] = w1[0]
    m["hw1"] = w1p
    col = lambda a: np.ascontiguousarray(a.reshape(-1, 1)).astype(np.float32)
    m["hb1"] = col(inp["hyena_b1"][l]); m["hb2"] = col(inp["hyena_b2"][l])
    m["hf0"] = col(inp["hyena_freq"][l][0]); m["hf1"] = col(inp["hyena_freq"][l][1])
    m["hw2"] = np.ascontiguousarray(inp["hyena_w2"][l])
    w3 = inp["hyena_w3"][l].reshape(64, 2, 1024)
    m["hw3"] = np.ascontiguousarray(w3[:, :, c * 128:(c + 1) * 128])
    m["hdec"] = np.ascontiguousarray(inp["hyena_decay"][l][:, c * 128:(c + 1) * 128].T)
    m["hsk"] = col(inp["hyena_skip"][l][c * 128:(c + 1) * 128])
    cw = np.zeros((128, 3, 4), np.float32)
    for j_ in range(3):
        sl_ = slice(j_ * 1024 + c * 128, j_ * 1024 + (c + 1) * 128)
        cw[:, j_, 0:3] = inp["hyena_conv_w"][l][:, sl_].T
        cw[:, j_, 3] = inp["hyena_conv_b"][l][sl_]
    m["hcw"] = cw
    return m


def _shared_B(inp, l):
    m = {}
    m["wg"] = np.ascontiguousarray(inp["w_in"][l][:, 8208:])
    m["wmo"] = np.ascontiguousarray(inp["w_mlstm_out"][l]); m["who"] = np.ascontiguousarray(inp["w_hyena_out"][l])
    m["wsg"] = np.ascontiguousarray(inp["w_s5_glu"][l]); m["wo"] = np.ascontiguousarray(inp["w_out"][l])
    m["wq"] = np.ascontiguousarray(inp["peer_w_q"][l])
    m["pu"] = np.ascontiguousarray(inp["peer_u"][l].reshape(8, 2048, 2048).transpose(0, 2, 1).reshape(16384, 2048))
    m["pv"] = np.ascontiguousarray(inp["peer_v"][l])
    m["psk"] = np.ascontiguousarray(inp["peer_subkeys"][l].reshape(16, 128, 128).transpose(2, 0, 1))
    lnrow = np.concatenate([inp["ln1_g"][l], inp["ln1_b"][l], inp["ln2_g"][l], inp["ln2_b"][l]])
    m["ln"] = np.ascontiguousarray(np.broadcast_to(lnrow[None, :], (128, 4 * D))).astype(np.float32)
    return m


class Prog:
    def __init__(self, mode, debug=False, phases="mshtp"):
        self.mode = mode
        self.phases = phases
        self.debug = debug
        self.nc = bass.Bass("TRN2", target_bir_lowering=False)
        self.es = ExitStack()

    def ext(self, name, shape, dt=F32):
        return self.k.dram(name, shape, dt, kind="ExternalInput")

    def build(self):
        nc, es = self.nc, self.es
        with es:
            k = self.k = K(nc, es)
            self.pq = 0
            self.identf = k.sb(es, "identf", [128, 128])
            self.identb = k.sb(es, "identb", [128, 128], BF16)
            self.onesf = k.sb(es, "onesf", [128, 128])
            self.PS = [k.ps(es, f"PS{i}", [128, 512]) for i in range(8)]
            self.psn = 0
            k.op('dve', lambda e: e.memset(self.onesf[:, :], 1.0), W=[self.onesf])
            self.epst = k.sb(es, "epst", [128, 1])
            k.op('dve', lambda e: e.memset(self.epst[:, :], LN_EPS), W=[self.epst])
            k.op('pool', lambda e: e.affine_select(out=self.identf[:, :], in_=self.onesf[:, :], pattern=[[-1, 128]], compare_op=ALU.is_equal,
                                                   fill=0.0, base=0, channel_multiplier=1), R=[self.onesf], W=[self.identf])
            k.op('dve', lambda e: e.tensor_copy(out=self.identb[:, :], in_=self.identf[:, :]), R=[self.identf], W=[self.identb])
            if self.mode == 'A':
                self.build_A()
            else:
                self.build_B()
        return nc

    def build_A(self):
        k, es = self.k, self.es
        d = {}
        for nm, shp in [("wf", [D, NWF]), ("wt", [D, NWT]), ("gb", [128, 4]), ("mg", [128, 256]),
                        ("hw1", [65, 64]), ("hb1", [64, 1]), ("hb2", [64, 1]), ("hf0", [64, 1]), ("hf1", [64, 1]), ("hw2", [64, 64]),
                        ("hw3", [64, 2, 128]), ("hdec", [128, 2]), ("hsk", [128, 1]), ("hcw", [128, 3, 4]),
                        ("s5p", [128, 8, 3]), ("s5B", [128, 8, 2, 16]), ("s5C", [128, 8, 2, 16]), ("s5k", [128, 1])]:
            d[nm] = self.ext(nm, shp)
        self.W = [d]
        self.xT_ext = self.ext("xT", [D, S])
        self.PT = k.dram("PT", [NWF, S], F32)
        self.KV = k.dram("KV", [S, NWT], F32)
        self.HN_sh = k.dram("HN_sh", [256, S], BF16, kind="ExternalOutput")
        self.S5_sh = k.dram("S5_sh", [128, S], BF16, kind="ExternalOutput")
        self.HY_sh = k.dram("HY_sh", [128, S], BF16, kind="ExternalOutput")
        self.phase2_inproj(0)
        if 'm' in self.phases:
            self.phase3_mlstm(0)
        if 's' in self.phases:
            self.phase3_s5(0)
        if 'h' in self.phases:
            self.phase3_hyena(0)
        for t in (self.HN_sh, self.S5_sh, self.HY_sh):
            k._wait('sp', t.w)

    def build_B(self):
        k, es = self.k, self.es
        d = {}
        import os
        notab = bool(os.environ.get("B_NOTAB"))
        for nm, shp in [("wg", [D, 6144]), ("wmo", [BW, D]), ("who", [BW, D]), ("wsg", [BW, 2 * D]), ("wo", [D, D]), ("wq", [D, D]),
                        ("pu", [16384, 2048]), ("pv", [16384, D]), ("psk", [128, 16, 128]), ("ln", [128, 4 * D])]:
            if notab and nm in ("pu", "pv"):
                continue
            if 't' not in self.phases and nm in ("wg", "wmo", "who", "wsg", "wo"):
                continue
            d[nm] = self.ext(nm, shp)
        self.W = [d]
        import os as _os
        nblk = int(_os.environ.get("B_NBLK", str(NBLK)))
        self.x_tok = self.ext("x_tok", [nblk * TPC, D])
        xTs = self.ext("xTs", [D, nblk * TPC])
        self.actin = [self.ext(nm, [BW, nblk * TPC], BF16) for nm in ("hn", "hy", "s5")] if 't' in self.phases else []
        self.out = k.dram("out", [nblk * TPC, D], F32, kind="ExternalOutput")
        self.xT_sh = k.dram("xT_sh", [D, TPC], BF16)
        if self.debug:
            self.dbgX1 = k.dram("dbgX1", [nblk * TPC, D], F32, kind="ExternalOutput")
        self.X = [k.sb(es, f"X{i}", [128, D]) for i in range(8)]
        g = {}
        for nm, rows, cols in [("wg", D, 6144), ("wmo", BW, D), ("who", BW, D), ("wsg", BW, 2 * D), ("wo", D, D), ("wq", D, D),
                               ("pu", 16384, 2048), ("pv", 16384, D)]:
            if nm not in d:
                continue
            full = k.dram(f"{nm}F", [rows, cols], BF16)
            nch = 8 if rows >= 8192 else 1
            rr = rows // nch
            for i in range(nch):
                k.dma('pool', full[i * rr:(i + 1) * rr, :], d[nm][i * rr:(i + 1) * rr, :], R=[d[nm]], W=[full])
            g[nm] = full
        self.GW = [g]
        for blk in range(nblk):
            self.blk = blk
            b0 = blk * TPC
            for i in range(8):
                k.dma('sp', self.X[i][:, :], self.x_tok[b0 + i * 128:b0 + (i + 1) * 128, :], R=[self.x_tok], W=[self.X[i]])
            k.dma('pool', self.xT_sh[:, :], xTs[:, b0:b0 + TPC], R=[xTs], W=[self.xT_sh])
            if 't' in self.phases:
                self.phase4_token(0)
            if self.debug:
                for i in range(8):
                    k.dma('sp', self.dbgX1[b0 + i * 128:b0 + (i + 1) * 128, :], self.X[i][:, :], R=[self.X[i]], W=[self.dbgX1])
            if 'p' in self.phases:
                self.phase5_peer(0)
            for i in range(8):
                k.dma('sp', self.out[b0 + i * 128:b0 + (i + 1) * 128, :], self.X[i][:, :], R=[self.X[i]], W=[self.out])
        k._wait('sp', self.out.w)
        if self.debug:
            k._wait('sp', self.dbgX1.w)

    def rstd(self, m2):
        k = self.k
        k.op('act', lambda e: e.activation(out=m2[:, 1:2], in_=m2[:, 1:2], func=AF.Ln, bias=self.epst[:, 0:1]), R=[m2, self.epst], W=[m2])
        k.op('act', lambda e: e.activation(out=m2[:, 1:2], in_=m2[:, 1:2], func=AF.Exp, scale=-0.5), R=[m2], W=[m2])

    def sctmp(self, es, n):
        k = self.k
        return (k.sb(es, "sc_ki", [128, n], mybir.dt.int32), k.sb(es, "sc_kf", [128, n]), k.sb(es, "sc_y", [128, n]))

    def sinr(self, tmp, srcT, src, n, outT, out, shift, p0=0, p1=128):
        k = self.k
        ki_, kf_, y_ = tmp
        ki, kf, y = ki_.t[p0:p1, 0:n], kf_.t[p0:p1, 0:n], y_.t[p0:p1, 0:n]
        k.op('dve', lambda e: e.tensor_scalar(out=y, in0=src, scalar1=shift, scalar2=None, op0=ALU.add), R=[srcT], W=[y_])
        k.op('dve', lambda e: e.tensor_single_scalar(out=ki, in_=y, scalar=1.0 / TWO_PI, op=ALU.mult), R=[y_], W=[ki_])
        k.op('dve', lambda e: e.tensor_copy(out=kf, in_=ki), R=[ki_], W=[kf_])
        k.op('dve', lambda e: e.scalar_tensor_tensor(out=y, in0=kf, scalar=-TWO_PI, in1=y, op0=ALU.mult, op1=ALU.add), R=[kf_, y_], W=[y_])
        k.op('dve', lambda e: e.tensor_scalar(out=y, in0=y, scalar1=-3.1415925, scalar2=3.1415925, op0=ALU.max, op1=ALU.min), R=[y_], W=[y_])
        k.op('act', lambda e: e.activation(out=out, in_=y, func=AF.Sin), R=[y_], W=[outT])

    def sincos(self, tmp, angT, ang, n, sinT, sin_out, cosT, cos_out):
        k = self.k
        ki_, kf_, y_ = tmp

        class V:
            def __init__(s_, t):
                s_.t = t

            def __getitem__(s_, idx):
                return s_.t.t[:, 0:n]
        ki, kf, y = V(ki_), V(kf_), V(y_)
        for shift, oT, out in ((0.0, sinT, sin_out), (0.5 * math.pi, cosT, cos_out)):
            k.op('dve', lambda e: e.tensor_scalar(out=y[:, :], in0=ang, scalar1=shift, scalar2=None, op0=ALU.add), R=[angT], W=[y_])
            k.op('dve', lambda e: e.tensor_single_scalar(out=ki[:, :], in_=y[:, :], scalar=1.0 / TWO_PI, op=ALU.mult), R=[y_], W=[ki_])
            k.op('dve', lambda e: e.tensor_copy(out=kf[:, :], in_=ki[:, :]), R=[ki_], W=[kf_])
            k.op('dve', lambda e: e.scalar_tensor_tensor(out=y[:, :], in0=kf[:, :], scalar=-TWO_PI, in1=y[:, :], op0=ALU.mult, op1=ALU.add), R=[kf_, y_], W=[y_])
            k.op('dve', lambda e: e.tensor_scalar(out=y[:, :], in0=y[:, :], scalar1=-3.1415925, scalar2=3.1415925, op0=ALU.max, op1=ALU.min), R=[y_], W=[y_])
            k.op('act', lambda e: e.activation(out=out, in_=y[:, :], func=AF.Sin), R=[y_], W=[oT])

    def phase3_s5(self, l):
        k = self.k
        W = self.W[l]
        PL = 1024
        NP = S // PL
        TT = lambda *a, **kw: None
        with ExitStack() as es:
            p3 = k.sb(es, "s_p3", [128, 8, 3]); sB = k.sb(es, "s_B", [128, 8, 2, 16]); sC = k.sb(es, "s_C", [128, 8, 2, 16]); sk = k.sb(es, "s_k", [128, 1])
            k.dma('sp', p3[:, :, :], W["s5p"][:, :, :], R=[W["s5p"]], W=[p3])
            k.dma('sp', sB[:, :, :, :], W["s5B"][:, :, :, :], R=[W["s5B"]], W=[sB])
            k.dma('sp', sC[:, :, :, :], W["s5C"][:, :, :, :], R=[W["s5C"]], W=[sC])
            k.dma('sp', sk[:, :], W["s5k"][:, :], R=[W["s5k"]], W=[sk])
            uT = k.sb(es, "s_uT", [128, S], BF16)
            k.dma('pool', uT[:, :], self.PT[896:1024, :], R=[self.PT], W=[uT])
            acc = k.sb(es, "s_acc", [128, S])
            P = {n: k.sb(es, f"s_{n}", [128, 8]) for n in ["step", "th", "rho", "r", "sn", "cs", "are", "aim", "nre", "nim", "den", "fre", "fim",
                                                            "t1", "t2", "thL", "snL", "csL", "nth"]}
            lre, lim, lst = p3[:, :, 0], p3[:, :, 1], p3[:, :, 2]

            def tt(o, a, b, op, R, Wt):
                k.op('dve', lambda e: e.tensor_tensor(out=o, in0=a, in1=b, op=op), R=R, W=Wt)
            k.op('act', lambda e: e.activation(out=P["step"][:, :], in_=lst, func=AF.Exp), R=[p3], W=[P["step"]])
            tt(P["th"][:, :], lim, P["step"][:, :], ALU.mult, [p3, P["step"]], [P["th"]])
            tt(P["rho"][:, :], lre, P["step"][:, :], ALU.mult, [p3, P["step"]], [P["rho"]])
            k.op('act', lambda e: e.activation(out=P["r"][:, :], in_=P["rho"][:, :], func=AF.Exp), R=[P["rho"]], W=[P["r"]])
            scs = self.sctmp(es, PL)
            self.sincos(scs, P["th"], P["th"][:, :], 8, P["sn"], P["sn"][:, :], P["cs"], P["cs"][:, :])
            k.op('dve', lambda e: e.tensor_single_scalar(out=P["thL"][:, :], in_=P["th"][:, :], scalar=float(PL), op=ALU.mult), R=[P["th"]], W=[P["thL"]])
            self.sincos(scs, P["thL"], P["thL"][:, :], 8, P["snL"], P["snL"][:, :], P["csL"], P["csL"][:, :])
            tt(P["are"][:, :], P["r"][:, :], P["cs"][:, :], ALU.mult, [P["r"], P["cs"]], [P["are"]])
            tt(P["aim"][:, :], P["r"][:, :], P["sn"][:, :], ALU.mult, [P["r"], P["sn"]], [P["aim"]])
            k.op('dve', lambda e: e.tensor_single_scalar(out=P["t1"][:, :], in_=P["are"][:, :], scalar=-1.0, op=ALU.add), R=[P["are"]], W=[P["t1"]])
            tt(P["nre"][:, :], P["t1"][:, :], lre, ALU.mult, [P["t1"], p3], [P["nre"]])
            tt(P["t2"][:, :], P["aim"][:, :], lim, ALU.mult, [P["aim"], p3], [P["t2"]])
            tt(P["nre"][:, :], P["nre"][:, :], P["t2"][:, :], ALU.add, [P["nre"], P["t2"]], [P["nre"]])
            tt(P["nim"][:, :], P["aim"][:, :], lre, ALU.mult, [P["aim"], p3], [P["nim"]])
            tt(P["t2"][:, :], P["t1"][:, :], lim, ALU.mult, [P["t1"], p3], [P["t2"]])
            tt(P["nim"][:, :], P["nim"][:, :], P["t2"][:, :], ALU.subtract, [P["nim"], P["t2"]], [P["nim"]])
            tt(P["den"][:, :], lre, lre, ALU.mult, [p3], [P["den"]])
            tt(P["t2"][:, :], lim, lim, ALU.mult, [p3], [P["t2"]])
            tt(P["den"][:, :], P["den"][:, :], P["t2"][:, :], ALU.add, [P["den"], P["t2"]], [P["den"]])
            k.op('dve', lambda e: e.reciprocal(out=P["den"][:, :], in_=P["den"][:, :]), R=[P["den"]], W=[P["den"]])
            tt(P["fre"][:, :], P["nre"][:, :], P["den"][:, :], ALU.mult, [P["nre"], P["den"]], [P["fre"]])
            tt(P["fim"][:, :], P["nim"][:, :], P["den"][:, :], ALU.mult, [P["nim"], P["den"]], [P["fim"]])
            k.op('dve', lambda e: e.tensor_single_scalar(out=P["nth"][:, :], in_=P["th"][:, :], scalar=-1.0, op=ALU.mult), R=[P["th"]], W=[P["nth"]])
            bbr = k.sb(es, "s_bbr", [128, 8, 16]); bbi = k.sb(es, "s_bbi", [128, 8, 16]); tb = k.sb(es, "s_tb", [128, 8, 16])
            fre_b = P["fre"][:, :].unsqueeze(2).to_broadcast([128, 8, 16]); fim_b = P["fim"][:, :].unsqueeze(2).to_broadcast([128, 8, 16])
            tt(bbr[:, :, :], sB[:, :, 0, :], fre_b, ALU.mult, [sB, P["fre"]], [bbr])
            tt(tb[:, :, :], sB[:, :, 1, :], fim_b, ALU.mult, [sB, P["fim"]], [tb])
            tt(bbr[:, :, :], bbr[:, :, :], tb[:, :, :], ALU.subtract, [bbr, tb], [bbr])
            tt(bbi[:, :, :], sB[:, :, 1, :], fre_b, ALU.mult, [sB, P["fre"]], [bbi])
            tt(tb[:, :, :], sB[:, :, 0, :], fim_b, ALU.mult, [sB, P["fim"]], [tb])
            tt(bbi[:, :, :], bbi[:, :, :], tb[:, :, :], ALU.add, [bbi, tb], [bbi])
            iota0 = k.sb(es, "s_iota", [128, PL]); ang = k.sb(es, "s_ang", [128, PL]); COS = k.sb(es, "s_cos", [128, PL]); SIN = k.sb(es, "s_sin", [128, PL])
            k.op('pool', lambda e: e.iota(iota0[:, :], pattern=[[1, PL]], base=0, channel_multiplier=0, allow_small_or_imprecise_dtypes=True), W=[iota0])
            pad = k.sb(es, "s_pad", [128, 128])
            LB = [k.sb(es, f"s_LB{i}", [128, 128], BF16) for i in range(2)]
            LC = [k.sb(es, f"s_LC{i}", [128, 128], BF16) for i in range(2)]
            mre = k.sb(es, "s_mre", [128, PL]); mim = k.sb(es, "s_mim", [128, PL])
            wre = k.sb(es, "s_wre", [128, PL]); wim = k.sb(es, "s_wim", [128, PL])
            xre = k.sb(es, "s_xre", [128, PL], BF16); xim = k.sb(es, "s_xim", [128, PL], BF16)
            tq = [k.sb(es, f"s_tq{i}", [128, 512]) for i in range(4)]
            tw = [k.sb(es, f"s_tw{i}", [128, PL]) for i in range(2)]
            ini = k.sb(es, "s_ini", [128, 2]); ini2 = k.sb(es, "s_ini2", [128, 2]); itmp = k.sb(es, "s_itmp", [128, 2])
            first_acc = [True] * (S // 512)
            scb = scs
            for d in range(2):
                rev = (d == 1)
                R_ = (lambda ap: ap[:, ::-1]) if rev else (lambda ap: ap)
                for pr in range(4):
                    t_ = d * 4 + pr
                    c0 = 32 * pr
                    for which, src in ((0, bbr), (1, bbi)):
                        k.op('dve', lambda e: e.memset(pad[:, :], 0.0), W=[pad])
                        k.op('dve', lambda e: e.tensor_copy(out=pad[0:64, c0:c0 + 16], in_=src[0:64, t_, :]), R=[src], W=[pad])
                        k.op('dve', lambda e: e.tensor_copy(out=pad[64:128, c0 + 16:c0 + 32], in_=src[64:128, t_, :]), R=[src], W=[pad])
                        ps = self.psum()
                        k.op('pe', lambda e: e.transpose(out=ps[:, 0:128], in_=pad[:, :], identity=self.identf[:, :]), R=[pad, self.identf], W=[ps])
                        k.op('dve', lambda e: e.tensor_copy(out=LB[which][:, :], in_=ps[:, 0:128]), R=[ps], W=[LB[which]])
                    for which, sgn in ((0, 1.0), (1, -1.0)):
                        k.op('dve', lambda e: e.memset(LC[which][:, :], 0.0), W=[LC[which]])
                        k.op('dve', lambda e: e.tensor_single_scalar(out=LC[which][0:64, c0:c0 + 16], in_=sC[0:64, t_, which, :], scalar=sgn, op=ALU.mult),
                             R=[sC], W=[LC[which]])
                        k.op('dve', lambda e: e.tensor_single_scalar(out=LC[which][64:128, c0 + 16:c0 + 32], in_=sC[64:128, t_, which, :], scalar=sgn, op=ALU.mult),
                             R=[sC], W=[LC[which]])
                    k.op('dve', lambda e: e.tensor_scalar(out=ang[:, :], in0=iota0[:, :], scalar1=P["th"][:, t_:t_ + 1], scalar2=None, op0=ALU.mult),
                         R=[iota0, P["th"]], W=[ang])
                    self.sincos(scb, ang, ang[:, :], PL, SIN, SIN[:, :], COS, COS[:, :])
                    rcol = P["r"][:, t_:t_ + 1]
                    for qi in range(NP):
                        q = (NP - 1 - qi) if rev else qi
                        base = q * PL
                        for ch in range(PL // 512):
                            cs = slice(ch * 512, (ch + 1) * 512)
                            gs_ = slice(base + ch * 512, base + (ch + 1) * 512)
                            pr_ = self.psum(); pi_ = self.psum()
                            k.op('pe', lambda e: e.matmul(pr_[:, :], lhsT=LB[0][:, :], rhs=uT[:, gs_], start=True, stop=True), R=[LB[0], uT], W=[pr_])
                            k.op('pe', lambda e: e.matmul(pi_[:, :], lhsT=LB[1][:, :], rhs=uT[:, gs_], start=True, stop=True), R=[LB[1], uT], W=[pi_])
                            if rev:
                                Cc = COS[:, PL - 1 - ch * 512 - 511:PL - ch * 512][:, ::-1]
                                Sc = SIN[:, PL - 1 - ch * 512 - 511:PL - ch * 512][:, ::-1]
                            else:
                                Cc = COS[:, cs]; Sc = SIN[:, cs]
                            tt(tq[0][:, :], pr_[:, :], Cc, ALU.mult, [pr_, COS], [tq[0]])
                            tt(tq[1][:, :], pi_[:, :], Sc, ALU.mult, [pi_, SIN], [tq[1]])
                            tt(tq[2][:, :], pi_[:, :], Cc, ALU.mult, [pi_, COS], [tq[2]])
                            tt(tq[3][:, :], pr_[:, :], Sc, ALU.mult, [pr_, SIN], [tq[3]])
                            k.op('pool', lambda e: e.tensor_tensor(out=mre[:, cs], in0=tq[0][:, :], in1=tq[1][:, :], op=ALU.add), R=[tq[0], tq[1]], W=[mre])
                            k.op('pool', lambda e: e.tensor_tensor(out=mim[:, cs], in0=tq[2][:, :], in1=tq[3][:, :], op=ALU.subtract), R=[tq[2], tq[3]], W=[mim])
                        if qi == 0:
                            k.op('dve', lambda e: e.memset(ini2[:, :], 0.0), W=[ini2])
                        else:
                            cl = P["csL"][:, t_:t_ + 1]; sl = P["snL"][:, t_:t_ + 1]
                            k.op('dve', lambda e: e.tensor_scalar(out=itmp[:, 0:1], in0=ini[:, 1:2], scalar1=sl, scalar2=-1.0, op0=ALU.mult, op1=ALU.mult), R=[ini, P["snL"]], W=[itmp])
                            k.op('dve', lambda e: e.scalar_tensor_tensor(out=ini2[:, 0:1], in0=ini[:, 0:1], scalar=cl, in1=itmp[:, 0:1], op0=ALU.mult, op1=ALU.add),
                                 R=[ini, itmp, P["csL"]], W=[ini2])
                            k.op('dve', lambda e: e.tensor_scalar(out=itmp[:, 1:2], in0=ini[:, 0:1], scalar1=sl, scalar2=None, op0=ALU.mult), R=[ini, P["snL"]], W=[itmp])
                            k.op('dve', lambda e: e.scalar_tensor_tensor(out=ini2[:, 1:2], in0=ini[:, 1:2], scalar=cl, in1=itmp[:, 1:2], op0=ALU.mult, op1=ALU.add),
                                 R=[ini, itmp, P["csL"]], W=[ini2])
                        rb = rcol.to_broadcast([128, PL])
                        k.op('dve', lambda e: e.tensor_tensor_scan(out=R_(wre[:, :]), data0=rb, data1=R_(mre[:, :]), initial=ini2[:, 0:1], op0=ALU.mult, op1=ALU.add),
                             R=[mre, ini2, P["r"]], W=[wre])
                        k.op('dve', lambda e: e.tensor_tensor_scan(out=R_(wim[:, :]), data0=rb, data1=R_(mim[:, :]), initial=ini2[:, 1:2], op0=ALU.mult, op1=ALU.add),
                             R=[mim, ini2, P["r"]], W=[wim])
                        lastc = 0 if rev else PL - 1
                        k.op('dve', lambda e: e.tensor_copy(out=ini[:, 0:1], in_=wre[:, lastc:lastc + 1]), R=[wre], W=[ini])
                        k.op('dve', lambda e: e.tensor_copy(out=ini[:, 1:2], in_=wim[:, lastc:lastc + 1]), R=[wim], W=[ini])
                        Cf_ = R_(COS[:, :]); Sf_ = R_(SIN[:, :])
                        tt(tw[0][:, :], wre[:, :], Cf_, ALU.mult, [wre, COS], [tw[0]])
                        k.op('pool', lambda e: e.tensor_tensor(out=tw[1][:, :], in0=wim[:, :], in1=Sf_, op=ALU.mult), R=[wim, SIN], W=[tw[1]])
                        tt(xre[:, :], tw[0][:, :], tw[1][:, :], ALU.subtract, [tw[0], tw[1]], [xre])
                        tt(tw[0][:, :], wre[:, :], Sf_, ALU.mult, [wre, SIN], [tw[0]])
                        k.op('pool', lambda e: e.tensor_tensor(out=tw[1][:, :], in0=wim[:, :], in1=Cf_, op=ALU.mult), R=[wim, COS], W=[tw[1]])
                        tt(xim[:, :], tw[0][:, :], tw[1][:, :], ALU.add, [tw[0], tw[1]], [xim])
                        for ch in range(PL // 512):
                            cs = slice(ch * 512, (ch + 1) * 512)
                            gi = (base + ch * 512) // 512
                            gs_ = slice(base + ch * 512, base + (ch + 1) * 512)
                            py = self.psum()
                            k.op('pe', lambda e: e.matmul(py[:, :], lhsT=LC[0][:, :], rhs=xre[:, cs], start=True, stop=False), R=[LC[0], xre], W=[py])
                            k.op('pe', lambda e: e.matmul(py[:, :], lhsT=LC[1][:, :], rhs=xim[:, cs], start=False, stop=True), R=[LC[1], xim], W=[py])
                            if first_acc[gi]:
                                first_acc[gi] = False
                                k.op('act', lambda e: e.activation(out=acc[:, gs_], in_=py[:, :], func=AF.Identity), R=[py], W=[acc])
                            else:
                                tt(acc[:, gs_], acc[:, gs_], py[:, :], ALU.add, [acc, py], [acc])
            uf = [k.sb(es, f"s_uf{i}", [128, 2048]) for i in range(2)]
            ob = [k.sb(es, f"s_ob{i}", [128, 2048], BF16) for i in range(2)]
            for pc in range(4):
                cs = slice(pc * 2048, (pc + 1) * 2048)
                k.dma('sp', uf[pc % 2][:, :], self.PT[896:1024, cs], R=[self.PT], W=[uf[pc % 2]])
                k.op('dve', lambda e: e.scalar_tensor_tensor(out=ob[pc % 2][:, :], in0=uf[pc % 2][:, :], scalar=sk[:, 0:1], in1=acc[:, cs], op0=ALU.mult, op1=ALU.add),
                     R=[uf[pc % 2], sk, acc], W=[ob[pc % 2]])
                k.dma('sp', self.S5_sh[:, cs], ob[pc % 2][:, :], R=[ob[pc % 2]], W=[self.S5_sh])
            k.barrier()

    def phase3_hyena(self, l):
        k = self.k
        W = self.W[l]
        L = S
        NB = S // 128
        TW = 16256
        HX0 = k.dram(f"h_X0_{l}", [128, S]); HZ = k.dram(f"h_Z_{l}", [128, S]); GD = k.dram(f"h_GD_{l}", [128, 16384], BF16)
        with ExitStack() as es:
            zTr = k.sb(es, "h_zTr", [128, 128, NB], BF16)
            cw = k.sb(es, "h_cw", [128, 3, 4]); hsk = k.sb(es, "h_sk", [128, 1]); hdec = k.sb(es, "h_dec", [128, 2])
            k.dma('sp', cw[:, :, :], W["hcw"][:, :, :], R=[W["hcw"]], W=[cw])
            k.dma('sp', hsk[:, :], W["hsk"][:, :], R=[W["hsk"]], W=[hsk])
            k.dma('sp', hdec[:, :], W["hdec"][:, :], R=[W["hdec"]], W=[hdec])
            with ExitStack() as esA:
                xs = k.sb(esA, "h_xs", [128, S]); u = k.sb(esA, "h_u", [128, S]); z = k.sb(esA, "h_z", [128, S])
                for j in range(3):
                    k.dma('sp', xs[:, :], self.PT[512 + j * 128:512 + (j + 1) * 128, :], R=[self.PT], W=[xs])
                    dst = z if j == 1 else u
                    k.op('act', lambda e: e.activation(out=dst[:, :], in_=xs[:, :], func=AF.Identity, scale=cw[:, j, 1:2], bias=cw[:, j, 3:4]), R=[xs, cw], W=[dst])
                    k.op('dve', lambda e: e.scalar_tensor_tensor(out=dst[:, 1:S], in0=xs[:, 0:S - 1], scalar=cw[:, j, 0:1], in1=dst[:, 1:S], op0=ALU.mult, op1=ALU.add),
                         R=[xs, cw, dst], W=[dst])
                    k.op('dve', lambda e: e.scalar_tensor_tensor(out=dst[:, 0:S - 1], in0=xs[:, 1:S], scalar=cw[:, j, 2:3], in1=dst[:, 0:S - 1], op0=ALU.mult, op1=ALU.add),
                         R=[xs, cw, dst], W=[dst])
                    if j == 0:
                        k.dma('sp', HX0[:, :], u[:, :], R=[u], W=[HX0])
                    if j == 2:
                        k.op('dve', lambda e: e.tensor_tensor(out=z[:, :], in0=z[:, :], in1=u[:, :], op=ALU.mult), R=[z, u], W=[z])
                k.dma('sp', HZ[:, :], z[:, :], R=[z], W=[HZ])
                k.op('dve', lambda e: e.tensor_copy(out=u[:, :], in_=z[:, ::-1]), R=[z], W=[u])
                for ap_ in range(0, NB, 4):
                    ps = self.psum()
                    for j in range(4):
                        a_ = ap_ + j
                        k.op('pe', lambda e: e.transpose(out=ps[:, j * 128:(j + 1) * 128], in_=u[:, a_ * 128:(a_ + 1) * 128], identity=self.identf[:, :]),
                             R=[u, self.identf], W=[ps])
                    for j in range(4):
                        a = NB - 1 - (ap_ + j)
                        self.evac(zTr[:, :, a], ps[:, j * 128:(j + 1) * 128], R=[ps], W=[zTr])
                k.barrier()
            with ExitStack() as esB:
                F = [k.sb(esB, f"h_F{d}", [128, L]) for d in range(2)]
                w1 = k.sb(esB, "h_w1", [65, 64]); w2 = k.sb(esB, "h_w2", [64, 64]); w3 = k.sb(esB, "h_w3", [64, 2, 128])
                cols = {n: k.sb(esB, f"h_{n}", [64, 1]) for n in ["hb1", "hb2", "hf0", "hf1", "fb1", "fb2"]}
                k.dma('sp', w1[:, :], W["hw1"][:, :], R=[W["hw1"]], W=[w1])
                k.dma('sp', w2[:, :], W["hw2"][:, :], R=[W["hw2"]], W=[w2])
                k.dma('sp', w3[:, :, :], W["hw3"][:, :, :], R=[W["hw3"]], W=[w3])
                for n in ["hb1", "hb2", "hf0", "hf1"]:
                    k.dma('sp', cols[n][:, :], W[n][:, :], R=[W[n]], W=[cols[n]])
                k.op('dve', lambda e: e.tensor_tensor(out=cols["fb1"][:, :], in0=cols["hb1"][:, :], in1=cols["hf0"][:, :], op=ALU.mult), R=[cols["hb1"], cols["hf0"]], W=[cols["fb1"]])
                k.op('dve', lambda e: e.tensor_tensor(out=cols["fb2"][:, :], in0=cols["hb2"][:, :], in1=cols["hf1"][:, :], op=ALU.mult), R=[cols["hb2"], cols["hf1"]], W=[cols["fb2"]])
                pidx = k.sb(esB, "h_pidx", [128, 1]); om = k.sb(esB, "h_om", [128, 1]); ndec = k.sb(esB, "h_ndec", [128, 2])
                k.op('pool', lambda e: e.iota(pidx[:, :], pattern=[[0, 1]], base=0, channel_multiplier=1, allow_small_or_imprecise_dtypes=True), W=[pidx])
                dlt = (15.0 - 1e-4) / 15.0
                k.op('dve', lambda e: e.tensor_scalar(out=om[0:16, :], in0=pidx[0:16, :], scalar1=dlt, scalar2=1e-4, op0=ALU.mult, op1=ALU.add), R=[pidx], W=[om])
                k.op('dve', lambda e: e.tensor_scalar(out=om[32:48, :], in0=pidx[32:48, :], scalar1=dlt, scalar2=1e-4 - 32.0 * dlt, op0=ALU.mult, op1=ALU.add), R=[pidx], W=[om])
                k.op('dve', lambda e: e.tensor_single_scalar(out=om[0:48, :], in_=om[0:48, :], scalar=TWO_PI / L, op=ALU.mult), R=[om], W=[om])
                k.op('act', lambda e: e.activation(out=ndec[:, :], in_=hdec[:, :], func=AF.Abs), R=[hdec], W=[ndec])
                k.op('dve', lambda e: e.tensor_single_scalar(out=ndec[:, :], in_=ndec[:, :], scalar=-1.0 / (L - 1), op=ALU.mult), R=[ndec], W=[ndec])
                CH = 512
                iot = k.sb(esB, "h_iot", [128, CH]); feat = k.sb(esB, "h_feat", [65, CH]); ang = k.sb(esB, "h_ang", [48, CH])
                a1 = k.sb(esB, "h_a1", [64, CH]); h1 = k.sb(esB, "h_h1", [64, CH]); h2 = k.sb(esB, "h_h2", [64, CH]); win = k.sb(esB, "h_win", [128, CH])
                asum = k.sb(esB, "h_asum", [128, 2, L // CH]); tot = k.sb(esB, "h_tot", [128, 2])
                tmp = self.sctmp(esB, CH)
                k.op('dve', lambda e: e.memset(feat[:, :], 0.0), W=[feat])
                for ci in range(L // CH):
                    j0 = ci * CH
                    k.op('pool', lambda e: e.iota(iot[:, :], pattern=[[1, CH]], base=j0, channel_multiplier=0, allow_small_or_imprecise_dtypes=True), W=[iot])
                    k.op('dve', lambda e: e.tensor_scalar(out=ang[:, :], in0=iot[0:48, :], scalar1=om[0:48, 0:1], scalar2=None, op0=ALU.mult), R=[iot, om], W=[ang])
                    self.sinr(tmp, ang, ang[0:16, :], CH, feat, feat[0:16, :], 0.5 * math.pi, 0, 16)
                    self.sinr(tmp, ang, ang[32:48, :], CH, feat, feat[32:48, :], math.pi, 32, 48)
                    k.op('dve', lambda e: e.tensor_single_scalar(out=feat[64:65, :], in_=iot[64:65, :], scalar=1.0 / (L - 1), op=ALU.mult), R=[iot], W=[feat])
                    ps = self.psum()
                    k.op('pe', lambda e: e.matmul(ps[0:64, 0:CH], lhsT=w1[:, :], rhs=feat[:, :], start=True, stop=True), R=[w1, feat], W=[ps])
                    k.op('act', lambda e: e.activation(out=a1[:, :], in_=ps[0:64, 0:CH], func=AF.Identity, scale=cols["hf0"][:, 0:1], bias=cols["fb1"][:, 0:1]),
                         R=[ps, cols["hf0"], cols["fb1"]], W=[a1])
                    self.sinr(tmp, a1, a1[:, :], CH, h1, h1[:, :], 0.0, 0, 64)
                    ps = self.psum()
                    k.op('pe', lambda e: e.matmul(ps[0:64, 0:CH], lhsT=w2[:, :], rhs=h1[:, :], start=True, stop=True), R=[w2, h1], W=[ps])
                    k.op('act', lambda e: e.activation(out=a1[:, :], in_=ps[0:64, 0:CH], func=AF.Identity, scale=cols["hf1"][:, 0:1], bias=cols["fb2"][:, 0:1]),
                         R=[ps, cols["hf1"], cols["fb2"]], W=[a1])
                    self.sinr(tmp, a1, a1[:, :], CH, h2, h2[:, :], 0.0, 0, 64)
                    for d in range(2):
                        ps = self.psum()
                        k.op('pe', lambda e: e.matmul(ps[:, 0:CH], lhsT=w3[:, d, :], rhs=h2[:, :], start=True, stop=True), R=[w3, h2], W=[ps])
                        k.op('act', lambda e: e.activation(out=win[:, :], in_=iot[:, :], func=AF.Exp, scale=ndec[:, d:d + 1]), R=[iot, ndec], W=[win])
                        k.op('dve', lambda e: e.tensor_tensor(out=F[d][:, j0:j0 + CH], in0=ps[:, 0:CH], in1=win[:, :], op=ALU.mult), R=[ps, win], W=[F[d]])
                        k.op('dve', lambda e: e.tensor_reduce(out=asum[:, d, ci:ci + 1], in_=F[d][:, j0:j0 + CH], axis=AX.X, op=ALU.add, apply_absolute_value=True),
                             R=[F[d]], W=[asum])
                k.op('dve', lambda e: e.tensor_reduce(out=tot[:, :], in_=asum[:, :, :], axis=AX.X, op=ALU.add), R=[asum], W=[tot])
                k.op('dve', lambda e: e.tensor_single_scalar(out=tot[:, :], in_=tot[:, :], scalar=1e-6, op=ALU.add), R=[tot], W=[tot])
                k.op('dve', lambda e: e.reciprocal(out=tot[:, :], in_=tot[:, :]), R=[tot], W=[tot])
                gb16 = [k.sb(esB, f"h_gb{i}", [128, 2048], BF16) for i in range(2)]
                n_ = 0
                for pc in range(4):
                    g = gb16[n_ % 2]; n_ += 1
                    k.op('dve', lambda e: e.tensor_scalar(out=g[:, :], in0=F[0][:, pc * 2048:(pc + 1) * 2048], scalar1=tot[:, 0:1], scalar2=None, op0=ALU.mult),
                         R=[F[0], tot], W=[g])
                    k.dma('sp', GD[:, 8191 + pc * 2048:8191 + (pc + 1) * 2048], g[:, :], R=[g], W=[GD])
                for pc in range(4):
                    g = gb16[n_ % 2]; n_ += 1
                    i0 = pc * 2048
                    n = 2048 if pc < 3 else 2047
                    lo = 8191 - i0 - n + 1
                    src = F[1][:, lo:lo + n]
                    k.op('dve', lambda e: e.tensor_scalar(out=g[:, 0:n], in0=src[:, ::-1], scalar1=tot[:, 1:2], scalar2=None, op0=ALU.mult), R=[F[1], tot], W=[g])
                    k.dma('sp', GD[:, i0:i0 + n], g[:, 0:n], R=[g], W=[GD])
                k.barrier()
            with ExitStack() as esC:
                Tt = [k.sb(esC, f"h_Tt{i}", [128, TW], BF16) for i in range(2)]
                Yt = k.sb(esC, "h_Yt", [128, NB, 128])
                for c8 in range(16):
                    ps = self.psum()
                    for cj in range(8):
                        c = c8 * 8 + cj
                        tt_ = Tt[c % 2]
                        src = bass.AP(tensor=GD.t, offset=c * 16384, ap=[[1, 128], [1, TW]])
                        k.dma('sp', tt_[:, :], src, R=[GD], W=[tt_])
                        o0 = cj * 64
                        dl = [0] + [x for d_ in range(1, NB) for x in (d_, -d_)]
                        for n_i, dlt_ in enumerate(dl):
                            y0 = 128 * dlt_ + 8064
                            if dlt_ >= 0:
                                rhs = zTr[:, c, 0:NB - dlt_]; out = ps[:, o0 + dlt_:o0 + NB]
                            else:
                                rhs = zTr[:, c, -dlt_:NB]; out = ps[:, o0:o0 + NB + dlt_]
                            k.op('pe', lambda e: e.matmul(out, lhsT=tt_[:, y0:y0 + 128], rhs=rhs, start=(n_i == 0), stop=(n_i == len(dl) - 1), skip_group_check=True),
                                 R=[tt_, zTr], W=[ps])
                    self.evac(Yt[:, :, c8 * 8:(c8 + 1) * 8], ps[:, :].rearrange("p (c a) -> p a c", c=8), R=[ps], W=[Yt])
                zf = [k.sb(esC, f"h_zf{i}", [128, 512]) for i in range(2)]
                xf = [k.sb(esC, f"h_xf{i}", [128, 512]) for i in range(2)]
                ob = [k.sb(esC, f"h_ob{i}", [128, 512], BF16) for i in range(2)]
                for a4 in range(0, NB, 4):
                    i2 = (a4 // 4) % 2
                    cs = slice(a4 * 128, (a4 + 4) * 128)
                    k.dma('sp', zf[i2][:, :], HZ[:, cs], R=[HZ], W=[zf[i2]])
                    k.dma('sp', xf[i2][:, :], HX0[:, cs], R=[HX0], W=[xf[i2]])
                    ps = self.psum()
                    for j in range(4):
                        k.op('pe', lambda e: e.transpose(out=ps[:, j * 128:(j + 1) * 128], in_=Yt[:, a4 + j, :], identity=self.identf[:, :]), R=[Yt, self.identf], W=[ps])
                    k.op('dve', lambda e: e.scalar_tensor_tensor(out=zf[i2][:, :], in0=zf[i2][:, :], scalar=hsk[:, 0:1], in1=ps[:, :], op0=ALU.mult, op1=ALU.add),
                         R=[zf[i2], hsk, ps], W=[zf[i2]])
                    k.op('dve', lambda e: e.tensor_tensor(out=ob[i2][:, :], in0=zf[i2][:, :], in1=xf[i2][:, :], op=ALU.mult), R=[zf[i2], xf[i2]], W=[ob[i2]])
                    k.dma('sp', self.HY_sh[:, cs], ob[i2][:, :], R=[ob[i2]], W=[self.HY_sh])
                k.barrier()

    def layernorm(self, es, Y, grow, brow, out):
        k = self.k
        s6 = k.sb(es, "ln_s6", [128, 4, 6]); m2 = k.sb(es, "ln_m2", [128, 2])
        for c4 in range(4):
            k.op('dve', lambda e: e.bn_stats(out=s6[:, c4, :], in_=Y[:, c4 * 512:(c4 + 1) * 512]), R=[Y], W=[s6])
        k.op('dve', lambda e: e.bn_aggr(out=m2[:, :], in_=s6[:, :, :].rearrange("p a b -> p (a b)")), R=[s6], W=[m2])
        self.rstd(m2)
        k.op('dve', lambda e: e.tensor_scalar(out=Y[:, :], in0=Y[:, :], scalar1=m2[:, 0:1], scalar2=m2[:, 1:2], op0=ALU.subtract, op1=ALU.mult),
             R=[Y, m2], W=[Y])
        k.op('pool', lambda e: e.tensor_tensor(out=Y[:, :], in0=Y[:, :], in1=grow[:, :], op=ALU.mult), R=[Y, grow], W=[Y])
        k.op('dve', lambda e: e.tensor_tensor(out=out[:, :], in0=Y[:, :], in1=brow[:, :], op=ALU.add), R=[Y, brow], W=[out])

    def phase4_token(self, l):
        k = self.k
        W = self.W[l]; G = self.GW[l]
        with ExitStack() as es:
            lnt = [k.sb(es, f"t_ln{i}", [128, D]) for i in range(4)]
            for i in range(4):
                k.dma('sp', lnt[i][:, :], W["ln"][:, i * D:(i + 1) * D], R=[W["ln"]], W=[lnt[i]])
            xt = k.sb(es, "t_xt", [128, 16, 128], BF16)
            act3 = [k.sb(es, f"t_a{i}", [128, 8, 128], BF16) for i in range(3)]
            wgt = [k.sb(es, f"t_wg{i}", [128, 16, 512], BF16) for i in range(2)]
            w8 = [k.sb(es, f"t_w8{i}", [128, 8, 512], BF16) for i in range(2)]
            gs = k.sb(es, "t_gs", [128, 512]); tmp = k.sb(es, "t_tmp", [128, 512]); sg = k.sb(es, "t_sg", [128, 512])
            merged = k.sb(es, "t_merged", [128, D]); Y = k.sb(es, "t_Y", [128, D])
            mT = k.sb(es, "t_mT", [128, 16, 128], BF16)
            nw = [0, 0]

            def gate(i, b, cc):
                wt = wgt[nw[0] % 2]; nw[0] += 1
                c0 = b * D + cc * 512
                k.dma('sp', wt[:, :, :], G["wg"][:, c0:c0 + 512].rearrange("(a p) c -> p a c", p=128), R=[G["wg"]], W=[wt])
                ps = self.psum()
                for dk in range(16):
                    k.op('pe', lambda e: e.matmul(ps[:, :], lhsT=xt[:, dk, :], rhs=wt[:, dk, :], start=(dk == 0), stop=(dk == 15)), R=[xt, wt], W=[ps])
                k.op('act', lambda e: e.activation(out=gs[:, :], in_=ps[:, :], func=AF.Sigmoid), R=[ps], W=[gs])

            def proj8(a, wfull, c0):
                wt = w8[nw[1] % 2]; nw[1] += 1
                k.dma('sp', wt[:, :, :], wfull[:, c0:c0 + 512].rearrange("(a p) c -> p a c", p=128), R=[wfull], W=[wt])
                ps = self.psum()
                for dk in range(8):
                    k.op('pe', lambda e: e.matmul(ps[:, :], lhsT=a[:, dk, :], rhs=wt[:, dk, :], start=(dk == 0), stop=(dk == 7)), R=[a, wt], W=[ps])
                return ps

            for i in range(8):
                tk = slice(self.blk * TPC + i * 128, self.blk * TPC + (i + 1) * 128)
                k.dma('sp', xt[:, :, :], self.xT_sh[:, i * 128:(i + 1) * 128].rearrange("(a p) t -> p a t", p=128), R=[self.xT_sh], W=[xt])
                for j_ in range(3):
                    k.dma('sp', act3[j_][:, :, :], self.actin[j_][:, tk].rearrange("(a p) t -> p a t", p=128), R=[self.actin[j_]], W=[act3[j_]])
                for cc in range(4):
                    csl = slice(cc * 512, (cc + 1) * 512)
                    gate(i, 0, cc)
                    ps = proj8(act3[0], G["wmo"], cc * 512)
                    k.op('dve', lambda e: e.tensor_tensor(out=merged[:, csl], in0=ps[:, :], in1=gs[:, :], op=ALU.mult), R=[ps, gs], W=[merged])
                    gate(i, 1, cc)
                    ps = proj8(act3[1], G["who"], cc * 512)
                    k.op('dve', lambda e: e.tensor_tensor(out=tmp[:, :], in0=ps[:, :], in1=gs[:, :], op=ALU.mult), R=[ps, gs], W=[tmp])
                    k.op('pool', lambda e: e.tensor_tensor(out=merged[:, csl], in0=merged[:, csl], in1=tmp[:, :], op=ALU.add), R=[merged, tmp], W=[merged])
                    gate(i, 2, cc)
                    psg = proj8(act3[2], G["wsg"], D + cc * 512)
                    k.op('act', lambda e: e.activation(out=sg[:, :], in_=psg[:, :], func=AF.Sigmoid), R=[psg], W=[sg])
                    ps = proj8(act3[2], G["wsg"], cc * 512)
                    k.op('dve', lambda e: e.tensor_tensor(out=sg[:, :], in0=ps[:, :], in1=sg[:, :], op=ALU.mult), R=[ps, sg], W=[sg])
                    k.op('dve', lambda e: e.tensor_tensor(out=tmp[:, :], in0=sg[:, :], in1=gs[:, :], op=ALU.mult), R=[sg, gs], W=[tmp])
                    k.op('pool', lambda e: e.tensor_tensor(out=merged[:, csl], in0=merged[:, csl], in1=tmp[:, :], op=ALU.add), R=[merged, tmp], W=[merged])
                for g4 in range(4):
                    ps = self.psum()
                    for j in range(4):
                        dk = g4 * 4 + j
                        k.op('pe', lambda e: e.transpose(out=ps[:, j * 128:(j + 1) * 128], in_=merged[:, dk * 128:(dk + 1) * 128], identity=self.identf[:, :]),
                             R=[merged, self.identf], W=[ps])
                    self.evac(mT[:, g4 * 4:(g4 + 1) * 4, :], ps[:, :].rearrange("p (a b) -> p a b", a=4), R=[ps], W=[mT])
                for cc in range(4):
                    csl = slice(cc * 512, (cc + 1) * 512)
                    wt = wgt[nw[0] % 2]; nw[0] += 1
                    k.dma('sp', wt[:, :, :], G["wo"][:, csl].rearrange("(a p) c -> p a c", p=128), R=[G["wo"]], W=[wt])
                    ps = self.psum()
                    for dk in range(16):
                        k.op('pe', lambda e: e.matmul(ps[:, :], lhsT=mT[:, dk, :], rhs=wt[:, dk, :], start=(dk == 0), stop=(dk == 15)), R=[mT, wt], W=[ps])
                    k.op('dve', lambda e: e.scalar_tensor_tensor(out=Y[:, csl], in0=self.X[i][:, csl], scalar=ALPHA, in1=ps[:, :], op0=ALU.mult, op1=ALU.add),
                         R=[self.X[i], ps], W=[Y])
                with ExitStack() as es2:
                    self.layernorm(es2, Y, lnt[0], lnt[1], self.X[i])
            k.barrier()

    def phase5_peer(self, l):
        k = self.k
        W = self.W[l]; G = self.GW[l]
        NEG = -1e30
        C1 = math.sqrt(0.044715)
        C2 = 2.0 * math.sqrt(2.0 / math.pi)
        with ExitStack() as es:
            psk = k.sb(es, "p_sk", [128, 16, 128])
            k.dma('sp', psk[:, :, :], W["psk"][:, :, :], R=[W["psk"]], W=[psk])
            x1T = k.sb(es, "p_x1T", [128, 16, 512], BF16)
            skhi = k.sb(es, "p_skhi", [128, 16, 128], BF16); sklo = k.sb(es, "p_sklo", [128, 16, 128], BF16)
            qhi = k.sb(es, "p_qhi", [128, 512], BF16); qlo = k.sb(es, "p_qlo", [128, 512], BF16)
            k.op('dve', lambda e: e.tensor_copy(out=skhi[:, :, :], in_=psk[:, :, :]), R=[psk], W=[skhi])
            k.op('dve', lambda e: e.tensor_tensor(out=sklo[:, :, :], in0=psk[:, :, :], in1=skhi[:, :, :], op=ALU.subtract), R=[psk, skhi], W=[sklo])
            s1 = [k.sb(es, f"p_s1_{i}", [128, 8, 128]) for i in range(4)]
            s2 = [k.sb(es, f"p_s2_{i}", [128, 8, 128]) for i in range(4)]
            tau = [k.sb(es, f"p_tau{i}", [128, 8]) for i in range(4)]
            nkap = [k.sb(es, f"p_nkap{i}", [128, 8]) for i in range(4)]
            wqt = [k.sb(es, f"p_wq{i}", [128, 16, 128], BF16) for i in range(1)]
            qT = [k.sb(es, f"p_qT{i}", [128, 512]) for i in range(1)]
            uT = [k.sb(es, f"p_uT{i}", [128, 16, 512], BF16) for i in range(1)]
            Vt = k.sb(es, "p_V", [128, 4, D], BF16)
            HT = [k.sb(es, f"p_HT{i}", [128, 512], BF16) for i in range(4)]
            gh = [k.sb(es, f"p_gh{i}", [128, 4, 128], BF16) for i in range(8)]
            sm = [k.sb(es, f"p_sm{i}", [128, 4, 128]) for i in range(2)]
            ex = [k.sb(es, f"p_ex{i}", [128, 4, 128]) for i in range(2)]
            g1 = k.sb(es, "p_g1", [128, 512]); g2 = k.sb(es, "p_g2", [128, 512]); g3 = k.sb(es, "p_g3", [128, 512])
            t16 = [k.sb(es, f"p_t16_{i}", [128, 24]) for i in range(2)]
            wk = k.sb(es, "p_wk", [128, 128]); wk2 = k.sb(es, "p_wk2", [128, 128])
            cand = k.sb(es, "p_cand", [128, 24, 24]); cwk = k.sb(es, "p_cwk", [128, 576]); cwk2 = k.sb(es, "p_cwk2", [128, 576])
            b24 = k.sb(es, "p_b24", [128, 24]); e16 = k.sb(es, "p_e16", [128, 16]); st = k.sb(es, "p_st", [128, 4])
            nq = 0
            import os
            STAGE = int(os.environ.get("PEER_STAGE", "9")); NEG_ = int(os.environ.get("PEER_NEG", "32")); NHF = int(os.environ.get("PEER_NHF", "2"))
            for hf in range(NHF):
                Xh = self.X[hf * 4:(hf + 1) * 4]
                for ti in range(4):
                    for g4 in range(4):
                        ps = self.psum()
                        for j in range(4):
                            dk = g4 * 4 + j
                            k.op('pe', lambda e: e.transpose(out=ps[:, j * 128:(j + 1) * 128], in_=Xh[ti][:, dk * 128:(dk + 1) * 128], identity=self.identf[:, :]),
                                 R=[Xh[ti], self.identf], W=[ps])
                        self.evac(x1T[:, g4 * 4:(g4 + 1) * 4, ti * 128:(ti + 1) * 128], ps[:, :].rearrange("p (a b) -> p a b", a=4), R=[ps], W=[x1T])
                    k.op('pool', lambda e: e.tensor_single_scalar(out=Xh[ti][:, :], in_=Xh[ti][:, :], scalar=ALPHA, op=ALU.mult), R=[Xh[ti]], W=[Xh[ti]])
                for hc in range(16 if STAGE >= 1 else 0):
                    h, c = hc // 2, hc % 2
                    wt = wqt[0]; q_ = qT[0]; nq += 1
                    WQ = G[os.environ.get("PEER_WQ", "wq")]
                    k.dma('sp', wt[:, :, :], WQ[:, hc * 128:(hc + 1) * 128].rearrange("(a p) c -> p a c", p=128), R=[WQ], W=[wt])
                    ps = self.psum()
                    for dk in range(16):
                        k.op('pe', lambda e: e.matmul(ps[:, :], lhsT=wt[:, dk, :], rhs=x1T[:, dk, :], start=(dk == 0), stop=(dk == 15)), R=[wt, x1T], W=[ps])
                    self.evac(q_[:, :], ps[:, :], R=[ps], W=[q_])
                    if os.environ.get("PEER_NOSC"):
                        continue
                    k.op('dve', lambda e: e.tensor_copy(out=qhi[:, :], in_=q_[:, :]), R=[q_], W=[qhi])
                    k.op('dve', lambda e: e.tensor_tensor(out=qlo[:, :], in0=q_[:, :], in1=qhi[:, :], op=ALU.subtract), R=[q_, qhi], W=[qlo])
                    PX = int(os.environ.get("PEER_X", "0"))
                    if PX == 2:
                        continue
                    ps2 = self.psum()
                    for ti in range(4):
                        tsl = slice(ti * 128, (ti + 1) * 128)
                        k.op('pe', lambda e: e.matmul(ps2[:, tsl], lhsT=qhi[:, tsl], rhs=skhi[:, hc, :], start=True, stop=False), R=[qhi, skhi], W=[ps2])
                        k.op('pe', lambda e: e.matmul(ps2[:, tsl], lhsT=qhi[:, tsl], rhs=sklo[:, hc, :], start=False, stop=False), R=[qhi, sklo], W=[ps2])
                        k.op('pe', lambda e: e.matmul(ps2[:, tsl], lhsT=qlo[:, tsl], rhs=skhi[:, hc, :], start=False, stop=True), R=[qlo, skhi], W=[ps2])
                    for ti in range(4 if PX != 1 else 0):
                        dst = (s1 if c == 0 else s2)[ti]
                        k.op('dve', lambda e: e.tensor_copy(out=dst[:, h, :], in_=ps2[:, ti * 128:(ti + 1) * 128]), R=[ps2], W=[dst])
                for ti in range(4 if STAGE >= 2 else 0):
                    for h in range(8):
                        for c, sc in ((0, s1[ti]), (1, s2[ti])):
                            t_ = t16[c]
                            k.op('dve', lambda e: e.max(out=t_[:, 0:8], in_=sc[:, h, :]), R=[sc], W=[t_])
                            k.op('dve', lambda e: e.match_replace(out=wk[:, :], in_to_replace=t_[:, 0:8], in_values=sc[:, h, :], imm_value=NEG), R=[sc, t_], W=[wk])
                            k.op('dve', lambda e: e.max(out=t_[:, 8:16], in_=wk[:, :]), R=[wk], W=[t_])
                            k.op('dve', lambda e: e.match_replace(out=wk2[:, :], in_to_replace=t_[:, 8:16], in_values=wk[:, :], imm_value=NEG), R=[wk, t_], W=[wk2])
                            k.op('dve', lambda e: e.max(out=t_[:, 16:24], in_=wk2[:, :]), R=[wk2], W=[t_])
                        k.op('dve', lambda e: e.tensor_tensor(out=cand[:, :, :], in0=t16[0][:, :].unsqueeze(2).to_broadcast([128, 24, 24]),
                                                              in1=t16[1][:, :].unsqueeze(1).to_broadcast([128, 24, 24]), op=ALU.add), R=[t16[0], t16[1]], W=[cand])
                        cf = cand[:, :, :].rearrange("p a b -> p (a b)")
                        k.op('dve', lambda e: e.max(out=b24[:, 0:8], in_=cf), R=[cand], W=[b24])
                        k.op('dve', lambda e: e.match_replace(out=cwk[:, :], in_to_replace=b24[:, 0:8], in_values=cf, imm_value=NEG), R=[cand, b24], W=[cwk])
                        k.op('dve', lambda e: e.max(out=b24[:, 8:16], in_=cwk[:, :]), R=[cwk], W=[b24])
                        k.op('dve', lambda e: e.match_replace(out=cwk2[:, :], in_to_replace=b24[:, 8:16], in_values=cwk[:, :], imm_value=NEG), R=[cwk, b24], W=[cwk2])
                        k.op('dve', lambda e: e.max(out=b24[:, 16:24], in_=cwk2[:, :]), R=[cwk2], W=[b24])
                        k.op('dve', lambda e: e.tensor_tensor(out=tau[ti][:, h:h + 1], in0=b24[:, 15:16], in1=b24[:, 16:17], op=ALU.add), R=[b24], W=[tau[ti]])
                        k.op('dve', lambda e: e.tensor_single_scalar(out=st[:, 0:1], in_=b24[:, 0:1], scalar=-1.0, op=ALU.mult), R=[b24], W=[st])
                        k.op('act', lambda e: e.activation(out=e16[:, :], in_=b24[:, 0:16], func=AF.Exp, bias=st[:, 0:1]), R=[b24, st], W=[e16])
                        k.op('dve', lambda e: e.tensor_reduce(out=st[:, 1:2], in_=e16[:, :], axis=AX.X, op=ALU.add), R=[e16], W=[st])
                        k.op('act', lambda e: e.activation(out=st[:, 2:3], in_=st[:, 1:2], func=AF.Ln), R=[st], W=[st])
                        k.op('dve', lambda e: e.tensor_tensor(out=nkap[ti][:, h:h + 1], in0=st[:, 0:1], in1=st[:, 2:3], op=ALU.subtract), R=[st], W=[nkap[ti]])
                    k.op('dve', lambda e: e.tensor_single_scalar(out=tau[ti][:, :], in_=tau[ti][:, :], scalar=0.5, op=ALU.mult), R=[tau[ti]], W=[tau[ti]])
                for eg in range(NEG_ if STAGE >= 3 else 0):
                    u_ = uT[0]
                    r_, e0 = (eg * 512) // 2048, (eg * 512) % 2048
                    k.dma('sp', u_[:, :, :], G["pu"][r_ * D:(r_ + 1) * D, e0:e0 + 512].rearrange("(a p) e -> p a e", p=128), R=[G["pu"]], W=[u_])
                    k.dma('sp', Vt[:, :, :], G["pv"][eg * 512:(eg + 1) * 512, :].rearrange("(a p) d -> p a d", p=128), R=[G["pv"]], W=[Vt])
                    pg = [self.psum() for _ in range(4)]
                    for ti in range(4):
                        for h in range(8):
                            sm_, ex_, g_ = sm[h % 2], ex[h % 2], gh[h]
                            k.op('pool', lambda e: e.tensor_tensor(out=sm_[:, :, :], in0=s1[ti][:, h, eg * 4:(eg + 1) * 4].unsqueeze(2).to_broadcast([128, 4, 128]),
                                                                   in1=s2[ti][:, h, :].unsqueeze(1).to_broadcast([128, 4, 128]), op=ALU.add),
                                 R=[s1[ti], s2[ti]], W=[sm_])
                            k.op('act', lambda e: e.activation(out=ex_[:, :, :], in_=sm_[:, :, :], func=AF.Exp, bias=nkap[ti][:, h:h + 1]), R=[sm_, nkap[ti]], W=[ex_])
                            k.op('dve', lambda e: e.scalar_tensor_tensor(out=g_[:, :, :], in0=sm_[:, :, :], scalar=tau[ti][:, h:h + 1], in1=ex_[:, :, :],
                                                                         op0=ALU.is_gt, op1=ALU.mult), R=[sm_, ex_, tau[ti]], W=[g_])
                        for il in range(4):
                            for h in range(8):
                                k.op('pe', lambda e: e.matmul(pg[il][:, ti * 128:(ti + 1) * 128], lhsT=gh[h][:, il, :], rhs=self.identb[:, :], start=(h == 0), stop=(h == 7)),
                                     R=[gh[h], self.identb], W=[pg[il]])
                    for il in range(4):
                        pa = self.psum()
                        for dk in range(16):
                            k.op('pe', lambda e: e.matmul(pa[:, :], lhsT=u_[:, dk, il * 128:(il + 1) * 128], rhs=x1T[:, dk, :], start=(dk == 0), stop=(dk == 15)),
                                 R=[u_, x1T], W=[pa])
                        k.op('act', lambda e: e.activation(out=g1[:, :], in_=pa[:, :], func=AF.Square, scale=C1), R=[pa], W=[g1])
                        k.op('dve', lambda e: e.scalar_tensor_tensor(out=g2[:, :], in0=g1[:, :], scalar=1.0, in1=pa[:, :], op0=ALU.add, op1=ALU.mult), R=[g1, pa], W=[g2])
                        k.op('act', lambda e: e.activation(out=g3[:, :], in_=g2[:, :], func=AF.Sigmoid, scale=C2), R=[g2], W=[g3])
                        k.op('dve', lambda e: e.tensor_tensor(out=g2[:, :], in0=g3[:, :], in1=pa[:, :], op=ALU.mult), R=[g3, pa], W=[g2])
                        k.op('dve', lambda e: e.tensor_tensor(out=HT[il][:, :], in0=g2[:, :], in1=pg[il][:, :], op=ALU.mult), R=[g2, pg[il]], W=[HT[il]])
                    for ti in range(4):
                        for dc in range(4):
                            po = self.psum()
                            for il in range(4):
                                k.op('pe', lambda e: e.matmul(po[:, :], lhsT=HT[il][:, ti * 128:(ti + 1) * 128], rhs=Vt[:, il, dc * 512:(dc + 1) * 512], start=(il == 0), stop=(il == 3)),
                                     R=[HT[il], Vt], W=[po])
                            k.op('dve', lambda e: e.tensor_tensor(out=Xh[ti][:, dc * 512:(dc + 1) * 512], in0=Xh[ti][:, dc * 512:(dc + 1) * 512], in1=po[:, :], op=ALU.add),
                                 R=[Xh[ti], po], W=[Xh[ti]])
            k.barrier()
        with ExitStack() as es:
            lnt = [k.sb(es, f"p_ln{i}", [128, D]) for i in range(2)]
            Y = k.sb(es, "p_Y", [128, D])
            for i in range(2):
                k.dma('sp', lnt[i][:, :], W["ln"][:, (2 + i) * D:(3 + i) * D], R=[W["ln"]], W=[lnt[i]])
            for ti in range(8):
                k.op('act', lambda e: e.activation(out=Y[:, :], in_=self.X[ti][:, :], func=AF.Identity), R=[self.X[ti]], W=[Y])
                with ExitStack() as es2:
                    self.layernorm(es2, Y, lnt[0], lnt[1], self.X[ti])
            k.barrier()

    def psum(self):
        p = self.PS[self.psn % 8]
        self.psn += 1
        return p

    def evac(self, out, in_, R, W):
        self.pq += 1
        import os
        fe = os.environ.get("EVAC_ENG")
        if (self.pq % 2 and fe != "dve") or fe == "act":
            self.k.op('act', lambda e: e.activation(out=out, in_=in_, func=AF.Identity), R=R, W=W)
        else:
            self.k.op('dve', lambda e: e.tensor_copy(out=out, in_=in_), R=R, W=W)

    def phase2_inproj(self, l):
        k = self.k
        W = self.W[l]
        with ExitStack() as es:
            WF = k.sb(es, "WF", [128, 16, NWF], BF16)
            WT = k.sb(es, "WT", [128, 16, NWT], BF16)
            XR = [k.sb(es, f"XR{i}", [128, 16, TPC], BF16) for i in range(2)]
            SF = [k.sb(es, f"SF{i}", [128, 1024]) for i in range(2)]
            ST = [k.sb(es, f"ST{i}", [128, NWT]) for i in range(2)]
            k.dma('pool', WF[:, :, :], W["wf"][:, :].rearrange("(a p) c -> p a c", p=128), R=[W["wf"]], W=[WF])
            k.dma('pool', WT[:, :, :], W["wt"][:, :].rearrange("(a p) c -> p a c", p=128), R=[W["wt"]], W=[WT])
            nsf = nst = 0
            for r in range(NCORES):
                xr = XR[r % 2]
                k.dma('pool', xr[:, :, :], self.xT_ext[:, r * TPC:(r + 1) * TPC].rearrange("(a p) t -> p a t", p=128), R=[self.xT_ext], W=[xr])
                for g in range(9):
                    c0 = g * 128
                    m = 128 if g < 8 else 4
                    sf = SF[nsf % 2]
                    nsf += 1
                    for hf in range(2):
                        ps = self.psum()
                        for dk in range(16):
                            k.op('pe', lambda e: e.matmul(ps[0:m, :], lhsT=WF[:, dk, c0:c0 + m], rhs=xr[:, dk, hf * 512:(hf + 1) * 512],
                                                          start=(dk == 0), stop=(dk == 15)), R=[WF, xr], W=[ps])
                        self.evac(sf[0:m, hf * 512:(hf + 1) * 512], ps[0:m, :], R=[ps], W=[sf])
                    k.dma('sp', self.PT[c0:c0 + m, r * TPC:(r + 1) * TPC], sf[0:m, :], R=[sf], W=[self.PT])
                for tt in range(8):
                    st = ST[nst % 2]
                    nst += 1
                    for (n0, n1) in ((0, 512), (512, 768)):
                        ps = self.psum()
                        for dk in range(16):
                            k.op('pe', lambda e: e.matmul(ps[:, 0:n1 - n0], lhsT=xr[:, dk, tt * 128:(tt + 1) * 128], rhs=WT[:, dk, n0:n1],
                                                          start=(dk == 0), stop=(dk == 15)), R=[WT, xr], W=[ps])
                        self.evac(st[:, n0:n1], ps[:, 0:n1 - n0], R=[ps], W=[st])
                    t0 = r * TPC + tt * 128
                    k.dma('sp', self.KV[t0:t0 + 128, :], st[:, :], R=[st], W=[self.KV])
            k.barrier()


    def phase3_mlstm(self, l):
        k = self.k
        W = self.W[l]
        NCH = S // 128
        with ExitStack() as es:
            kT = k.sb(es, "m_kT", [128, 2, S], BF16)
            qT = k.sb(es, "m_qT", [128, 2, S], BF16)
            gb = k.sb(es, "m_gb", [128, 4]); ngb = k.sb(es, "m_ngb", [128, 4])
            mgain = k.sb(es, "m_gain", [128, 256])
            trif = k.sb(es, "m_trif", [128, 128]); trib = k.sb(es, "m_trib", [128, 128])
            G4 = k.sb(es, "m_G4", [64, 4, 128]); l1s = k.sb(es, "m_l1s", [64, 128]); bks = k.sb(es, "m_bks", [64, 128])
            ones64 = k.sb(es, "m_ones64", [64, 128])
            bkcol = [k.sb(es, f"m_bkcol{d}", [128, NCH]) for d in range(2)]
            dec = [k.sb(es, f"m_dec{d}", [128, NCH]) for d in range(2)]
            msk = k.sb(es, "m_msk", [128, 2048])
            lfp = [k.sb(es, f"m_lfp{i}", [128, 2048]) for i in range(2)]
            qst = [k.sb(es, f"m_qst{i}", [128, 2048]) for i in range(2)]
            vst = [k.sb(es, f"m_vst{i}", [128, 512]) for i in range(3)]
            ktb = [k.sb(es, f"m_ktb{i}", [128, 256], BF16) for i in range(3)]
            vtl = [k.sb(es, f"m_vtl{i}", [128, 257], BF16) for i in range(3)]
            sm = [k.sb(es, f"m_sm{i}", [128, 128], BF16) for i in range(2)]
            Cf = k.sb(es, "m_Cf", [128, 2, 257]); Cb = k.sb(es, "m_Cb", [128, 2, 257], BF16); Ct = k.sb(es, "m_Ct", [128, 2, 257])
            rr = [k.sb(es, f"m_rr{i}", [128, 1]) for i in range(2)]
            hst = [k.sb(es, f"m_hst{i}", [128, 256]) for i in range(2)]
            HD = [k.dram(f"m_HD{l}_{d}", [S, 256]) for d in range(2)]
            k.dma('sp', gb[:, :], W["gb"][:, :], R=[W["gb"]], W=[gb])
            k.dma('sp', mgain[:, :], W["mg"][:, :], R=[W["mg"]], W=[mgain])
            k.op('dve', lambda e: e.tensor_single_scalar(out=ngb[:, :], in_=gb[:, :], scalar=-1.0, op=ALU.mult), R=[gb], W=[ngb])
            k.op('pool', lambda e: e.affine_select(out=trif[:, :], in_=self.onesf[:, :], pattern=[[1, 128]], compare_op=ALU.is_ge, fill=0.0,
                                                   base=0, channel_multiplier=-1), R=[self.onesf], W=[trif])
            k.op('pool', lambda e: e.affine_select(out=trib[:, :], in_=self.onesf[:, :], pattern=[[-1, 128]], compare_op=ALU.is_ge, fill=0.0,
                                                   base=0, channel_multiplier=1), R=[self.onesf], W=[trib])
            k.op('pool', lambda e: e.iota(msk[:, :], pattern=[[0, 16], [1, 128]], base=0, channel_multiplier=0,
                                          allow_small_or_imprecise_dtypes=True), W=[msk])
            k.op('dve', lambda e: e.tensor_single_scalar(out=msk[:, :], in_=msk[:, :], scalar=0.0, op=ALU.is_gt), R=[msk], W=[msk])
            k.op('dve', lambda e: e.memset(ones64[:, :], 1.0), W=[ones64])
            for dk in range(2):
                k.dma('pool', kT[:, dk, :], self.PT[256 + dk * 128:256 + (dk + 1) * 128, :], R=[self.PT], W=[kT])
            k.dma('sp', G4[:, :, :], self.PT[1024:1028, :].rearrange("g (j s) -> j g s", s=128), R=[self.PT], W=[G4])
            for d in range(2):
                rev = (d == 1)
                R_ = (lambda ap: ap[:, ::-1]) if rev else (lambda ap: ap)
                k.op('act', lambda e: e.activation(out=l1s[:, :], in_=G4[:, 2 * d + 1, :], func=AF.Exp, scale=-1.0, bias=ngb[0:64, 2 * d + 1:2 * d + 2]),
                     R=[G4, ngb], W=[l1s])
                k.op('act', lambda e: e.activation(out=l1s[:, :], in_=l1s[:, :], func=AF.Ln, bias=self.onesf[0:64, 0:1]), R=[l1s, self.onesf], W=[l1s])
                k.op('dve', lambda e: e.tensor_tensor_scan(out=R_(bks[:, :]), data0=ones64[:, :], data1=R_(l1s[:, :]), initial=0.0,
                                                           op0=ALU.mult, op1=ALU.add), R=[ones64, l1s], W=[bks])
                k.op('dve', lambda e: e.tensor_tensor(out=bks[:, :], in0=bks[:, :], in1=G4[:, 2 * d, :], op=ALU.add), R=[bks, G4], W=[bks])
                k.op('act', lambda e: e.activation(out=bks[:, :], in_=bks[:, :], func=AF.Exp, bias=gb[0:64, 2 * d:2 * d + 1]), R=[bks, gb], W=[bks])
                ps = self.psum()
                k.op('pe', lambda e: e.transpose(out=ps[:, 0:64], in_=bks[:, :], identity=self.identf[0:64, 0:64]), R=[bks, self.identf], W=[ps])
                k.op('dve', lambda e: e.tensor_single_scalar(out=bkcol[d][:, :], in_=ps[:, 0:64], scalar=0.0625, op=ALU.mult), R=[ps], W=[bkcol[d]])
                for pc in range(4):
                    t0 = pc * 2048
                    lf = lfp[pc % 2]
                    k.dma('sp', lf[:, :], self.PT[1024 + 2 * d + 1:1024 + 2 * d + 2, t0:t0 + 2048].partition_broadcast(128), R=[self.PT], W=[lf])
                    k.op('act', lambda e: e.activation(out=lf[:, :], in_=lf[:, :], func=AF.Exp, scale=-1.0, bias=ngb[:, 2 * d + 1:2 * d + 2]),
                         R=[lf, ngb], W=[lf])
                    k.op('act', lambda e: e.activation(out=lf[:, :], in_=lf[:, :], func=AF.Ln, bias=self.onesf[:, 0:1]), R=[lf, self.onesf], W=[lf])
                    k.op('dve', lambda e: e.tensor_tensor_scan(out=R_(lf[:, :]), data0=msk[:, :], data1=R_(lf[:, :]), initial=0.0,
                                                               op0=ALU.mult, op1=ALU.add), R=[msk, lf], W=[lf])
                    k.op('act', lambda e: e.activation(out=lf[:, :], in_=lf[:, :], func=AF.Exp, scale=-1.0), R=[lf], W=[lf])
                    o0 = 0 if rev else 127
                    k.op('dve', lambda e: e.tensor_copy(out=dec[d][:, pc * 16:(pc + 1) * 16], in_=lf[:, o0::128]), R=[lf], W=[dec[d]])
                    for dk in range(2):
                        qs = qst[dk]
                        k.dma('sp', qs[:, :], self.PT[dk * 128:(dk + 1) * 128, t0:t0 + 2048], R=[self.PT], W=[qs])
                        k.op('pool' if dk else 'dve', lambda e: e.tensor_tensor(out=qT[:, dk, t0:t0 + 2048], in0=qs[:, :], in1=lf[:, :], op=ALU.mult),
                             R=[qs, lf], W=[qT])
                k.op('dve', lambda e: e.memset(Cf[:, :, :], 0.0), W=[Cf])
                k.op('dve', lambda e: e.memset(Cb[:, :, :], 0.0), W=[Cb])
                tri = trib if rev else trif
                for ji in range(NCH):
                    j = (NCH - 1 - ji) if rev else ji
                    t0 = j * 128
                    vs, kb, vt, smm, r1, hs = vst[ji % 3], ktb[ji % 3], vtl[ji % 3], sm[ji % 2], rr[ji % 2], hst[ji % 2]
                    k.dma('sp', vs[:, :], self.KV[t0:t0 + 128, 0:512], R=[self.KV], W=[vs])
                    k.op('act', lambda e: e.activation(out=kb[:, :], in_=vs[:, 256:512], func=AF.Identity), R=[vs], W=[kb])
                    k.op('dve', lambda e: e.tensor_scalar(out=vt[:, 0:256], in0=vs[:, 0:256], scalar1=bkcol[d][:, j:j + 1], scalar2=None, op0=ALU.mult),
                         R=[vs, bkcol[d]], W=[vt])
                    k.op('act', lambda e: e.activation(out=vt[:, 256:257], in_=bkcol[d][:, j:j + 1], func=AF.Identity), R=[bkcol[d]], W=[vt])
                    ps_s = self.psum()
                    for dk in range(2):
                        k.op('pe', lambda e: e.matmul(ps_s[:, 0:128], lhsT=kT[:, dk, t0:t0 + 128], rhs=qT[:, dk, t0:t0 + 128], start=(dk == 0), stop=(dk == 1)),
                             R=[kT, qT], W=[ps_s])
                    k.op('dve', lambda e: e.tensor_tensor(out=smm[:, :], in0=ps_s[:, 0:128], in1=tri[:, :], op=ALU.mult), R=[ps_s, tri], W=[smm])
                    ps_n = self.psum()
                    k.op('pe', lambda e: e.matmul(ps_n[:, 0:257], lhsT=smm[:, :], rhs=vt[:, :], start=True, stop=False), R=[smm, vt], W=[ps_n])
                    for dk in range(2):
                        k.op('pe', lambda e: e.matmul(ps_n[:, 0:257], lhsT=qT[:, dk, t0:t0 + 128], rhs=Cb[:, dk, :], start=False, stop=(dk == 1)),
                             R=[qT, Cb], W=[ps_n])
                    for dk in range(2):
                        ps_u = self.psum()
                        k.op('pe', lambda e: e.matmul(ps_u[:, 0:257], lhsT=kb[:, dk * 128:(dk + 1) * 128], rhs=vt[:, :], start=True, stop=True),
                             R=[kb, vt], W=[ps_u])
                        k.op('dve', lambda e: e.tensor_tensor(out=Ct[:, dk, :], in0=ps_u[:, 0:257], in1=Cf[:, dk, :], op=ALU.add), R=[ps_u, Cf], W=[Ct])
                    k.op('act', lambda e: e.activation(out=Cf[:, :, :], in_=Ct[:, :, :], func=AF.Copy, scale=dec[d][:, j:j + 1]), R=[Ct, dec[d]], W=[Cf])
                    k.op('act', lambda e: e.activation(out=Cb[:, :, :], in_=Ct[:, :, :], func=AF.Copy, scale=dec[d][:, j:j + 1]), R=[Ct, dec[d]], W=[Cb])
                    k.op('act', lambda e: e.activation(out=r1[:, :], in_=ps_n[:, 256:257], func=AF.Abs), R=[ps_n], W=[r1])
                    k.op('dve', lambda e: e.tensor_single_scalar(out=r1[:, :], in_=r1[:, :], scalar=1.0, op=ALU.max), R=[r1], W=[r1])
                    k.op('dve', lambda e: e.reciprocal(out=r1[:, :], in_=r1[:, :]), R=[r1], W=[r1])
                    k.op('act', lambda e: e.activation(out=hs[:, :], in_=ps_n[:, 0:256], func=AF.Copy, scale=r1[:, 0:1]), R=[ps_n, r1], W=[hs])
                    k.dma('sp', HD[d][t0:t0 + 128, :], hs[:, :], R=[hs], W=[HD[d]])
            hA = [k.sb(es, f"m_hA{i}", [128, 256]) for i in range(2)]
            hB = [k.sb(es, f"m_hB{i}", [128, 256]) for i in range(2)]
            oo = [k.sb(es, f"m_oo{i}", [128, 256]) for i in range(2)]
            st6 = [k.sb(es, f"m_st6{i}", [128, 6]) for i in range(2)]
            mv = [k.sb(es, f"m_mv{i}", [128, 2]) for i in range(2)]
            hT = [k.sb(es, f"m_hT{i}", [128, 2, 128], BF16) for i in range(2)]
            for j in range(NCH):
                t0 = j * 128
                a, b, o, s6, m2, ht = hA[j % 2], hB[j % 2], oo[j % 2], st6[j % 2], mv[j % 2], hT[j % 2]
                k.dma('sp', a[:, :], HD[0][t0:t0 + 128, :], R=[HD[0]], W=[a])
                k.dma('sp', b[:, :], HD[1][t0:t0 + 128, :], R=[HD[1]], W=[b])
                k.dma('sp', o[:, :], self.KV[t0:t0 + 128, 512:768], R=[self.KV], W=[o])
                k.op('act', lambda e: e.activation(out=o[:, :], in_=o[:, :], func=AF.Sigmoid), R=[o], W=[o])
                k.op('dve', lambda e: e.tensor_tensor(out=a[:, :], in0=a[:, :], in1=b[:, :], op=ALU.add), R=[a, b], W=[a])
                k.op('dve', lambda e: e.tensor_tensor(out=a[:, :], in0=a[:, :], in1=o[:, :], op=ALU.mult), R=[a, o], W=[a])
                k.op('dve', lambda e: e.bn_stats(out=s6[:, :], in_=a[:, :]), R=[a], W=[s6])
                k.op('dve', lambda e: e.bn_aggr(out=m2[:, :], in_=s6[:, :]), R=[s6], W=[m2])
                self.rstd(m2)
                k.op('dve', lambda e: e.tensor_scalar(out=a[:, :], in0=a[:, :], scalar1=m2[:, 0:1], scalar2=m2[:, 1:2], op0=ALU.subtract, op1=ALU.mult),
                     R=[a, m2], W=[a])
                k.op('dve', lambda e: e.tensor_tensor(out=a[:, :], in0=a[:, :], in1=mgain[:, :], op=ALU.mult), R=[a, mgain], W=[a])
                ps = self.psum()
                for dk in range(2):
                    k.op('pe', lambda e: e.transpose(out=ps[:, dk * 128:(dk + 1) * 128], in_=a[:, dk * 128:(dk + 1) * 128], identity=self.identf[:, :]),
                         R=[a, self.identf], W=[ps])
                self.evac(ht[:, :, :], ps[:, 0:256].rearrange("p (a b) -> p a b", a=2), R=[ps], W=[ht])
                k.dma('sp', self.HN_sh[:, t0:t0 + 128].rearrange("(a p) t -> p a t", p=128), ht[:, :, :], R=[ht], W=[self.HN_sh])
            k.barrier()


_PROGS = {}


def _prog(mode):
    if mode not in _PROGS:
        p = Prog(mode)
        _PROGS[mode] = p.build()
    return _PROGS[mode]


def kernel(**inputs):
    inp = {k_: np.asarray(v) for k_, v in inputs.items()}
    x = np.ascontiguousarray(inp["x"][0]).astype(np.float32)
    cores = list(range(NCORES))
    for l in range(DEPTH):
        xT = np.ascontiguousarray(x.T)
        resA = run_bass_kernel_spmd(_prog('A'), [_inputs_A(inp, l, c, xT) for c in cores], core_ids=cores).results
        hn = np.concatenate([resA[c]["HN_sh"] for c in range(4)], axis=0)
        hy = np.concatenate([resA[c]["HY_sh"] for c in cores], axis=0)
        s5 = np.concatenate([resA[c]["S5_sh"] for c in cores], axis=0)
        del resA
        shared = _shared_B(inp, l)
        in_maps = []
        bcores = list(range(NCB))
        tpb = NBLK * TPC
        for c in bcores:
            ts = slice(c * tpb, (c + 1) * tpb)
            m = dict(shared)
            m["x_tok"] = np.ascontiguousarray(x[ts])
            m["xTs"] = np.ascontiguousarray(xT[:, ts])
            m["hn"] = np.ascontiguousarray(hn[:, ts]); m["hy"] = np.ascontiguousarray(hy[:, ts]); m["s5"] = np.ascontiguousarray(s5[:, ts])
            in_maps.append(m)
        resB = run_bass_kernel_spmd(_prog('B'), in_maps, core_ids=bcores).results
        x = np.concatenate([resB[c]["out"] for c in bcores], axis=0)
        del resB, in_maps, shared
    return np.ascontiguousarray(x[None]).astype(np.float32)
```

```python
import math
from contextlib import ExitStack
import numpy as np
import concourse.bass as bass
import concourse.mybir as mybir
from concourse.bass_utils import run_bass_kernel_spmd

F32 = mybir.dt.float32
BF16 = mybir.dt.bfloat16
U32 = mybir.dt.uint32
AF = mybir.ActivationFunctionType
ALU = mybir.AluOpType
AX = mybir.AxisListType

NCORES = 8
D = 2048
S = 8192
DEPTH = 4
TPC = S // NCORES
NCB = 4
NBLK = S // TPC // NCB
BW = 1024
NWF = 1028
NWT = 768
ALPHA = (2 * DEPTH) ** 0.25
LN_EPS = 1e-5
TWO_PI = 2.0 * math.pi


class T:
    def __init__(self, t, name):
        self.t = t
        self.name = name
        self.w = None
        self.r = {}

    def __getitem__(self, idx):
        return self.t[idx]


class K:
    NDS = 24
    EPOCH = 30000

    def __init__(self, nc, es):
        self.nc = nc
        self.es = es
        self.eng = {'pe': nc.tensor, 'dve': nc.vector, 'act': nc.scalar, 'pool': nc.gpsimd, 'sp': nc.sync}
        self.nsem = 0
        self.sem = {k: self._newsem() for k in self.eng}
        self.semid = {k: (k, 0) for k in self.eng}
        self.cnt = {k: 0 for k in self.eng}
        self.waited = {k: {} for k in self.eng}
        self.dsem = [self._newsem() for _ in range(self.NDS)]
        self.dcnt = [0] * self.NDS
        self.dnext = 0
        self.last = {}
        self.ninst = 0

    def _newsem(self):
        self.nsem += 1
        return self.es.enter_context(self.nc.semaphore(f"s{self.nsem}"))

    def _wait(self, e, ev):
        if ev is None:
            return
        sem, val, key = ev
        if key == ('pe', self.semid['pe'][1]) and e == 'pe':
            return
        if self.waited[e].get(key, 0) >= val:
            return
        self.eng[e].wait_ge(sem, val)
        self.waited[e][key] = val

    def deps(self, e, R, W):
        for b in R:
            self._wait(e, b.w)
        for b in W:
            self._wait(e, b.w)
            for ev in b.r.values():
                self._wait(e, ev)

    def commit(self, ev, R, W):
        self.last[ev[2]] = ev
        for b in R:
            b.r[ev[2]] = ev
        for b in W:
            b.w = ev
            b.r = {}

    def op(self, e, fn, R=(), W=()):
        self.deps(e, R, W)
        if self.cnt[e] >= self.EPOCH:
            self.sem[e] = self._newsem()
            self.semid[e] = (e, self.semid[e][1] + 1)
            self.cnt[e] = 0
        ins = fn(self.eng[e])
        self.cnt[e] += 1
        ins.then_inc(self.sem[e], 1)
        ev = (self.sem[e], self.cnt[e], self.semid[e])
        self.commit(ev, R, W)
        self.ninst += 1
        return ev

    def dma(self, q, out, in_, R=(), W=(), **kw):
        i = self.dnext
        self.dnext = (i + 1) % self.NDS
        if self.dcnt[i] > 0:
            self._wait(q, (self.dsem[i], 16 * self.dcnt[i], ('d', i)))
        self.deps(q, R, W)
        self.eng[q].dma_start(out=out, in_=in_, **kw).then_inc(self.dsem[i], 16)
        self.dcnt[i] += 1
        ev = (self.dsem[i], 16 * self.dcnt[i], ('d', i))
        self.commit(ev, R, W)
        self.ninst += 1
        return ev

    def allgather(self, out, in_, R=(), W=()):
        e = 'pool'
        self.deps(e, R, W)
        sem = self._newsem()
        self.nc.gpsimd.collective_compute("AllGather", ALU.bypass, replica_groups=[list(range(NCORES))],
                                          ins=[in_], outs=[out]).then_inc(sem, 1)
        ev = (sem, 1, ('cc', self.nsem))
        self.commit(ev, R, W)
        return ev

    def barrier(self):
        for e in self.eng:
            for ev in list(self.last.values()):
                self._wait(e, ev)

    def sb(self, es, name, shape, dt=F32):
        self.nsem += 1
        name = f"{name}_{self.nsem}"
        return T(es.enter_context(self.nc.sbuf_tensor(name, list(shape), dt)), name)

    def ps(self, es, name, shape, dt=F32):
        return T(es.enter_context(self.nc.psum_tensor(name, list(shape), dt)), name)

    def dram(self, name, shape, dt=F32, kind="Internal"):
        return T(self.nc.dram_tensor(name, list(shape), dt, kind=kind), name)


def _inputs_A(inp, l, c, xT):
    h = c % 4
    m = {}
    m["xT"] = xT
    w = inp["w_in"][l]
    q0 = h * 256
    colsF = np.concatenate([
        np.arange(q0, q0 + 256), 1024 + np.arange(q0, q0 + 256),
        4112 + 0 * 1024 + c * 128 + np.arange(128), 4112 + 1 * 1024 + c * 128 + np.arange(128),
        4112 + 2 * 1024 + c * 128 + np.arange(128), 7184 + c * 128 + np.arange(128),
        4096 + np.array([0 * 8 + 0 * 4 + h, 0 * 8 + 1 * 4 + h, 1 * 8 + 0 * 4 + h, 1 * 8 + 1 * 4 + h]),
    ])
    colsT = np.concatenate([2048 + np.arange(q0, q0 + 256), 1024 + np.arange(q0, q0 + 256), 3072 + np.arange(q0, q0 + 256)])
    m["wf"] = np.ascontiguousarray(w[:, colsF])
    m["wt"] = np.ascontiguousarray(w[:, colsT])
    gb = inp["mlstm_gate_bias"][l][:, :, h].reshape(4)
    m["gb"] = np.ascontiguousarray(np.broadcast_to(gb[None, :], (128, 4))).astype(np.float32)
    m["mg"] = np.ascontiguousarray(np.broadcast_to(inp["mlstm_norm_gain"][l][None, q0:q0 + 256], (128, 256))).astype(np.float32)
    p3 = np.zeros((128, 8, 3), np.float32); sB = np.zeros((128, 8, 2, 16), np.float32); sC = np.zeros((128, 8, 2, 16), np.float32)
    for d_ in range(2):
        for pr in range(4):
            for g2 in range(2):
                g = 8 * c + 2 * pr + g2
                rs = slice(g2 * 64, (g2 + 1) * 64)
                t_ = d_ * 4 + pr
                p3[rs, t_, 0] = inp["s5_lambda_re"][l][d_, g]
                p3[rs, t_, 1] = inp["s5_lambda_im"][l][d_, g]
                p3[rs, t_, 2] = inp["s5_log_step"][l][d_, g]
                sB[rs, t_, 0] = inp["s5_b_re"][l][d_, g]
                sB[rs, t_, 1] = inp["s5_b_im"][l][d_, g]
                sC[rs, t_, 0] = inp["s5_c_re"][l][d_, g].T
                sC[rs, t_, 1] = inp["s5_c_im"][l][d_, g].T
    m["s5p"] = p3; m["s5B"] = sB; m["s5C"] = sC
    m["s5k"] = np.ascontiguousarray(inp["s5_skip"][l][c * 128:(c + 1) * 128].reshape(128, 1))
    w1 = inp["hyena_w1"][l]
    w1p = np.zeros((65, 64), np.float32); w1p[0:16] = w1[1:17]; w1p[32:48] = w1[17:33]; w1p[64] = w1[0]
    m["hw1"] = w1p
    col = lambda a: np.ascontiguousarray(a.reshape(-1, 1)).astype(np.float32)
    m["hb1"] = col(inp["hyena_b1"][l]); m["hb2"] = col(inp["hyena_b2"][l])
    m["hf0"] = col(inp["hyena_freq"][l][0]); m["hf1"] = col(inp["hyena_freq"][l][1])
    m["hw2"] = np.ascontiguousarray(inp["hyena_w2"][l])
    w3 = inp["hyena_w3"][l].reshape(64, 2, 1024)
    m["hw3"] = np.ascontiguousarray(w3[:, :, c * 128:(c + 1) * 128])
    m["hdec"] = np.ascontiguousarray(inp["hyena_decay"][l][:, c * 128:(c + 1) * 128].T)
    m["hsk"] = col(inp["hyena_skip"][l][c * 128:(c + 1) * 128])
    cw = np.zeros((128, 3, 4), np.float32)
    for j_ in range(3):
        sl_ = slice(j_ * 1024 + c * 128, j_ * 1024 + (c + 1) * 128)
        cw[:, j_, 0:3] = inp["hyena_conv_w"][l][:, sl_].T
        cw[:, j_, 3] = inp["hyena_conv_b"][l][sl_]
    m["hcw"] = cw
    return m


def _shared_B(inp, l):
    m = {}
    m["wg"] = np.ascontiguousarray(inp["w_in"][l][:, 8208:])
    m["wmo"] = np.ascontiguousarray(inp["w_mlstm_out"][l]); m["who"] = np.ascontiguousarray(inp["w_hyena_out"][l])
    m["wsg"] = np.ascontiguousarray(inp["w_s5_glu"][l]); m["wo"] = np.ascontiguousarray(inp["w_out"][l])
    m["wq"] = np.ascontiguousarray(inp["peer_w_q"][l])
    m["pu"] = np.ascontiguousarray(inp["peer_u"][l].reshape(8, 2048, 2048).transpose(0, 2, 1).reshape(16384, 2048))
    m["pv"] = np.ascontiguousarray(inp["peer_v"][l])
    m["psk"] = np.ascontiguousarray(inp["peer_subkeys"][l].reshape(16, 128, 128).transpose(2, 0, 1))
    lnrow = np.concatenate([inp["ln1_g"][l], inp["ln1_b"][l], inp["ln2_g"][l], inp["ln2_b"][l]])
    m["ln"] = np.ascontiguousarray(np.broadcast_to(lnrow[None, :], (128, 4 * D))).astype(np.float32)
    return m


class Prog:
    def __init__(self, mode, debug=False, phases="mshtp"):
        self.mode = mode
        self.phases = phases
        self.debug = debug
        self.nc = bass.Bass("TRN2", target_bir_lowering=False)
        self.es = ExitStack()

    def ext(self, name, shape, dt=F32):
        return self.k.dram(name, shape, dt, kind="ExternalInput")

    def build(self):
        nc, es = self.nc, self.es
        with es:
            k = self.k = K(nc, es)
            self.pq = 0
            self.identf = k.sb(es, "identf", [128, 128])
            self.identb = k.sb(es, "identb", [128, 128], BF16)
            self.onesf = k.sb(es, "onesf", [128, 128])
            self.PS = [k.ps(es, f"PS{i}", [128, 512]) for i in range(8)]
            self.psn = 0
            k.op('dve', lambda e: e.memset(self.onesf[:, :], 1.0), W=[self.onesf])
            self.epst = k.sb(es, "epst", [128, 1])
            k.op('dve', lambda e: e.memset(self.epst[:, :], LN_EPS), W=[self.epst])
            k.op('pool', lambda e: e.affine_select(out=self.identf[:, :], in_=self.onesf[:, :], pattern=[[-1, 128]], compare_op=ALU.is_equal,
                                                   fill=0.0, base=0, channel_multiplier=1), R=[self.onesf], W=[self.identf])
            k.op('dve', lambda e: e.tensor_copy(out=self.identb[:, :], in_=self.identf[:, :]), R=[self.identf], W=[self.identb])
            if self.mode == 'A':
                self.build_A()
            else:
                self.build_B()
        return nc

    def build_A(self):
        k, es = self.k, self.es
        d = {}
        for nm, shp in [("wf", [D, NWF]), ("wt", [D, NWT]), ("gb", [128, 4]), ("mg", [128, 256]),
                        ("hw1", [65, 64]), ("hb1", [64, 1]), ("hb2", [64, 1]), ("hf0", [64, 1]), ("hf1", [64, 1]), ("hw2", [64, 64]),
                        ("hw3", [64, 2, 128]), ("hdec", [128, 2]), ("hsk", [128, 1]), ("hcw", [128, 3, 4]),
                        ("s5p", [128, 8, 3]), ("s5B", [128, 8, 2, 16]), ("s5C", [128, 8, 2, 16]), ("s5k", [128, 1])]:
            d[nm] = self.ext(nm, shp)
        self.W = [d]
        self.xT_ext = self.ext("xT", [D, S])
        self.PT = k.dram("PT", [NWF, S], F32)
        self.KV = k.dram("KV", [S, NWT], F32)
        self.HN_sh = k.dram("HN_sh", [256, S], BF16, kind="ExternalOutput")
        self.S5_sh = k.dram("S5_sh", [128, S], BF16, kind="ExternalOutput")
        self.HY_sh = k.dram("HY_sh", [128, S], BF16, kind="ExternalOutput")
        self.phase2_inproj(0)
        if 'm' in self.phases:
            self.phase3_mlstm(0)
        if 's' in self.phases:
            self.phase3_s5(0)
        if 'h' in self.phases:
            self.phase3_hyena(0)
        for t in (self.HN_sh, self.S5_sh, self.HY_sh):
            k._wait('sp', t.w)

    def build_B(self):
        k, es = self.k, self.es
        d = {}
        import os
        notab = bool(os.environ.get("B_NOTAB"))
        for nm, shp in [("wg", [D, 6144]), ("wmo", [BW, D]), ("who", [BW, D]), ("wsg", [BW, 2 * D]), ("wo", [D, D]), ("wq", [D, D]),
                        ("pu", [16384, 2048]), ("pv", [16384, D]), ("psk", [128, 16, 128]), ("ln", [128, 4 * D])]:
            if notab and nm in ("pu", "pv"):
                continue
            if 't' not in self.phases and nm in ("wg", "wmo", "who", "wsg", "wo"):
                continue
            d[nm] = self.ext(nm, shp)
        self.W = [d]
        import os as _os
        nblk = int(_os.environ.get("B_NBLK", str(NBLK)))
        self.x_tok = self.ext("x_tok", [nblk * TPC, D])
        xTs = self.ext("xTs", [D, nblk * TPC])
        self.actin = [self.ext(nm, [BW, nblk * TPC], BF16) for nm in ("hn", "hy", "s5")] if 't' in self.phases else []
        self.out = k.dram("out", [nblk * TPC, D], F32, kind="ExternalOutput")
        self.xT_sh = k.dram("xT_sh", [D, TPC], BF16)
        if self.debug:
            self.dbgX1 = k.dram("dbgX1", [nblk * TPC, D], F32, kind="ExternalOutput")
        self.X = [k.sb(es, f"X{i}", [128, D]) for i in range(8)]
        g = {}
        for nm, rows, cols in [("wg", D, 6144), ("wmo", BW, D), ("who", BW, D), ("wsg", BW, 2 * D), ("wo", D, D), ("wq", D, D),
                               ("pu", 16384, 2048), ("pv", 16384, D)]:
            if nm not in d:
                continue
            full = k.dram(f"{nm}F", [rows, cols], BF16)
            nch = 8 if rows >= 8192 else 1
            rr = rows // nch
            for i in range(nch):
                k.dma('pool', full[i * rr:(i + 1) * rr, :], d[nm][i * rr:(i + 1) * rr, :], R=[d[nm]], W=[full])
            g[nm] = full
        self.GW = [g]
        for blk in range(nblk):
            self.blk = blk
            b0 = blk * TPC
            for i in range(8):
                k.dma('sp', self.X[i][:, :], self.x_tok[b0 + i * 128:b0 + (i + 1) * 128, :], R=[self.x_tok], W=[self.X[i]])
            k.dma('pool', self.xT_sh[:, :], xTs[:, b0:b0 + TPC], R=[xTs], W=[self.xT_sh])
            if 't' in self.phases:
                self.phase4_token(0)
            if self.debug:
                for i in range(8):
                    k.dma('sp', self.dbgX1[b0 + i * 128:b0 + (i + 1) * 128, :], self.X[i][:, :], R=[self.X[i]], W=[self.dbgX1])
            if 'p' in self.phases:
                self.phase5_peer(0)
            for i in range(8):
                k.dma('sp', self.out[b0 + i * 128:b0 + (i + 1) * 128, :], self.X[i][:, :], R=[self.X[i]], W=[self.out])
        k._wait('sp', self.out.w)
        if self.debug:
            k._wait('sp', self.dbgX1.w)

    def rstd(self, m2):
        k = self.k
        k.op('act', lambda e: e.activation(out=m2[:, 1:2], in_=m2[:, 1:2], func=AF.Ln, bias=self.epst[:, 0:1]), R=[m2, self.epst], W=[m2])
        k.op('act', lambda e: e.activation(out=m2[:, 1:2], in_=m2[:, 1:2], func=AF.Exp, scale=-0.5), R=[m2], W=[m2])

    def sctmp(self, es, n):
        k = self.k
        return (k.sb(es, "sc_ki", [128, n], mybir.dt.int32), k.sb(es, "sc_kf", [128, n]), k.sb(es, "sc_y", [128, n]))

    def sinr(self, tmp, srcT, src, n, outT, out, shift, p0=0, p1=128):
        k = self.k
        ki_, kf_, y_ = tmp
        ki, kf, y = ki_.t[p0:p1, 0:n], kf_.t[p0:p1, 0:n], y_.t[p0:p1, 0:n]
        k.op('dve', lambda e: e.tensor_scalar(out=y, in0=src, scalar1=shift, scalar2=None, op0=ALU.add), R=[srcT], W=[y_])
        k.op('dve', lambda e: e.tensor_single_scalar(out=ki, in_=y, scalar=1.0 / TWO_PI, op=ALU.mult), R=[y_], W=[ki_])
        k.op('dve', lambda e: e.tensor_copy(out=kf, in_=ki), R=[ki_], W=[kf_])
        k.op('dve', lambda e: e.scalar_tensor_tensor(out=y, in0=kf, scalar=-TWO_PI, in1=y, op0=ALU.mult, op1=ALU.add), R=[kf_, y_], W=[y_])
        k.op('dve', lambda e: e.tensor_scalar(out=y, in0=y, scalar1=-3.1415925, scalar2=3.1415925, op0=ALU.max, op1=ALU.min), R=[y_], W=[y_])
        k.op('act', lambda e: e.activation(out=out, in_=y, func=AF.Sin), R=[y_], W=[outT])

    def sincos(self, tmp, angT, ang, n, sinT, sin_out, cosT, cos_out):
        k = self.k
        ki_, kf_, y_ = tmp

        class V:
            def __init__(s_, t):
                s_.t = t

            def __getitem__(s_, idx):
                return s_.t.t[:, 0:n]
        ki, kf, y = V(ki_), V(kf_), V(y_)
        for shift, oT, out in ((0.0, sinT, sin_out), (0.5 * math.pi, cosT, cos_out)):
            k.op('dve', lambda e: e.tensor_scalar(out=y[:, :], in0=ang, scalar1=shift, scalar2=None, op0=ALU.add), R=[angT], W=[y_])
            k.op('dve', lambda e: e.tensor_single_scalar(out=ki[:, :], in_=y[:, :], scalar=1.0 / TWO_PI, op=ALU.mult), R=[y_], W=[ki_])
            k.op('dve', lambda e: e.tensor_copy(out=kf[:, :], in_=ki[:, :]), R=[ki_], W=[kf_])
            k.op('dve', lambda e: e.scalar_tensor_tensor(out=y[:, :], in0=kf[:, :], scalar=-TWO_PI, in1=y[:, :], op0=ALU.mult, op1=ALU.add), R=[kf_, y_], W=[y_])
            k.op('dve', lambda e: e.tensor_scalar(out=y[:, :], in0=y[:, :], scalar1=-3.1415925, scalar2=3.1415925, op0=ALU.max, op1=ALU.min), R=[y_], W=[y_])
            k.op('act', lambda e: e.activation(out=out, in_=y[:, :], func=AF.Sin), R=[y_], W=[oT])

    def phase3_s5(self, l):
        k = self.k
        W = self.W[l]
        PL = 1024
        NP = S // PL
        TT = lambda *a, **kw: None
        with ExitStack() as es:
            p3 = k.sb(es, "s_p3", [128, 8, 3]); sB = k.sb(es, "s_B", [128, 8, 2, 16]); sC = k.sb(es, "s_C", [128, 8, 2, 16]); sk = k.sb(es, "s_k", [128, 1])
            k.dma('sp', p3[:, :, :], W["s5p"][:, :, :], R=[W["s5p"]], W=[p3])
            k.dma('sp', sB[:, :, :, :], W["s5B"][:, :, :, :], R=[W["s5B"]], W=[sB])
            k.dma('sp', sC[:, :, :, :], W["s5C"][:, :, :, :], R=[W["s5C"]], W=[sC])
            k.dma('sp', sk[:, :], W["s5k"][:, :], R=[W["s5k"]], W=[sk])
            uT = k.sb(es, "s_uT", [128, S], BF16)
            k.dma('pool', uT[:, :], self.PT[896:1024, :], R=[self.PT], W=[uT])
            acc = k.sb(es, "s_acc", [128, S])
            P = {n: k.sb(es, f"s_{n}", [128, 8]) for n in ["step", "th", "rho", "r", "sn", "cs", "are", "aim", "nre", "nim", "den", "fre", "fim",
                                                            "t1", "t2", "thL", "snL", "csL", "nth"]}
            lre, lim, lst = p3[:, :, 0], p3[:, :, 1], p3[:, :, 2]

            def tt(o, a, b, op, R, Wt):
                k.op('dve', lambda e: e.tensor_tensor(out=o, in0=a, in1=b, op=op), R=R, W=Wt)
            k.op('act', lambda e: e.activation(out=P["step"][:, :], in_=lst, func=AF.Exp), R=[p3], W=[P["step"]])
            tt(P["th"][:, :], lim, P["step"][:, :], ALU.mult, [p3, P["step"]], [P["th"]])
            tt(P["rho"][:, :], lre, P["step"][:, :], ALU.mult, [p3, P["step"]], [P["rho"]])
            k.op('act', lambda e: e.activation(out=P["r"][:, :], in_=P["rho"][:, :], func=AF.Exp), R=[P["rho"]], W=[P["r"]])
            scs = self.sctmp(es, PL)
            self.sincos(scs, P["th"], P["th"][:, :], 8, P["sn"], P["sn"][:, :], P["cs"], P["cs"][:, :])
            k.op('dve', lambda e: e.tensor_single_scalar(out=P["thL"][:, :], in_=P["th"][:, :], scalar=float(PL), op=ALU.mult), R=[P["th"]], W=[P["thL"]])
            self.sincos(scs, P["thL"], P["thL"][:, :], 8, P["snL"], P["snL"][:, :], P["csL"], P["csL"][:, :])
            tt(P["are"][:, :], P["r"][:, :], P["cs"][:, :], ALU.mult, [P["r"], P["cs"]], [P["are"]])
            tt(P["aim"][:, :], P["r"][:, :], P["sn"][:, :], ALU.mult, [P["r"], P["sn"]], [P["aim"]])
            k.op('dve', lambda e: e.tensor_single_scalar(out=P["t1"][:, :], in_=P["are"][:, :], scalar=-1.0, op=ALU.add), R=[P["are"]], W=[P["t1"]])
            tt(P["nre"][:, :], P["t1"][:, :], lre, ALU.mult, [P["t1"], p3], [P["nre"]])
            tt(P["t2"][:, :], P["aim"][:, :], lim, ALU.mult, [P["aim"], p3], [P["t2"]])
            tt(P["nre"][:, :], P["nre"][:, :], P["t2"][:, :], ALU.add, [P["nre"], P["t2"]], [P["nre"]])
            tt(P["nim"][:, :], P["aim"][:, :], lre, ALU.mult, [P["aim"], p3], [P["nim"]])
            tt(P["t2"][:, :], P["t1"][:, :], lim, ALU.mult, [P["t1"], p3], [P["t2"]])
            tt(P["nim"][:, :], P["nim"][:, :], P["t2"][:, :], ALU.subtract, [P["nim"], P["t2"]], [P["nim"]])
            tt(P["den"][:, :], lre, lre, ALU.mult, [p3], [P["den"]])
            tt(P["t2"][:, :], lim, lim, ALU.mult, [p3], [P["t2"]])
            tt(P["den"][:, :], P["den"][:, :], P["t2"][:, :], ALU.add, [P["den"], P["t2"]], [P["den"]])
            k.op('dve', lambda e: e.reciprocal(out=P["den"][:, :], in_=P["den"][:, :]), R=[P["den"]], W=[P["den"]])
            tt(P["fre"][:, :], P["nre"][:, :], P["den"][:, :], ALU.mult, [P["nre"], P["den"]], [P["fre"]])
            tt(P["fim"][:, :], P["nim"][:, :], P["den"][:, :], ALU.mult, [P["nim"], P["den"]], [P["fim"]])
            k.op('dve', lambda e: e.tensor_single_scalar(out=P["nth"][:, :], in_=P["th"][:, :], scalar=-1.0, op=ALU.mult), R=[P["th"]], W=[P["nth"]])
            bbr = k.sb(es, "s_bbr", [128, 8, 16]); bbi = k.sb(es, "s_bbi", [128, 8, 16]); tb = k.sb(es, "s_tb", [128, 8, 16])
            fre_b = P["fre"][:, :].unsqueeze(2).to_broadcast([128, 8, 16]); fim_b = P["fim"][:, :].unsqueeze(2).to_broadcast([128, 8, 16])
            tt(bbr[:, :, :], sB[:, :, 0, :], fre_b, ALU.mult, [sB, P["fre"]], [bbr])
            tt(tb[:, :, :], sB[:, :, 1, :], fim_b, ALU.mult, [sB, P["fim"]], [tb])
            tt(bbr[:, :, :], bbr[:, :, :], tb[:, :, :], ALU.subtract, [bbr, tb], [bbr])
            tt(bbi[:, :, :], sB[:, :, 1, :], fre_b, ALU.mult, [sB, P["fre"]], [bbi])
            tt(tb[:, :, :], sB[:, :, 0, :], fim_b, ALU.mult, [sB, P["fim"]], [tb])
            tt(bbi[:, :, :], bbi[:, :, :], tb[:, :, :], ALU.add, [bbi, tb], [bbi])
            iota0 = k.sb(es, "s_iota", [128, PL]); ang = k.sb(es, "s_ang", [128, PL]); COS = k.sb(es, "s_cos", [128, PL]); SIN = k.sb(es, "s_sin", [128, PL])
            k.op('pool', lambda e: e.iota(iota0[:, :], pattern=[[1, PL]], base=0, channel_multiplier=0, allow_small_or_imprecise_dtypes=True), W=[iota0])
            pad = k.sb(es, "s_pad", [128, 128])
            LB = [k.sb(es, f"s_LB{i}", [128, 128], BF16) for i in range(2)]
            LC = [k.sb(es, f"s_LC{i}", [128, 128], BF16) for i in range(2)]
            mre = k.sb(es, "s_mre", [128, PL]); mim = k.sb(es, "s_mim", [128, PL])
            wre = k.sb(es, "s_wre", [128, PL]); wim = k.sb(es, "s_wim", [128, PL])
            xre = k.sb(es, "s_xre", [128, PL], BF16); xim = k.sb(es, "s_xim", [128, PL], BF16)
            tq = [k.sb(es, f"s_tq{i}", [128, 512]) for i in range(4)]
            tw = [k.sb(es, f"s_tw{i}", [128, PL]) for i in range(2)]
            ini = k.sb(es, "s_ini", [128, 2]); ini2 = k.sb(es, "s_ini2", [128, 2]); itmp = k.sb(es, "s_itmp", [128, 2])
            first_acc = [True] * (S // 512)
            scb = scs
            for d in range(2):
                rev = (d == 1)
                R_ = (lambda ap: ap[:, ::-1]) if rev else (lambda ap: ap)
                for pr in range(4):
                    t_ = d * 4 + pr
                    c0 = 32 * pr
                    for which, src in ((0, bbr), (1, bbi)):
                        k.op('dve', lambda e: e.memset(pad[:, :], 0.0), W=[pad])
                        k.op('dve', lambda e: e.tensor_copy(out=pad[0:64, c0:c0 + 16], in_=src[0:64, t_, :]), R=[src], W=[pad])
                        k.op('dve', lambda e: e.tensor_copy(out=pad[64:128, c0 + 16:c0 + 32], in_=src[64:128, t_, :]), R=[src], W=[pad])
                        ps = self.psum()
                        k.op('pe', lambda e: e.transpose(out=ps[:, 0:128], in_=pad[:, :], identity=self.identf[:, :]), R=[pad, self.identf], W=[ps])
                        k.op('dve', lambda e: e.tensor_copy(out=LB[which][:, :], in_=ps[:, 0:128]), R=[ps], W=[LB[which]])
                    for which, sgn in ((0, 1.0), (1, -1.0)):
                        k.op('dve', lambda e: e.memset(LC[which][:, :], 0.0), W=[LC[which]])
                        k.op('dve', lambda e: e.tensor_single_scalar(out=LC[which][0:64, c0:c0 + 16], in_=sC[0:64, t_, which, :], scalar=sgn, op=ALU.mult),
                             R=[sC], W=[LC[which]])
                        k.op('dve', lambda e: e.tensor_single_scalar(out=LC[which][64:128, c0 + 16:c0 + 32], in_=sC[64:128, t_, which, :], scalar=sgn, op=ALU.mult),
                             R=[sC], W=[LC[which]])
                    k.op('dve', lambda e: e.tensor_scalar(out=ang[:, :], in0=iota0[:, :], scalar1=P["th"][:, t_:t_ + 1], scalar2=None, op0=ALU.mult),
                         R=[iota0, P["th"]], W=[ang])
                    self.sincos(scb, ang, ang[:, :], PL, SIN, SIN[:, :], COS, COS[:, :])
                    rcol = P["r"][:, t_:t_ + 1]
                    for qi in range(NP):
                        q = (NP - 1 - qi) if rev else qi
                        base = q * PL
                        for ch in range(PL // 512):
                            cs = slice(ch * 512, (ch + 1) * 512)
                            gs_ = slice(base + ch * 512, base + (ch + 1) * 512)
                            pr_ = self.psum(); pi_ = self.psum()
                            k.op('pe', lambda e: e.matmul(pr_[:, :], lhsT=LB[0][:, :], rhs=uT[:, gs_], start=True, stop=True), R=[LB[0], uT], W=[pr_])
                            k.op('pe', lambda e: e.matmul(pi_[:, :], lhsT=LB[1][:, :], rhs=uT[:, gs_], start=True, stop=True), R=[LB[1], uT], W=[pi_])
                            if rev:
                                Cc = COS[:, PL - 1 - ch * 512 - 511:PL - ch * 512][:, ::-1]
                                Sc = SIN[:, PL - 1 - ch * 512 - 511:PL - ch * 512][:, ::-1]
                            else:
                                Cc = COS[:, cs]; Sc = SIN[:, cs]
                            tt(tq[0][:, :], pr_[:, :], Cc, ALU.mult, [pr_, COS], [tq[0]])
                            tt(tq[1][:, :], pi_[:, :], Sc, ALU.mult, [pi_, SIN], [tq[1]])
                            tt(tq[2][:, :], pi_[:, :], Cc, ALU.mult, [pi_, COS], [tq[2]])
                            tt(tq[3][:, :], pr_[:, :], Sc, ALU.mult, [pr_, SIN], [tq[3]])
                            k.op('pool', lambda e: e.tensor_tensor(out=mre[:, cs], in0=tq[0][:, :], in1=tq[1][:, :], op=ALU.add), R=[tq[0], tq[1]], W=[mre])
                            k.op('pool', lambda e: e.tensor_tensor(out=mim[:, cs], in0=tq[2][:, :], in1=tq[3][:, :], op=ALU.subtract), R=[tq[2], tq[3]], W=[mim])
                        if qi == 0:
                            k.op('dve', lambda e: e.memset(ini2[:, :], 0.0), W=[ini2])
                        else:
                            cl = P["csL"][:, t_:t_ + 1]; sl = P["snL"][:, t_:t_ + 1]
                            k.op('dve', lambda e: e.tensor_scalar(out=itmp[:, 0:1], in0=ini[:, 1:2], scalar1=sl, scalar2=-1.0, op0=ALU.mult, op1=ALU.mult), R=[ini, P["snL"]], W=[itmp])
                            k.op('dve', lambda e: e.scalar_tensor_tensor(out=ini2[:, 0:1], in0=ini[:, 0:1], scalar=cl, in1=itmp[:, 0:1], op0=ALU.mult, op1=ALU.add),
                                 R=[ini, itmp, P["csL"]], W=[ini2])
                            k.op('dve', lambda e: e.tensor_scalar(out=itmp[:, 1:2], in0=ini[:, 0:1], scalar1=sl, scalar2=None, op0=ALU.mult), R=[ini, P["snL"]], W=[itmp])
                            k.op('dve', lambda e: e.scalar_tensor_tensor(out=ini2[:, 1:2], in0=ini[:, 1:2], scalar=cl, in1=itmp[:, 1:2], op0=ALU.mult, op1=ALU.add),
                                 R=[ini, itmp, P["csL"]], W=[ini2])
                        rb = rcol.to_broadcast([128, PL])
                        k.op('dve', lambda e: e.tensor_tensor_scan(out=R_(wre[:, :]), data0=rb, data1=R_(mre[:, :]), initial=ini2[:, 0:1], op0=ALU.mult, op1=ALU.add),
                             R=[mre, ini2, P["r"]], W=[wre])
                        k.op('dve', lambda e: e.tensor_tensor_scan(out=R_(wim[:, :]), data0=rb, data1=R_(mim[:, :]), initial=ini2[:, 1:2], op0=ALU.mult, op1=ALU.add),
                             R=[mim, ini2, P["r"]], W=[wim])
                        lastc = 0 if rev else PL - 1
                        k.op('dve', lambda e: e.tensor_copy(out=ini[:, 0:1], in_=wre[:, lastc:lastc + 1]), R=[wre], W=[ini])
                        k.op('dve', lambda e: e.tensor_copy(out=ini[:, 1:2], in_=wim[:, lastc:lastc + 1]), R=[wim], W=[ini])
                        Cf_ = R_(COS[:, :]); Sf_ = R_(SIN[:, :])
                        tt(tw[0][:, :], wre[:, :], Cf_, ALU.mult, [wre, COS], [tw[0]])
                        k.op('pool', lambda e: e.tensor_tensor(out=tw[1][:, :], in0=wim[:, :], in1=Sf_, op=ALU.mult), R=[wim, SIN], W=[tw[1]])
                        tt(xre[:, :], tw[0][:, :], tw[1][:, :], ALU.subtract, [tw[0], tw[1]], [xre])
                        tt(tw[0][:, :], wre[:, :], Sf_, ALU.mult, [wre, SIN], [tw[0]])
                        k.op('pool', lambda e: e.tensor_tensor(out=tw[1][:, :], in0=wim[:, :], in1=Cf_, op=ALU.mult), R=[wim, COS], W=[tw[1]])
                        tt(xim[:, :], tw[0][:, :], tw[1][:, :], ALU.add, [tw[0], tw[1]], [xim])
                        for ch in range(PL // 512):
                            cs = slice(ch * 512, (ch + 1) * 512)
                            gi = (base + ch * 512) // 512
                            gs_ = slice(base + ch * 512, base + (ch + 1) * 512)
                            py = self.psum()
                            k.op('pe', lambda e: e.matmul(py[:, :], lhsT=LC[0][:, :], rhs=xre[:, cs], start=True, stop=False), R=[LC[0], xre], W=[py])
                            k.op('pe', lambda e: e.matmul(py[:, :], lhsT=LC[1][:, :], rhs=xim[:, cs], start=False, stop=True), R=[LC[1], xim], W=[py])
                            if first_acc[gi]:
                                first_acc[gi] = False
                                k.op('act', lambda e: e.activation(out=acc[:, gs_], in_=py[:, :], func=AF.Identity), R=[py], W=[acc])
                            else:
                                tt(acc[:, gs_], acc[:, gs_], py[:, :], ALU.add, [acc, py], [acc])
            uf = [k.sb(es, f"s_uf{i}", [128, 2048]) for i in range(2)]
            ob = [k.sb(es, f"s_ob{i}", [128, 2048], BF16) for i in range(2)]
            for pc in range(4):
                cs = slice(pc * 2048, (pc + 1) * 2048)
                k.dma('sp', uf[pc % 2][:, :], self.PT[896:1024, cs], R=[self.PT], W=[uf[pc % 2]])
                k.op('dve', lambda e: e.scalar_tensor_tensor(out=ob[pc % 2][:, :], in0=uf[pc % 2][:, :], scalar=sk[:, 0:1], in1=acc[:, cs], op0=ALU.mult, op1=ALU.add),
                     R=[uf[pc % 2], sk, acc], W=[ob[pc % 2]])
                k.dma('sp', self.S5_sh[:, cs], ob[pc % 2][:, :], R=[ob[pc % 2]], W=[self.S5_sh])
            k.barrier()

    def phase3_hyena(self, l):
        k = self.k
        W = self.W[l]
        L = S
        NB = S // 128
        TW = 16256
        HX0 = k.dram(f"h_X0_{l}", [128, S]); HZ = k.dram(f"h_Z_{l}", [128, S]); GD = k.dram(f"h_GD_{l}", [128, 16384], BF16)
        with ExitStack() as es:
            zTr = k.sb(es, "h_zTr", [128, 128, NB], BF16)
            cw = k.sb(es, "h_cw", [128, 3, 4]); hsk = k.sb(es, "h_sk", [128, 1]); hdec = k.sb(es, "h_dec", [128, 2])
            k.dma('sp', cw[:, :, :], W["hcw"][:, :, :], R=[W["hcw"]], W=[cw])
            k.dma('sp', hsk[:, :], W["hsk"][:, :], R=[W["hsk"]], W=[hsk])
            k.dma('sp', hdec[:, :], W["hdec"][:, :], R=[W["hdec"]], W=[hdec])
            with ExitStack() as esA:
                xs = k.sb(esA, "h_xs", [128, S]); u = k.sb(esA, "h_u", [128, S]); z = k.sb(esA, "h_z", [128, S])
                for j in range(3):
                    k.dma('sp', xs[:, :], self.PT[512 + j * 128:512 + (j + 1) * 128, :], R=[self.PT], W=[xs])
                    dst = z if j == 1 else u
                    k.op('act', lambda e: e.activation(out=dst[:, :], in_=xs[:, :], func=AF.Identity, scale=cw[:, j, 1:2], bias=cw[:, j, 3:4]), R=[xs, cw], W=[dst])
                    k.op('dve', lambda e: e.scalar_tensor_tensor(out=dst[:, 1:S], in0=xs[:, 0:S - 1], scalar=cw[:, j, 0:1], in1=dst[:, 1:S], op0=ALU.mult, op1=ALU.add),
                         R=[xs, cw, dst], W=[dst])
                    k.op('dve', lambda e: e.scalar_tensor_tensor(out=dst[:, 0:S - 1], in0=xs[:, 1:S], scalar=cw[:, j, 2:3], in1=dst[:, 0:S - 1], op0=ALU.mult, op1=ALU.add),
                         R=[xs, cw, dst], W=[dst])
                    if j == 0:
                        k.dma('sp', HX0[:, :], u[:, :], R=[u], W=[HX0])
                    if j == 2:
                        k.op('dve', lambda e: e.tensor_tensor(out=z[:, :], in0=z[:, :], in1=u[:, :], op=ALU.mult), R=[z, u], W=[z])
                k.dma('sp', HZ[:, :], z[:, :], R=[z], W=[HZ])
                k.op('dve', lambda e: e.tensor_copy(out=u[:, :], in_=z[:, ::-1]), R=[z], W=[u])
                for ap_ in range(0, NB, 4):
                    ps = self.psum()
                    for j in range(4):
                        a_ = ap_ + j
                        k.op('pe', lambda e: e.transpose(out=ps[:, j * 128:(j + 1) * 128], in_=u[:, a_ * 128:(a_ + 1) * 128], identity=self.identf[:, :]),
                             R=[u, self.identf], W=[ps])
                    for j in range(4):
                        a = NB - 1 - (ap_ + j)
                        self.evac(zTr[:, :, a], ps[:, j * 128:(j + 1) * 128], R=[ps], W=[zTr])
                k.barrier()
            with ExitStack() as esB:
                F = [k.sb(esB, f"h_F{d}", [128, L]) for d in range(2)]
                w1 = k.sb(esB, "h_w1", [65, 64]); w2 = k.sb(esB, "h_w2", [64, 64]); w3 = k.sb(esB, "h_w3", [64, 2, 128])
                cols = {n: k.sb(esB, f"h_{n}", [64, 1]) for n in ["hb1", "hb2", "hf0", "hf1", "fb1", "fb2"]}
                k.dma('sp', w1[:, :], W["hw1"][:, :], R=[W["hw1"]], W=[w1])
                k.dma('sp', w2[:, :], W["hw2"][:, :], R=[W["hw2"]], W=[w2])
                k.dma('sp', w3[:, :, :], W["hw3"][:, :, :], R=[W["hw3"]], W=[w3])
                for n in ["hb1", "hb2", "hf0", "hf1"]:
                    k.dma('sp', cols[n][:, :], W[n][:, :], R=[W[n]], W=[cols[n]])
                k.op('dve', lambda e: e.tensor_tensor(out=cols["fb1"][:, :], in0=cols["hb1"][:, :], in1=cols["hf0"][:, :], op=ALU.mult), R=[cols["hb1"], cols["hf0"]], W=[cols["fb1"]])
                k.op('dve', lambda e: e.tensor_tensor(out=cols["fb2"][:, :], in0=cols["hb2"][:, :], in1=cols["hf1"][:, :], op=ALU.mult), R=[cols["hb2"], cols["hf1"]], W=[cols["fb2"]])
                pidx = k.sb(esB, "h_pidx", [128, 1]); om = k.sb(esB, "h_om", [128, 1]); ndec = k.sb(esB, "h_ndec", [128, 2])
                k.op('pool', lambda e: e.iota(pidx[:, :], pattern=[[0, 1]], base=0, channel_multiplier=1, allow_small_or_imprecise_dtypes=True), W=[pidx])
                dlt = (15.0 - 1e-4) / 15.0
                k.op('dve', lambda e: e.tensor_scalar(out=om[0:16, :], in0=pidx[0:16, :], scalar1=dlt, scalar2=1e-4, op0=ALU.mult, op1=ALU.add), R=[pidx], W=[om])
                k.op('dve', lambda e: e.tensor_scalar(out=om[32:48, :], in0=pidx[32:48, :], scalar1=dlt, scalar2=1e-4 - 32.0 * dlt, op0=ALU.mult, op1=ALU.add), R=[pidx], W=[om])
                k.op('dve', lambda e: e.tensor_single_scalar(out=om[0:48, :], in_=om[0:48, :], scalar=TWO_PI / L, op=ALU.mult), R=[om], W=[om])
                k.op('act', lambda e: e.activation(out=ndec[:, :], in_=hdec[:, :], func=AF.Abs), R=[hdec], W=[ndec])
                k.op('dve', lambda e: e.tensor_single_scalar(out=ndec[:, :], in_=ndec[:, :], scalar=-1.0 / (L - 1), op=ALU.mult), R=[ndec], W=[ndec])
                CH = 512
                iot = k.sb(esB, "h_iot", [128, CH]); feat = k.sb(esB, "h_feat", [65, CH]); ang = k.sb(esB, "h_ang", [48, CH])
                a1 = k.sb(esB, "h_a1", [64, CH]); h1 = k.sb(esB, "h_h1", [64, CH]); h2 = k.sb(esB, "h_h2", [64, CH]); win = k.sb(esB, "h_win", [128, CH])
                asum = k.sb(esB, "h_asum", [128, 2, L // CH]); tot = k.sb(esB, "h_tot", [128, 2])
                tmp = self.sctmp(esB, CH)
                k.op('dve', lambda e: e.memset(feat[:, :], 0.0), W=[feat])
                for ci in range(L // CH):
                    j0 = ci * CH
                    k.op('pool', lambda e: e.iota(iot[:, :], pattern=[[1, CH]], base=j0, channel_multiplier=0, allow_small_or_imprecise_dtypes=True), W=[iot])
                    k.op('dve', lambda e: e.tensor_scalar(out=ang[:, :], in0=iot[0:48, :], scalar1=om[0:48, 0:1], scalar2=None, op0=ALU.mult), R=[iot, om], W=[ang])
                    self.sinr(tmp, ang, ang[0:16, :], CH, feat, feat[0:16, :], 0.5 * math.pi, 0, 16)
                    self.sinr(tmp, ang, ang[32:48, :], CH, feat, feat[32:48, :], math.pi, 32, 48)
                    k.op('dve', lambda e: e.tensor_single_scalar(out=feat[64:65, :], in_=iot[64:65, :], scalar=1.0 / (L - 1), op=ALU.mult), R=[iot], W=[feat])
                    ps = self.psum()
                    k.op('pe', lambda e: e.matmul(ps[0:64, 0:CH], lhsT=w1[:, :], rhs=feat[:, :], start=True, stop=True), R=[w1, feat], W=[ps])
                    k.op('act', lambda e: e.activation(out=a1[:, :], in_=ps[0:64, 0:CH], func=AF.Identity, scale=cols["hf0"][:, 0:1], bias=cols["fb1"][:, 0:1]),
                         R=[ps, cols["hf0"], cols["fb1"]], W=[a1])
                    self.sinr(tmp, a1, a1[:, :], CH, h1, h1[:, :], 0.0, 0, 64)
                    ps = self.psum()
                    k.op('pe', lambda e: e.matmul(ps[0:64, 0:CH], lhsT=w2[:, :], rhs=h1[:, :], start=True, stop=True), R=[w2, h1], W=[ps])
                    k.op('act', lambda e: e.activation(out=a1[:, :], in_=ps[0:64, 0:CH], func=AF.Identity, scale=cols["hf1"][:, 0:1], bias=cols["fb2"][:, 0:1]),
                         R=[ps, cols["hf1"], cols["fb2"]], W=[a1])
                    self.sinr(tmp, a1, a1[:, :], CH, h2, h2[:, :], 0.0, 0, 64)
                    for d in range(2):
                        ps = self.psum()
                        k.op('pe', lambda e: e.matmul(ps[:, 0:CH], lhsT=w3[:, d, :], rhs=h2[:, :], start=True, stop=True), R=[w3, h2], W=[ps])
                        k.op('act', lambda e: e.activation(out=win[:, :], in_=iot[:, :], func=AF.Exp, scale=ndec[:, d:d + 1]), R=[iot, ndec], W=[win])
                        k.op('dve', lambda e: e.tensor_tensor(out=F[d][:, j0:j0 + CH], in0=ps[:, 0:CH], in1=win[:, :], op=ALU.mult), R=[ps, win], W=[F[d]])
                        k.op('dve', lambda e: e.tensor_reduce(out=asum[:, d, ci:ci + 1], in_=F[d][:, j0:j0 + CH], axis=AX.X, op=ALU.add, apply_absolute_value=True),
                             R=[F[d]], W=[asum])
                k.op('dve', lambda e: e.tensor_reduce(out=tot[:, :], in_=asum[:, :, :], axis=AX.X, op=ALU.add), R=[asum], W=[tot])
                k.op('dve', lambda e: e.tensor_single_scalar(out=tot[:, :], in_=tot[:, :], scalar=1e-6, op=ALU.add), R=[tot], W=[tot])
                k.op('dve', lambda e: e.reciprocal(out=tot[:, :], in_=tot[:, :]), R=[tot], W=[tot])
                gb16 = [k.sb(esB, f"h_gb{i}", [128, 2048], BF16) for i in range(2)]
                n_ = 0
                for pc in range(4):
                    g = gb16[n_ % 2]; n_ += 1
                    k.op('dve', lambda e: e.tensor_scalar(out=g[:, :], in0=F[0][:, pc * 2048:(pc + 1) * 2048], scalar1=tot[:, 0:1], scalar2=None, op0=ALU.mult),
                         R=[F[0], tot], W=[g])
                    k.dma('sp', GD[:, 8191 + pc * 2048:8191 + (pc + 1) * 2048], g[:, :], R=[g], W=[GD])
                for pc in range(4):
                    g = gb16[n_ % 2]; n_ += 1
                    i0 = pc * 2048
                    n = 2048 if pc < 3 else 2047
                    lo = 8191 - i0 - n + 1
                    src = F[1][:, lo:lo + n]
                    k.op('dve', lambda e: e.tensor_scalar(out=g[:, 0:n], in0=src[:, ::-1], scalar1=tot[:, 1:2], scalar2=None, op0=ALU.mult), R=[F[1], tot], W=[g])
                    k.dma('sp', GD[:, i0:i0 + n], g[:, 0:n], R=[g], W=[GD])
                k.barrier()
            with ExitStack() as esC:
                Tt = [k.sb(esC, f"h_Tt{i}", [128, TW], BF16) for i in range(2)]
                Yt = k.sb(esC, "h_Yt", [128, NB, 128])
                for c8 in range(16):
                    ps = self.psum()
                    for cj in range(8):
                        c = c8 * 8 + cj
                        tt_ = Tt[c % 2]
                        src = bass.AP(tensor=GD.t, offset=c * 16384, ap=[[1, 128], [1, TW]])
                        k.dma('sp', tt_[:, :], src, R=[GD], W=[tt_])
                        o0 = cj * 64
                        dl = [0] + [x for d_ in range(1, NB) for x in (d_, -d_)]
                        for n_i, dlt_ in enumerate(dl):
                            y0 = 128 * dlt_ + 8064
                            if dlt_ >= 0:
                                rhs = zTr[:, c, 0:NB - dlt_]; out = ps[:, o0 + dlt_:o0 + NB]
                            else:
                                rhs = zTr[:, c, -dlt_:NB]; out = ps[:, o0:o0 + NB + dlt_]
                            k.op('pe', lambda e: e.matmul(out, lhsT=tt_[:, y0:y0 + 128], rhs=rhs, start=(n_i == 0), stop=(n_i == len(dl) - 1), skip_group_check=True),
                                 R=[tt_, zTr], W=[ps])
                    self.evac(Yt[:, :, c8 * 8:(c8 + 1) * 8], ps[:, :].rearrange("p (c a) -> p a c", c=8), R=[ps], W=[Yt])
                zf = [k.sb(esC, f"h_zf{i}", [128, 512]) for i in range(2)]
                xf = [k.sb(esC, f"h_xf{i}", [128, 512]) for i in range(2)]
                ob = [k.sb(esC, f"h_ob{i}", [128, 512], BF16) for i in range(2)]
                for a4 in range(0, NB, 4):
                    i2 = (a4 // 4) % 2
                    cs = slice(a4 * 128, (a4 + 4) * 128)
                    k.dma('sp', zf[i2][:, :], HZ[:, cs], R=[HZ], W=[zf[i2]])
                    k.dma('sp', xf[i2][:, :], HX0[:, cs], R=[HX0], W=[xf[i2]])
                    ps = self.psum()
                    for j in range(4):
                        k.op('pe', lambda e: e.transpose(out=ps[:, j * 128:(j + 1) * 128], in_=Yt[:, a4 + j, :], identity=self.identf[:, :]), R=[Yt, self.identf], W=[ps])
                    k.op('dve', lambda e: e.scalar_tensor_tensor(out=zf[i2][:, :], in0=zf[i2][:, :], scalar=hsk[:, 0:1], in1=ps[:, :], op0=ALU.mult, op1=ALU.add),
                         R=[zf[i2], hsk, ps], W=[zf[i2]])
                    k.op('dve', lambda e: e.tensor_tensor(out=ob[i2][:, :], in0=zf[i2][:, :], in1=xf[i2][:, :], op=ALU.mult), R=[zf[i2], xf[i2]], W=[ob[i2]])
                    k.dma('sp', self.HY_sh[:, cs], ob[i2][:, :], R=[ob[i2]], W=[self.HY_sh])
                k.barrier()

    def layernorm(self, es, Y, grow, brow, out):
        k = self.k
        s6 = k.sb(es, "ln_s6", [128, 4, 6]); m2 = k.sb(es, "ln_m2", [128, 2])
        for c4 in range(4):
            k.op('dve', lambda e: e.bn_stats(out=s6[:, c4, :], in_=Y[:, c4 * 512:(c4 + 1) * 512]), R=[Y], W=[s6])
        k.op('dve', lambda e: e.bn_aggr(out=m2[:, :], in_=s6[:, :, :].rearrange("p a b -> p (a b)")), R=[s6], W=[m2])
        self.rstd(m2)
        k.op('dve', lambda e: e.tensor_scalar(out=Y[:, :], in0=Y[:, :], scalar1=m2[:, 0:1], scalar2=m2[:, 1:2], op0=ALU.subtract, op1=ALU.mult),
             R=[Y, m2], W=[Y])
        k.op('pool', lambda e: e.tensor_tensor(out=Y[:, :], in0=Y[:, :], in1=grow[:, :], op=ALU.mult), R=[Y, grow], W=[Y])
        k.op('dve', lambda e: e.tensor_tensor(out=out[:, :], in0=Y[:, :], in1=brow[:, :], op=ALU.add), R=[Y, brow], W=[out])

    def phase4_token(self, l):
        k = self.k
        W = self.W[l]; G = self.GW[l]
        with ExitStack() as es:
            lnt = [k.sb(es, f"t_ln{i}", [128, D]) for i in range(4)]
            for i in range(4):
                k.dma('sp', lnt[i][:, :], W["ln"][:, i * D:(i + 1) * D], R=[W["ln"]], W=[lnt[i]])
            xt = k.sb(es, "t_xt", [128, 16, 128], BF16)
            act3 = [k.sb(es, f"t_a{i}", [128, 8, 128], BF16) for i in range(3)]
            wgt = [k.sb(es, f"t_wg{i}", [128, 16, 512], BF16) for i in range(2)]
            w8 = [k.sb(es, f"t_w8{i}", [128, 8, 512], BF16) for i in range(2)]
            gs = k.sb(es, "t_gs", [128, 512]); tmp = k.sb(es, "t_tmp", [128, 512]); sg = k.sb(es, "t_sg", [128, 512])
            merged = k.sb(es, "t_merged", [128, D]); Y = k.sb(es, "t_Y", [128, D])
            mT = k.sb(es, "t_mT", [128, 16, 128], BF16)
            nw = [0, 0]

            def gate(i, b, cc):
                wt = wgt[nw[0] % 2]; nw[0] += 1
                c0 = b * D + cc * 512
                k.dma('sp', wt[:, :, :], G["wg"][:, c0:c0 + 512].rearrange("(a p) c -> p a c", p=128), R=[G["wg"]], W=[wt])
                ps = self.psum()
                for dk in range(16):
                    k.op('pe', lambda e: e.matmul(ps[:, :], lhsT=xt[:, dk, :], rhs=wt[:, dk, :], start=(dk == 0), stop=(dk == 15)), R=[xt, wt], W=[ps])
                k.op('act', lambda e: e.activation(out=gs[:, :], in_=ps[:, :], func=AF.Sigmoid), R=[ps], W=[gs])

            def proj8(a, wfull, c0):
                wt = w8[nw[1] % 2]; nw[1] += 1
                k.dma('sp', wt[:, :, :], wfull[:, c0:c0 + 512].rearrange("(a p) c -> p a c", p=128), R=[wfull], W=[wt])
                ps = self.psum()
                for dk in range(8):
                    k.op('pe', lambda e: e.matmul(ps[:, :], lhsT=a[:, dk, :], rhs=wt[:, dk, :], start=(dk == 0), stop=(dk == 7)), R=[a, wt], W=[ps])
                return ps

            for i in range(8):
                tk = slice(self.blk * TPC + i * 128, self.blk * TPC + (i + 1) * 128)
                k.dma('sp', xt[:, :, :], self.xT_sh[:, i * 128:(i + 1) * 128].rearrange("(a p) t -> p a t", p=128), R=[self.xT_sh], W=[xt])
                for j_ in range(3):
                    k.dma('sp', act3[j_][:, :, :], self.actin[j_][:, tk].rearrange("(a p) t -> p a t", p=128), R=[self.actin[j_]], W=[act3[j_]])
                for cc in range(4):
                    csl = slice(cc * 512, (cc + 1) * 512)
                    gate(i, 0, cc)
                    ps = proj8(act3[0], G["wmo"], cc * 512)
                    k.op('dve', lambda e: e.tensor_tensor(out=merged[:, csl], in0=ps[:, :], in1=gs[:, :], op=ALU.mult), R=[ps, gs], W=[merged])
                    gate(i, 1, cc)
                    ps = proj8(act3[1], G["who"], cc * 512)
                    k.op('dve', lambda e: e.tensor_tensor(out=tmp[:, :], in0=ps[:, :], in1=gs[:, :], op=ALU.mult), R=[ps, gs], W=[tmp])
                    k.op('pool', lambda e: e.tensor_tensor(out=merged[:, csl], in0=merged[:, csl], in1=tmp[:, :], op=ALU.add), R=[merged, tmp], W=[merged])
                    gate(i, 2, cc)
                    psg = proj8(act3[2], G["wsg"], D + cc * 512)
                    k.op('act', lambda e: e.activation(out=sg[:, :], in_=psg[:, :], func=AF.Sigmoid), R=[psg], W=[sg])
                    ps = proj8(act3[2], G["wsg"], cc * 512)
                    k.op('dve', lambda e: e.tensor_tensor(out=sg[:, :], in0=ps[:, :], in1=sg[:, :], op=ALU.mult), R=[ps, sg], W=[sg])
                    k.op('dve', lambda e: e.tensor_tensor(out=tmp[:, :], in0=sg[:, :], in1=gs[:, :], op=ALU.mult), R=[sg, gs], W=[tmp])
                    k.op('pool', lambda e: e.tensor_tensor(out=merged[:, csl], in0=merged[:, csl], in1=tmp[:, :], op=ALU.add), R=[merged, tmp], W=[merged])
                for g4 in range(4):
                    ps = self.psum()
                    for j in range(4):
                        dk = g4 * 4 + j
                        k.op('pe', lambda e: e.transpose(out=ps[:, j * 128:(j + 1) * 128], in_=merged[:, dk * 128:(dk + 1) * 128], identity=self.identf[:, :]),
                             R=[merged, self.identf], W=[ps])
                    self.evac(mT[:, g4 * 4:(g4 + 1) * 4, :], ps[:, :].rearrange("p (a b) -> p a b", a=4), R=[ps], W=[mT])
                for cc in range(4):
                    csl = slice(cc * 512, (cc + 1) * 512)
                    wt = wgt[nw[0] % 2]; nw[0] += 1
                    k.dma('sp', wt[:, :, :], G["wo"][:, csl].rearrange("(a p) c -> p a c", p=128), R=[G["wo"]], W=[wt])
                    ps = self.psum()
                    for dk in range(16):
                        k.op('pe', lambda e: e.matmul(ps[:, :], lhsT=mT[:, dk, :], rhs=wt[:, dk, :], start=(dk == 0), stop=(dk == 15)), R=[mT, wt], W=[ps])
                    k.op('dve', lambda e: e.scalar_tensor_tensor(out=Y[:, csl], in0=self.X[i][:, csl], scalar=ALPHA, in1=ps[:, :], op0=ALU.mult, op1=ALU.add),
                         R=[self.X[i], ps], W=[Y])
                with ExitStack() as es2:
                    self.layernorm(es2, Y, lnt[0], lnt[1], self.X[i])
            k.barrier()

    def phase5_peer(self, l):
        k = self.k
        W = self.W[l]; G = self.GW[l]
        NEG = -1e30
        C1 = math.sqrt(0.044715)
        C2 = 2.0 * math.sqrt(2.0 / math.pi)
        with ExitStack() as es:
            psk = k.sb(es, "p_sk", [128, 16, 128])
            k.dma('sp', psk[:, :, :], W["psk"][:, :, :], R=[W["psk"]], W=[psk])
            x1T = k.sb(es, "p_x1T", [128, 16, 512], BF16)
            skhi = k.sb(es, "p_skhi", [128, 16, 128], BF16); sklo = k.sb(es, "p_sklo", [128, 16, 128], BF16)
            qhi = k.sb(es, "p_qhi", [128, 512], BF16); qlo = k.sb(es, "p_qlo", [128, 512], BF16)
            k.op('dve', lambda e: e.tensor_copy(out=skhi[:, :, :], in_=psk[:, :, :]), R=[psk], W=[skhi])
            k.op('dve', lambda e: e.tensor_tensor(out=sklo[:, :, :], in0=psk[:, :, :], in1=skhi[:, :, :], op=ALU.subtract), R=[psk, skhi], W=[sklo])
            s1 = [k.sb(es, f"p_s1_{i}", [128, 8, 128]) for i in range(4)]
            s2 = [k.sb(es, f"p_s2_{i}", [128, 8, 128]) for i in range(4)]
            tau = [k.sb(es, f"p_tau{i}", [128, 8]) for i in range(4)]
            nkap = [k.sb(es, f"p_nkap{i}", [128, 8]) for i in range(4)]
            wqt = [k.sb(es, f"p_wq{i}", [128, 16, 128], BF16) for i in range(1)]
            qT = [k.sb(es, f"p_qT{i}", [128, 512]) for i in range(1)]
            uT = [k.sb(es, f"p_uT{i}", [128, 16, 512], BF16) for i in range(1)]
            Vt = k.sb(es, "p_V", [128, 4, D], BF16)
            HT = [k.sb(es, f"p_HT{i}", [128, 512], BF16) for i in range(4)]
            gh = [k.sb(es, f"p_gh{i}", [128, 4, 128], BF16) for i in range(8)]
            sm = [k.sb(es, f"p_sm{i}", [128, 4, 128]) for i in range(2)]
            ex = [k.sb(es, f"p_ex{i}", [128, 4, 128]) for i in range(2)]
            g1 = k.sb(es, "p_g1", [128, 512]); g2 = k.sb(es, "p_g2", [128, 512]); g3 = k.sb(es, "p_g3", [128, 512])
            t16 = [k.sb(es, f"p_t16_{i}", [128, 24]) for i in range(2)]
            wk = k.sb(es, "p_wk", [128, 128]); wk2 = k.sb(es, "p_wk2", [128, 128])
            cand = k.sb(es, "p_cand", [128, 24, 24]); cwk = k.sb(es, "p_cwk", [128, 576]); cwk2 = k.sb(es, "p_cwk2", [128, 576])
            b24 = k.sb(es, "p_b24", [128, 24]); e16 = k.sb(es, "p_e16", [128, 16]); st = k.sb(es, "p_st", [128, 4])
            nq = 0
            import os
            STAGE = int(os.environ.get("PEER_STAGE", "9")); NEG_ = int(os.environ.get("PEER_NEG", "32")); NHF = int(os.environ.get("PEER_NHF", "2"))
            for hf in range(NHF):
                Xh = self.X[hf * 4:(hf + 1) * 4]
                for ti in range(4):
                    for g4 in range(4):
                        ps = self.psum()
                        for j in range(4):
                            dk = g4 * 4 + j
                            k.op('pe', lambda e: e.transpose(out=ps[:, j * 128:(j + 1) * 128], in_=Xh[ti][:, dk * 128:(dk + 1) * 128], identity=self.identf[:, :]),
                                 R=[Xh[ti], self.identf], W=[ps])
                        self.evac(x1T[:, g4 * 4:(g4 + 1) * 4, ti * 128:(ti + 1) * 128], ps[:, :].rearrange("p (a b) -> p a b", a=4), R=[ps], W=[x1T])
                    k.op('pool', lambda e: e.tensor_single_scalar(out=Xh[ti][:, :], in_=Xh[ti][:, :], scalar=ALPHA, op=ALU.mult), R=[Xh[ti]], W=[Xh[ti]])
                for hc in range(16 if STAGE >= 1 else 0):
                    h, c = hc // 2, hc % 2
                    wt = wqt[0]; q_ = qT[0]; nq += 1
                    WQ = G[os.environ.get("PEER_WQ", "wq")]
                    k.dma('sp', wt[:, :, :], WQ[:, hc * 128:(hc + 1) * 128].rearrange("(a p) c -> p a c", p=128), R=[WQ], W=[wt])
                    ps = self.psum()
                    for dk in range(16):
                        k.op('pe', lambda e: e.matmul(ps[:, :], lhsT=wt[:, dk, :], rhs=x1T[:, dk, :], start=(dk == 0), stop=(dk == 15)), R=[wt, x1T], W=[ps])
                    self.evac(q_[:, :], ps[:, :], R=[ps], W=[q_])
                    if os.environ.get("PEER_NOSC"):
                        continue
                    k.op('dve', lambda e: e.tensor_copy(out=qhi[:, :], in_=q_[:, :]), R=[q_], W=[qhi])
                    k.op('dve', lambda e: e.tensor_tensor(out=qlo[:, :], in0=q_[:, :], in1=qhi[:, :], op=ALU.subtract), R=[q_, qhi], W=[qlo])
                    PX = int(os.environ.get("PEER_X", "0"))
                    if PX == 2:
                        continue
                    ps2 = self.psum()
                    for ti in range(4):
                        tsl = slice(ti * 128, (ti + 1) * 128)
                        k.op('pe', lambda e: e.matmul(ps2[:, tsl], lhsT=qhi[:, tsl], rhs=skhi[:, hc, :], start=True, stop=False), R=[qhi, skhi], W=[ps2])
                        k.op('pe', lambda e: e.matmul(ps2[:, tsl], lhsT=qhi[:, tsl], rhs=sklo[:, hc, :], start=False, stop=False), R=[qhi, sklo], W=[ps2])
                        k.op('pe', lambda e: e.matmul(ps2[:, tsl], lhsT=qlo[:, tsl], rhs=skhi[:, hc, :], start=False, stop=True), R=[qlo, skhi], W=[ps2])
                    for ti in range(4 if PX != 1 else 0):
                        dst = (s1 if c == 0 else s2)[ti]
                        k.op('dve', lambda e: e.tensor_copy(out=dst[:, h, :], in_=ps2[:, ti * 128:(ti + 1) * 128]), R=[ps2], W=[dst])
                for ti in range(4 if STAGE >= 2 else 0):
                    for h in range(8):
                        for c, sc in ((0, s1[ti]), (1, s2[ti])):
                            t_ = t16[c]
                            k.op('dve', lambda e: e.max(out=t_[:, 0:8], in_=sc[:, h, :]), R=[sc], W=[t_])
                            k.op('dve', lambda e: e.match_replace(out=wk[:, :], in_to_replace=t_[:, 0:8], in_values=sc[:, h, :], imm_value=NEG), R=[sc, t_], W=[wk])
                            k.op('dve', lambda e: e.max(out=t_[:, 8:16], in_=wk[:, :]), R=[wk], W=[t_])
                            k.op('dve', lambda e: e.match_replace(out=wk2[:, :], in_to_replace=t_[:, 8:16], in_values=wk[:, :], imm_value=NEG), R=[wk, t_], W=[wk2])
                            k.op('dve', lambda e: e.max(out=t_[:, 16:24], in_=wk2[:, :]), R=[wk2], W=[t_])
                        k.op('dve', lambda e: e.tensor_tensor(out=cand[:, :, :], in0=t16[0][:, :].unsqueeze(2).to_broadcast([128, 24, 24]),
                                                              in1=t16[1][:, :].unsqueeze(1).to_broadcast([128, 24, 24]), op=ALU.add), R=[t16[0], t16[1]], W=[cand])
                        cf = cand[:, :, :].rearrange("p a b -> p (a b)")
                        k.op('dve', lambda e: e.max(out=b24[:, 0:8], in_=cf), R=[cand], W=[b24])
                        k.op('dve', lambda e: e.match_replace(out=cwk[:, :], in_to_replace=b24[:, 0:8], in_values=cf, imm_value=NEG), R=[cand, b24], W=[cwk])
                        k.op('dve', lambda e: e.max(out=b24[:, 8:16], in_=cwk[:, :]), R=[cwk], W=[b24])
                        k.op('dve', lambda e: e.match_replace(out=cwk2[:, :], in_to_replace=b24[:, 8:16], in_values=cwk[:, :], imm_value=NEG), R=[cwk, b24], W=[cwk2])
                        k.op('dve', lambda e: e.max(out=b24[:, 16:24], in_=cwk2[:, :]), R=[cwk2], W=[b24])
                        k.op('dve', lambda e: e.tensor_tensor(out=tau[ti][:, h:h + 1], in0=b24[:, 15:16], in1=b24[:, 16:17], op=ALU.add), R=[b24], W=[tau[ti]])
                        k.op('dve', lambda e: e.tensor_single_scalar(out=st[:, 0:1], in_=b24[:, 0:1], scalar=-1.0, op=ALU.mult), R=[b24], W=[st])
                        k.op('act', lambda e: e.activation(out=e16[:, :], in_=b24[:, 0:16], func=AF.Exp, bias=st[:, 0:1]), R=[b24, st], W=[e16])
                        k.op('dve', lambda e: e.tensor_reduce(out=st[:, 1:2], in_=e16[:, :], axis=AX.X, op=ALU.add), R=[e16], W=[st])
                        k.op('act', lambda e: e.activation(out=st[:, 2:3], in_=st[:, 1:2], func=AF.Ln), R=[st], W=[st])
                        k.op('dve', lambda e: e.tensor_tensor(out=nkap[ti][:, h:h + 1], in0=st[:, 0:1], in1=st[:, 2:3], op=ALU.subtract), R=[st], W=[nkap[ti]])
                    k.op('dve', lambda e: e.tensor_single_scalar(out=tau[ti][:, :], in_=tau[ti][:, :], scalar=0.5, op=ALU.mult), R=[tau[ti]], W=[tau[ti]])
                for eg in range(NEG_ if STAGE >= 3 else 0):
                    u_ = uT[0]
                    r_, e0 = (eg * 512) // 2048, (eg * 512) % 2048
                    k.dma('sp', u_[:, :, :], G["pu"][r_ * D:(r_ + 1) * D, e0:e0 + 512].rearrange("(a p) e -> p a e", p=128), R=[G["pu"]], W=[u_])
                    k.dma('sp', Vt[:, :, :], G["pv"][eg * 512:(eg + 1) * 512, :].rearrange("(a p) d -> p a d", p=128), R=[G["pv"]], W=[Vt])
                    pg = [self.psum() for _ in range(4)]
                    for ti in range(4):
                        for h in range(8):
                            sm_, ex_, g_ = sm[h % 2], ex[h % 2], gh[h]
                            k.op('pool', lambda e: e.tensor_tensor(out=sm_[:, :, :], in0=s1[ti][:, h, eg * 4:(eg + 1) * 4].unsqueeze(2).to_broadcast([128, 4, 128]),
                                                                   in1=s2[ti][:, h, :].unsqueeze(1).to_broadcast([128, 4, 128]), op=ALU.add),
                                 R=[s1[ti], s2[ti]], W=[sm_])
                            k.op('act', lambda e: e.activation(out=ex_[:, :, :], in_=sm_[:, :, :], func=AF.Exp, bias=nkap[ti][:, h:h + 1]), R=[sm_, nkap[ti]], W=[ex_])
                            k.op('dve', lambda e: e.scalar_tensor_tensor(out=g_[:, :, :], in0=sm_[:, :, :], scalar=tau[ti][:, h:h + 1], in1=ex_[:, :, :],
                                                                         op0=ALU.is_gt, op1=ALU.mult), R=[sm_, ex_, tau[ti]], W=[g_])
                        for il in range(4):
                            for h in range(8):
                                k.op('pe', lambda e: e.matmul(pg[il][:, ti * 128:(ti + 1) * 128], lhsT=gh[h][:, il, :], rhs=self.identb[:, :], start=(h == 0), stop=(h == 7)),
                                     R=[gh[h], self.identb], W=[pg[il]])
                    for il in range(4):
                        pa = self.psum()
                        for dk in range(16):
                            k.op('pe', lambda e: e.matmul(pa[:, :], lhsT=u_[:, dk, il * 128:(il + 1) * 128], rhs=x1T[:, dk, :], start=(dk == 0), stop=(dk == 15)),
                                 R=[u_, x1T], W=[pa])
                        k.op('act', lambda e: e.activation(out=g1[:, :], in_=pa[:, :], func=AF.Square, scale=C1), R=[pa], W=[g1])
                        k.op('dve', lambda e: e.scalar_tensor_tensor(out=g2[:, :], in0=g1[:, :], scalar=1.0, in1=pa[:, :], op0=ALU.add, op1=ALU.mult), R=[g1, pa], W=[g2])
                        k.op('act', lambda e: e.activation(out=g3[:, :], in_=g2[:, :], func=AF.Sigmoid, scale=C2), R=[g2], W=[g3])
                        k.op('dve', lambda e: e.tensor_tensor(out=g2[:, :], in0=g3[:, :], in1=pa[:, :], op=ALU.mult), R=[g3, pa], W=[g2])
                        k.op('dve', lambda e: e.tensor_tensor(out=HT[il][:, :], in0=g2[:, :], in1=pg[il][:, :], op=ALU.mult), R=[g2, pg[il]], W=[HT[il]])
                    for ti in range(4):
                        for dc in range(4):
                            po = self.psum()
                            for il in range(4):
                                k.op('pe', lambda e: e.matmul(po[:, :], lhsT=HT[il][:, ti * 128:(ti + 1) * 128], rhs=Vt[:, il, dc * 512:(dc + 1) * 512], start=(il == 0), stop=(il == 3)),
                                     R=[HT[il], Vt], W=[po])
                            k.op('dve', lambda e: e.tensor_tensor(out=Xh[ti][:, dc * 512:(dc + 1) * 512], in0=Xh[ti][:, dc * 512:(dc + 1) * 512], in1=po[:, :], op=ALU.add),
                                 R=[Xh[ti], po], W=[Xh[ti]])
            k.barrier()
        with ExitStack() as es:
            lnt = [k.sb(es, f"p_ln{i}", [128, D]) for i in range(2)]
            Y = k.sb(es, "p_Y", [128, D])
            for i in range(2):
                k.dma('sp', lnt[i][:, :], W["ln"][:, (2 + i) * D:(3 + i) * D], R=[W["ln"]], W=[lnt[i]])
            for ti in range(8):
                k.op('act', lambda e: e.activation(out=Y[:, :], in_=self.X[ti][:, :], func=AF.Identity), R=[self.X[ti]], W=[Y])
                with ExitStack() as es2:
                    self.layernorm(es2, Y, lnt[0], lnt[1], self.X[ti])
            k.barrier()

    def psum(self):
        p = self.PS[self.psn % 8]
        self.psn += 1
        return p

    def evac(self, out, in_, R, W):
        self.pq += 1
        import os
        fe = os.environ.get("EVAC_ENG")
        if (self.pq % 2 and fe != "dve") or fe == "act":
            self.k.op('act', lambda e: e.activation(out=out, in_=in_, func=AF.Identity), R=R, W=W)
        else:
            self.k.op('dve', lambda e: e.tensor_copy(out=out, in_=in_), R=R, W=W)

    def phase2_inproj(self, l):
        k = self.k
        W = self.W[l]
        with ExitStack() as es:
            WF = k.sb(es, "WF", [128, 16, NWF], BF16)
            WT = k.sb(es, "WT", [128, 16, NWT], BF16)
            XR = [k.sb(es, f"XR{i}", [128, 16, TPC], BF16) for i in range(2)]
            SF = [k.sb(es, f"SF{i}", [128, 1024]) for i in range(2)]
            ST = [k.sb(es, f"ST{i}", [128, NWT]) for i in range(2)]
            k.dma('pool', WF[:, :, :], W["wf"][:, :].rearrange("(a p) c -> p a c", p=128), R=[W["wf"]], W=[WF])
            k.dma('pool', WT[:, :, :], W["wt"][:, :].rearrange("(a p) c -> p a c", p=128), R=[W["wt"]], W=[WT])
            nsf = nst = 0
            for r in range(NCORES):
                xr = XR[r % 2]
                k.dma('pool', xr[:, :, :], self.xT_ext[:, r * TPC:(r + 1) * TPC].rearrange("(a p) t -> p a t", p=128), R=[self.xT_ext], W=[xr])
                for g in range(9):
                    c0 = g * 128
                    m = 128 if g < 8 else 4
                    sf = SF[nsf % 2]
                    nsf += 1
                    for hf in range(2):
                        ps = self.psum()
                        for dk in range(16):
                            k.op('pe', lambda e: e.matmul(ps[0:m, :], lhsT=WF[:, dk, c0:c0 + m], rhs=xr[:, dk, hf * 512:(hf + 1) * 512],
                                                          start=(dk == 0), stop=(dk == 15)), R=[WF, xr], W=[ps])
                        self.evac(sf[0:m, hf * 512:(hf + 1) * 512], ps[0:m, :], R=[ps], W=[sf])
                    k.dma('sp', self.PT[c0:c0 + m, r * TPC:(r + 1) * TPC], sf[0:m, :], R=[sf], W=[self.PT])
                for tt in range(8):
                    st = ST[nst % 2]
                    nst += 1
                    for (n0, n1) in ((0, 512), (512, 768)):
                        ps = self.psum()
                        for dk in range(16):
                            k.op('pe', lambda e: e.matmul(ps[:, 0:n1 - n0], lhsT=xr[:, dk, tt * 128:(tt + 1) * 128], rhs=WT[:, dk, n0:n1],
                                                          start=(dk == 0), stop=(dk == 15)), R=[WT, xr], W=[ps])
                        self.evac(st[:, n0:n1], ps[:, 0:n1 - n0], R=[ps], W=[st])
                    t0 = r * TPC + tt * 128
                    k.dma('sp', self.KV[t0:t0 + 128, :], st[:, :], R=[st], W=[self.KV])
            k.barrier()


    def phase3_mlstm(self, l):
        k = self.k
        W = self.W[l]
        NCH = S // 128
        with ExitStack() as es:
            kT = k.sb(es, "m_kT", [128, 2, S], BF16)
            qT = k.sb(es, "m_qT", [128, 2, S], BF16)
            gb = k.sb(es, "m_gb", [128, 4]); ngb = k.sb(es, "m_ngb", [128, 4])
            mgain = k.sb(es, "m_gain", [128, 256])
            trif = k.sb(es, "m_trif", [128, 128]); trib = k.sb(es, "m_trib", [128, 128])
            G4 = k.sb(es, "m_G4", [64, 4, 128]); l1s = k.sb(es, "m_l1s", [64, 128]); bks = k.sb(es, "m_bks", [64, 128])
            ones64 = k.sb(es, "m_ones64", [64, 128])
            bkcol = [k.sb(es, f"m_bkcol{d}", [128, NCH]) for d in range(2)]
            dec = [k.sb(es, f"m_dec{d}", [128, NCH]) for d in range(2)]
            msk = k.sb(es, "m_msk", [128, 2048])
            lfp = [k.sb(es, f"m_lfp{i}", [128, 2048]) for i in range(2)]
            qst = [k.sb(es, f"m_qst{i}", [128, 2048]) for i in range(2)]
            vst = [k.sb(es, f"m_vst{i}", [128, 512]) for i in range(3)]
            ktb = [k.sb(es, f"m_ktb{i}", [128, 256], BF16) for i in range(3)]
            vtl = [k.sb(es, f"m_vtl{i}", [128, 257], BF16) for i in range(3)]
            sm = [k.sb(es, f"m_sm{i}", [128, 128], BF16) for i in range(2)]
            Cf = k.sb(es, "m_Cf", [128, 2, 257]); Cb = k.sb(es, "m_Cb", [128, 2, 257], BF16); Ct = k.sb(es, "m_Ct", [128, 2, 257])
            rr = [k.sb(es, f"m_rr{i}", [128, 1]) for i in range(2)]
            hst = [k.sb(es, f"m_hst{i}", [128, 256]) for i in range(2)]
            HD = [k.dram(f"m_HD{l}_{d}", [S, 256]) for d in range(2)]
            k.dma('sp', gb[:, :], W["gb"][:, :], R=[W["gb"]], W=[gb])
            k.dma('sp', mgain[:, :], W["mg"][:, :], R=[W["mg"]], W=[mgain])
            k.op('dve', lambda e: e.tensor_single_scalar(out=ngb[:, :], in_=gb[:, :], scalar=-1.0, op=ALU.mult), R=[gb], W=[ngb])
            k.op('pool', lambda e: e.affine_select(out=trif[:, :], in_=self.onesf[:, :], pattern=[[1, 128]], compare_op=ALU.is_ge, fill=0.0,
                                                   base=0, channel_multiplier=-1), R=[self.onesf], W=[trif])
            k.op('pool', lambda e: e.affine_select(out=trib[:, :], in_=self.onesf[:, :], pattern=[[-1, 128]], compare_op=ALU.is_ge, fill=0.0,
                                                   base=0, channel_multiplier=1), R=[self.onesf], W=[trib])
            k.op('pool', lambda e: e.iota(msk[:, :], pattern=[[0, 16], [1, 128]], base=0, channel_multiplier=0,
                                          allow_small_or_imprecise_dtypes=True), W=[msk])
            k.op('dve', lambda e: e.tensor_single_scalar(out=msk[:, :], in_=msk[:, :], scalar=0.0, op=ALU.is_gt), R=[msk], W=[msk])
            k.op('dve', lambda e: e.memset(ones64[:, :], 1.0), W=[ones64])
            for dk in range(2):
                k.dma('pool', kT[:, dk, :], self.PT[256 + dk * 128:256 + (dk + 1) * 128, :], R=[self.PT], W=[kT])
            k.dma('sp', G4[:, :, :], self.PT[1024:1028, :].rearrange("g (j s) -> j g s", s=128), R=[self.PT], W=[G4])
            for d in range(2):
                rev = (d == 1)
                R_ = (lambda ap: ap[:, ::-1]) if rev else (lambda ap: ap)
                k.op('act', lambda e: e.activation(out=l1s[:, :], in_=G4[:, 2 * d + 1, :], func=AF.Exp, scale=-1.0, bias=ngb[0:64, 2 * d + 1:2 * d + 2]),
                     R=[G4, ngb], W=[l1s])
                k.op('act', lambda e: e.activation(out=l1s[:, :], in_=l1s[:, :], func=AF.Ln, bias=self.onesf[0:64, 0:1]), R=[l1s, self.onesf], W=[l1s])
                k.op('dve', lambda e: e.tensor_tensor_scan(out=R_(bks[:, :]), data0=ones64[:, :], data1=R_(l1s[:, :]), initial=0.0,
                                                           op0=ALU.mult, op1=ALU.add), R=[ones64, l1s], W=[bks])
                k.op('dve', lambda e: e.tensor_tensor(out=bks[:, :], in0=bks[:, :], in1=G4[:, 2 * d, :], op=ALU.add), R=[bks, G4], W=[bks])
                k.op('act', lambda e: e.activation(out=bks[:, :], in_=bks[:, :], func=AF.Exp, bias=gb[0:64, 2 * d:2 * d + 1]), R=[bks, gb], W=[bks])
                ps = self.psum()
                k.op('pe', lambda e: e.transpose(out=ps[:, 0:64], in_=bks[:, :], identity=self.identf[0:64, 0:64]), R=[bks, self.identf], W=[ps])
                k.op('dve', lambda e: e.tensor_single_scalar(out=bkcol[d][:, :], in_=ps[:, 0:64], scalar=0.0625, op=ALU.mult), R=[ps], W=[bkcol[d]])
                for pc in range(4):
                    t0 = pc * 2048
                    lf = lfp[pc % 2]
                    k.dma('sp', lf[:, :], self.PT[1024 + 2 * d + 1:1024 + 2 * d + 2, t0:t0 + 2048].partition_broadcast(128), R=[self.PT], W=[lf])
                    k.op('act', lambda e: e.activation(out=lf[:, :], in_=lf[:, :], func=AF.Exp, scale=-1.0, bias=ngb[:, 2 * d + 1:2 * d + 2]),
                         R=[lf, ngb], W=[lf])
                    k.op('act', lambda e: e.activation(out=lf[:, :], in_=lf[:, :], func=AF.Ln, bias=self.onesf[:, 0:1]), R=[lf, self.onesf], W=[lf])
                    k.op('dve', lambda e: e.tensor_tensor_scan(out=R_(lf[:, :]), data0=msk[:, :], data1=R_(lf[:, :]), initial=0.0,
                                                               op0=ALU.mult, op1=ALU.add), R=[msk, lf], W=[lf])
                    k.op('act', lambda e: e.activation(out=lf[:, :], in_=lf[:, :], func=AF.Exp, scale=-1.0), R=[lf], W=[lf])
                    o0 = 0 if rev else 127
                    k.op('dve', lambda e: e.tensor_copy(out=dec[d][:, pc * 16:(pc + 1) * 16], in_=lf[:, o0::128]), R=[lf], W=[dec[d]])
                    for dk in range(2):
                        qs = qst[dk]
                        k.dma('sp', qs[:, :], self.PT[dk * 128:(dk + 1) * 128, t0:t0 + 2048], R=[self.PT], W=[qs])
                        k.op('pool' if dk else 'dve', lambda e: e.tensor_tensor(out=qT[:, dk, t0:t0 + 2048], in0=qs[:, :], in1=lf[:, :], op=ALU.mult),
                             R=[qs, lf], W=[qT])
                k.op('dve', lambda e: e.memset(Cf[:, :, :], 0.0), W=[Cf])
                k.op('dve', lambda e: e.memset(Cb[:, :, :], 0.0), W=[Cb])
                tri = trib if rev else trif
                for ji in range(NCH):
                    j = (NCH - 1 - ji) if rev else ji
                    t0 = j * 128
                    vs, kb, vt, smm, r1, hs = vst[ji % 3], ktb[ji % 3], vtl[ji % 3], sm[ji % 2], rr[ji % 2], hst[ji % 2]
                    k.dma('sp', vs[:, :], self.KV[t0:t0 + 128, 0:512], R=[self.KV], W=[vs])
                    k.op('act', lambda e: e.activation(out=kb[:, :], in_=vs[:, 256:512], func=AF.Identity), R=[vs], W=[kb])
                    k.op('dve', lambda e: e.tensor_scalar(out=vt[:, 0:256], in0=vs[:, 0:256], scalar1=bkcol[d][:, j:j + 1], scalar2=None, op0=ALU.mult),
                         R=[vs, bkcol[d]], W=[vt])
                    k.op('act', lambda e: e.activation(out=vt[:, 256:257], in_=bkcol[d][:, j:j + 1], func=AF.Identity), R=[bkcol[d]], W=[vt])
                    ps_s = self.psum()
                    for dk in range(2):
                        k.op('pe', lambda e: e.matmul(ps_s[:, 0:128], lhsT=kT[:, dk, t0:t0 + 128], rhs=qT[:, dk, t0:t0 + 128], start=(dk == 0), stop=(dk == 1)),
                             R=[kT, qT], W=[ps_s])
                    k.op('dve', lambda e: e.tensor_tensor(out=smm[:, :], in0=ps_s[:, 0:128], in1=tri[:, :], op=ALU.mult), R=[ps_s, tri], W=[smm])
                    ps_n = self.psum()
                    k.op('pe', lambda e: e.matmul(ps_n[:, 0:257], lhsT=smm[:, :], rhs=vt[:, :], start=True, stop=False), R=[smm, vt], W=[ps_n])
                    for dk in range(2):
                        k.op('pe', lambda e: e.matmul(ps_n[:, 0:257], lhsT=qT[:, dk, t0:t0 + 128], rhs=Cb[:, dk, :], start=False, stop=(dk == 1)),
                             R=[qT, Cb], W=[ps_n])
                    for dk in range(2):
                        ps_u = self.psum()
                        k.op('pe', lambda e: e.matmul(ps_u[:, 0:257], lhsT=kb[:, dk * 128:(dk + 1) * 128], rhs=vt[:, :], start=True, stop=True),
                             R=[kb, vt], W=[ps_u])
                        k.op('dve', lambda e: e.tensor_tensor(out=Ct[:, dk, :], in0=ps_u[:, 0:257], in1=Cf[:, dk, :], op=ALU.add), R=[ps_u, Cf], W=[Ct])
                    k.op('act', lambda e: e.activation(out=Cf[:, :, :], in_=Ct[:, :, :], func=AF.Copy, scale=dec[d][:, j:j + 1]), R=[Ct, dec[d]], W=[Cf])
                    k.op('act', lambda e: e.activation(out=Cb[:, :, :], in_=Ct[:, :, :], func=AF.Copy, scale=dec[d][:, j:j + 1]), R=[Ct, dec[d]], W=[Cb])
                    k.op('act', lambda e: e.activation(out=r1[:, :], in_=ps_n[:, 256:257], func=AF.Abs), R=[ps_n], W=[r1])
                    k.op('dve', lambda e: e.tensor_single_scalar(out=r1[:, :], in_=r1[:, :], scalar=1.0, op=ALU.max), R=[r1], W=[r1])
                    k.op('dve', lambda e: e.reciprocal(out=r1[:, :], in_=r1[:, :]), R=[r1], W=[r1])
                    k.op('act', lambda e: e.activation(out=hs[:, :], in_=ps_n[:, 0:256], func=AF.Copy, scale=r1[:, 0:1]), R=[ps_n, r1], W=[hs])
                    k.dma('sp', HD[d][t0:t0 + 128, :], hs[:, :], R=[hs], W=[HD[d]])
            hA = [k.sb(es, f"m_hA{i}", [128, 256]) for i in range(2)]
            hB = [k.sb(es, f"m_hB{i}", [128, 256]) for i in range(2)]
            oo = [k.sb(es, f"m_oo{i}", [128, 256]) for i in range(2)]
            st6 = [k.sb(es, f"m_st6{i}", [128, 6]) for i in range(2)]
            mv = [k.sb(es, f"m_mv{i}", [128, 2]) for i in range(2)]
            hT = [k.sb(es, f"m_hT{i}", [128, 2, 128], BF16) for i in range(2)]
            for j in range(NCH):
                t0 = j * 128
                a, b, o, s6, m2, ht = hA[j % 2], hB[j % 2], oo[j % 2], st6[j % 2], mv[j % 2], hT[j % 2]
                k.dma('sp', a[:, :], HD[0][t0:t0 + 128, :], R=[HD[0]], W=[a])
                k.dma('sp', b[:, :], HD[1][t0:t0 + 128, :], R=[HD[1]], W=[b])
                k.dma('sp', o[:, :], self.KV[t0:t0 + 128, 512:768], R=[self.KV], W=[o])
                k.op('act', lambda e: e.activation(out=o[:, :], in_=o[:, :], func=AF.Sigmoid), R=[o], W=[o])
                k.op('dve', lambda e: e.tensor_tensor(out=a[:, :], in0=a[:, :], in1=b[:, :], op=ALU.add), R=[a, b], W=[a])
                k.op('dve', lambda e: e.tensor_tensor(out=a[:, :], in0=a[:, :], in1=o[:, :], op=ALU.mult), R=[a, o], W=[a])
                k.op('dve', lambda e: e.bn_stats(out=s6[:, :], in_=a[:, :]), R=[a], W=[s6])
                k.op('dve', lambda e: e.bn_aggr(out=m2[:, :], in_=s6[:, :]), R=[s6], W=[m2])
                self.rstd(m2)
                k.op('dve', lambda e: e.tensor_scalar(out=a[:, :], in0=a[:, :], scalar1=m2[:, 0:1], scalar2=m2[:, 1:2], op0=ALU.subtract, op1=ALU.mult),
                     R=[a, m2], W=[a])
                k.op('dve', lambda e: e.tensor_tensor(out=a[:, :], in0=a[:, :], in1=mgain[:, :], op=ALU.mult), R=[a, mgain], W=[a])
                ps = self.psum()
                for dk in range(2):
                    k.op('pe', lambda e: e.transpose(out=ps[:, dk * 128:(dk + 1) * 128], in_=a[:, dk * 128:(dk + 1) * 128], identity=self.identf[:, :]),
                         R=[a, self.identf], W=[ps])
                self.evac(ht[:, :, :], ps[:, 0:256].rearrange("p (a b) -> p a b", a=2), R=[ps], W=[ht])
                k.dma('sp', self.HN_sh[:, t0:t0 + 128].rearrange("(a p) t -> p a t", p=128), ht[:, :, :], R=[ht], W=[self.HN_sh])
            k.barrier()


_PROGS = {}


def _prog(mode):
    if mode not in _PROGS:
        p = Prog(mode)
        _PROGS[mode] = p.build()
    return _PROGS[mode]


def kernel(**inputs):
    inp = {k_: np.asarray(v) for k_, v in inputs.items()}
    x = np.ascontiguousarray(inp["x"][0]).astype(np.float32)
    cores = list(range(NCORES))
    for l in range(DEPTH):
        xT = np.ascontiguousarray(x.T)
        resA = run_bass_kernel_spmd(_prog('A'), [_inputs_A(inp, l, c, xT) for c in cores], core_ids=cores).results
        hn = np.concatenate([resA[c]["HN_sh"] for c in range(4)], axis=0)
        hy = np.concatenate([resA[c]["HY_sh"] for c in cores], axis=0)
        s5 = np.concatenate([resA[c]["S5_sh"] for c in cores], axis=0)
        del resA
        shared = _shared_B(inp, l)
        in_maps = []
        bcores = list(range(NCB))
        tpb = NBLK * TPC
        for c in bcores:
            ts = slice(c * tpb, (c + 1) * tpb)
            m = dict(shared)
            m["x_tok"] = np.ascontiguousarray(x[ts])
            m["xTs"] = np.ascontiguousarray(xT[:, ts])
            m["hn"] = np.ascontiguousarray(hn[:, ts]); m["hy"] = np.ascontiguousarray(hy[:, ts]); m["s5"] = np.ascontiguousarray(s5[:, ts])
            in_maps.append(m)
        resB = run_bass_kernel_spmd(_prog('B'), in_maps, core_ids=bcores).results
        x = np.concatenate([resB[c]["out"] for c in bcores], axis=0)
        del resB, in_maps, shared
    return np.ascontiguousarray(x[None]).astype(np.float32)
```

```python
import math
from contextlib import ExitStack
import numpy as np
import concourse.bass as bass
import concourse.mybir as mybir
from concourse.bass_utils import run_bass_kernel_spmd

F32 = mybir.dt.float32
BF16 = mybir.dt.bfloat16
U32 = mybir.dt.uint32
AF = mybir.ActivationFunctionType
ALU = mybir.AluOpType
AX = mybir.AxisListType

NCORES = 8
D = 2048
S = 8192
DEPTH = 4
TPC = S // NCORES
NCB = 8
NBLK = S // TPC // NCB
BW = 1024
NWF = 1028
NWT = 768
ALPHA = (2 * DEPTH) ** 0.25
LN_EPS = 1e-5
TWO_PI = 2.0 * math.pi


class T:
    def __init__(self, t, name):
        self.t = t
        self.name = name
        self.w = None
        self.r = {}

    def __getitem__(self, idx):
        return self.t[idx]


class K:
    NDS = 24
    EPOCH = 30000

    def __init__(self, nc, es):
        self.nc = nc
        self.es = es
        self.eng = {'pe': nc.tensor, 'dve': nc.vector, 'act': nc.scalar, 'pool': nc.gpsimd, 'sp': nc.sync}
        self.nsem = 0
        self.sem = {k: self._newsem() for k in self.eng}
        self.semid = {k: (k, 0) for k in self.eng}
        self.cnt = {k: 0 for k in self.eng}
        self.waited = {k: {} for k in self.eng}
        self.dsem = [self._newsem() for _ in range(self.NDS)]
        self.dcnt = [0] * self.NDS
        self.dnext = 0
        self.last = {}
        self.ninst = 0

    def _newsem(self):
        self.nsem += 1
        return self.es.enter_context(self.nc.semaphore(f"s{self.nsem}"))

    def _wait(self, e, ev):
        if ev is None:
            return
        sem, val, key = ev
        if key == ('pe', self.semid['pe'][1]) and e == 'pe':
            return
        if self.waited[e].get(key, 0) >= val:
            return
        self.eng[e].wait_ge(sem, val)
        self.waited[e][key] = val

    def deps(self, e, R, W):
        for b in R:
            self._wait(e, b.w)
        for b in W:
            self._wait(e, b.w)
            for ev in b.r.values():
                self._wait(e, ev)

    def commit(self, ev, R, W):
        self.last[ev[2]] = ev
        for b in R:
            b.r[ev[2]] = ev
        for b in W:
            b.w = ev
            b.r = {}

    def op(self, e, fn, R=(), W=()):
        self.deps(e, R, W)
        if self.cnt[e] >= self.EPOCH:
            self.sem[e] = self._newsem()
            self.semid[e] = (e, self.semid[e][1] + 1)
            self.cnt[e] = 0
        ins = fn(self.eng[e])
        self.cnt[e] += 1
        ins.then_inc(self.sem[e], 1)
        ev = (self.sem[e], self.cnt[e], self.semid[e])
        self.commit(ev, R, W)
        self.ninst += 1
        return ev

    def dma(self, q, out, in_, R=(), W=(), **kw):
        i = self.dnext
        self.dnext = (i + 1) % self.NDS
        if self.dcnt[i] > 0:
            self._wait(q, (self.dsem[i], 16 * self.dcnt[i], ('d', i)))
        self.deps(q, R, W)
        self.eng[q].dma_start(out=out, in_=in_, **kw).then_inc(self.dsem[i], 16)
        self.dcnt[i] += 1
        ev = (self.dsem[i], 16 * self.dcnt[i], ('d', i))
        self.commit(ev, R, W)
        self.ninst += 1
        return ev

    def allgather(self, out, in_, R=(), W=()):
        e = 'pool'
        self.deps(e, R, W)
        sem = self._newsem()
        self.nc.gpsimd.collective_compute("AllGather", ALU.bypass, replica_groups=[list(range(NCORES))],
                                          ins=[in_], outs=[out]).then_inc(sem, 1)
        ev = (sem, 1, ('cc', self.nsem))
        self.commit(ev, R, W)
        return ev

    def barrier(self):
        for e in self.eng:
            for ev in list(self.last.values()):
                self._wait(e, ev)

    def sb(self, es, name, shape, dt=F32):
        self.nsem += 1
        name = f"{name}_{self.nsem}"
        return T(es.enter_context(self.nc.sbuf_tensor(name, list(shape), dt)), name)

    def ps(self, es, name, shape, dt=F32):
        return T(es.enter_context(self.nc.psum_tensor(name, list(shape), dt)), name)

    def dram(self, name, shape, dt=F32, kind="Internal"):
        return T(self.nc.dram_tensor(name, list(shape), dt, kind=kind), name)


def _inputs_A(inp, l, c, xT):
    h = c % 4
    m = {}
    m["xT"] = xT
    w = inp["w_in"][l]
    q0 = h * 256
    colsF = np.concatenate([
        np.arange(q0, q0 + 256), 1024 + np.arange(q0, q0 + 256),
        4112 + 0 * 1024 + c * 128 + np.arange(128), 4112 + 1 * 1024 + c * 128 + np.arange(128),
        4112 + 2 * 1024 + c * 128 + np.arange(128), 7184 + c * 128 + np.arange(128),
        4096 + np.array([0 * 8 + 0 * 4 + h, 0 * 8 + 1 * 4 + h, 1 * 8 + 0 * 4 + h, 1 * 8 + 1 * 4 + h]),
    ])
    colsT = np.concatenate([2048 + np.arange(q0, q0 + 256), 1024 + np.arange(q0, q0 + 256), 3072 + np.arange(q0, q0 + 256)])
    m["wf"] = np.ascontiguousarray(w[:, colsF])
    m["wt"] = np.ascontiguousarray(w[:, colsT])
    gb = inp["mlstm_gate_bias"][l][:, :, h].reshape(4)
    m["gb"] = np.ascontiguousarray(np.broadcast_to(gb[None, :], (128, 4))).astype(np.float32)
    m["mg"] = np.ascontiguousarray(np.broadcast_to(inp["mlstm_norm_gain"][l][None, q0:q0 + 256], (128, 256))).astype(np.float32)
    p3 = np.zeros((128, 8, 3), np.float32); sB = np.zeros((128, 8, 2, 16), np.float32); sC = np.zeros((128, 8, 2, 16), np.float32)
    for d_ in range(2):
        for pr in range(4):
            for g2 in range(2):
                g = 8 * c + 2 * pr + g2
                rs = slice(g2 * 64, (g2 + 1) * 64)
                t_ = d_ * 4 + pr
                p3[rs, t_, 0] = inp["s5_lambda_re"][l][d_, g]
                p3[rs, t_, 1] = inp["s5_lambda_im"][l][d_, g]
                p3[rs, t_, 2] = inp["s5_log_step"][l][d_, g]
                sB[rs, t_, 0] = inp["s5_b_re"][l][d_, g]
                sB[rs, t_, 1] = inp["s5_b_im"][l][d_, g]
                sC[rs, t_, 0] = inp["s5_c_re"][l][d_, g].T
                sC[rs, t_, 1] = inp["s5_c_im"][l][d_, g].T
    m["s5p"] = p3; m["s5B"] = sB; m["s5C"] = sC
    m["s5k"] = np.ascontiguousarray(inp["s5_skip"][l][c * 128:(c + 1) * 128].reshape(128, 1))
    w1 = inp["hyena_w1"][l]
    w1p = np.zeros((65, 64), np.float32); w1p[0:16] = w1[1:17]; w1p[32:48] = w1[17:33]; w1p[64] = w1[0]
    m["hw1"] = w1p
    col = lambda a: np.ascontiguousarray(a.reshape(-1, 1)).astype(np.float32)
    m["hb1"] = col(inp["hyena_b1"][l]); m["hb2"] = col(inp["hyena_b2"][l])
    m["hf0"] = col(inp["hyena_freq"][l][0]); m["hf1"] = col(inp["hyena_freq"][l][1])
    m["hw2"] = np.ascontiguousarray(inp["hyena_w2"][l])
    w3 = inp["hyena_w3"][l].reshape(64, 2, 1024)
    m["hw3"] = np.ascontiguousarray(w3[:, :, c * 128:(c + 1) * 128])
    m["hdec"] = np.ascontiguousarray(inp["hyena_decay"][l][:, c * 128:(c + 1) * 128].T)
    m["hsk"] = col(inp["hyena_skip"][l][c * 128:(c + 1) * 128])
    cw = np.zeros((128, 3, 4), np.float32)
    for j_ in range(3):
        sl_ = slice(j_ * 1024 + c * 128, j_ * 1024 + (c + 1) * 128)
        cw[:, j_, 0:3] = inp["hyena_conv_w"][l][:, sl_].T
        cw[:, j_, 3] = inp["hyena_conv_b"][l][sl_]
    m["hcw"] = cw
    return m


def _shared_B(inp, l):
    m = {}
    m["wg"] = np.ascontiguousarray(inp["w_in"][l][:, 8208:])
    m["wmo"] = np.ascontiguousarray(inp["w_mlstm_out"][l]); m["who"] = np.ascontiguousarray(inp["w_hyena_out"][l])
    m["wsg"] = np.ascontiguousarray(inp["w_s5_glu"][l]); m["wo"] = np.ascontiguousarray(inp["w_out"][l])
    m["wq"] = np.ascontiguousarray(inp["peer_w_q"][l])
    m["pu"] = np.ascontiguousarray(inp["peer_u"][l].reshape(8, 2048, 2048).transpose(0, 2, 1).reshape(16384, 2048))
    m["pv"] = np.ascontiguousarray(inp["peer_v"][l])
    m["psk"] = np.ascontiguousarray(inp["peer_subkeys"][l].reshape(16, 128, 128).transpose(2, 0, 1))
    lnrow = np.concatenate([inp["ln1_g"][l], inp["ln1_b"][l], inp["ln2_g"][l], inp["ln2_b"][l]])
    m["ln"] = np.ascontiguousarray(np.broadcast_to(lnrow[None, :], (128, 4 * D))).astype(np.float32)
    return m


class Prog:
    def __init__(self, mode, debug=False, phases="mshtp"):
        self.mode = mode
        self.phases = phases
        self.debug = debug
        self.nc = bass.Bass("TRN2", target_bir_lowering=False)
        self.es = ExitStack()

    def ext(self, name, shape, dt=F32):
        return self.k.dram(name, shape, dt, kind="ExternalInput")

    def build(self):
        nc, es = self.nc, self.es
        with es:
            k = self.k = K(nc, es)
            self.pq = 0
            self.identf = k.sb(es, "identf", [128, 128])
            self.identb = k.sb(es, "identb", [128, 128], BF16)
            self.onesf = k.sb(es, "onesf", [128, 128])
            self.PS = [k.ps(es, f"PS{i}", [128, 512]) for i in range(8)]
            self.psn = 0
            k.op('dve', lambda e: e.memset(self.onesf[:, :], 1.0), W=[self.onesf])
            self.epst = k.sb(es, "epst", [128, 1])
            k.op('dve', lambda e: e.memset(self.epst[:, :], LN_EPS), W=[self.epst])
            k.op('pool', lambda e: e.affine_select(out=self.identf[:, :], in_=self.onesf[:, :], pattern=[[-1, 128]], compare_op=ALU.is_equal,
                                                   fill=0.0, base=0, channel_multiplier=1), R=[self.onesf], W=[self.identf])
            k.op('dve', lambda e: e.tensor_copy(out=self.identb[:, :], in_=self.identf[:, :]), R=[self.identf], W=[self.identb])
            if self.mode == 'A':
                self.build_A()
            else:
                self.build_B()
        return nc

    def build_A(self):
        k, es = self.k, self.es
        d = {}
        for nm, shp in [("wf", [D, NWF]), ("wt", [D, NWT]), ("gb", [128, 4]), ("mg", [128, 256]),
                        ("hw1", [65, 64]), ("hb1", [64, 1]), ("hb2", [64, 1]), ("hf0", [64, 1]), ("hf1", [64, 1]), ("hw2", [64, 64]),
                        ("hw3", [64, 2, 128]), ("hdec", [128, 2]), ("hsk", [128, 1]), ("hcw", [128, 3, 4]),
                        ("s5p", [128, 8, 3]), ("s5B", [128, 8, 2, 16]), ("s5C", [128, 8, 2, 16]), ("s5k", [128, 1])]:
            d[nm] = self.ext(nm, shp)
        self.W = [d]
        self.xT_ext = self.ext("xT", [D, S])
        self.PT = k.dram("PT", [NWF, S], F32)
        self.KV = k.dram("KV", [S, NWT], F32)
        self.HN_sh = k.dram("HN_sh", [256, S], BF16, kind="ExternalOutput")
        self.S5_sh = k.dram("S5_sh", [128, S], BF16, kind="ExternalOutput")
        self.HY_sh = k.dram("HY_sh", [128, S], BF16, kind="ExternalOutput")
        self.phase2_inproj(0)
        if 'm' in self.phases:
            self.phase3_mlstm(0)
        if 's' in self.phases:
            self.phase3_s5(0)
        if 'h' in self.phases:
            self.phase3_hyena(0)
        for t in (self.HN_sh, self.S5_sh, self.HY_sh):
            k._wait('sp', t.w)

    def build_B(self):
        k, es = self.k, self.es
        d = {}
        import os
        notab = bool(os.environ.get("B_NOTAB"))
        for nm, shp in [("wg", [D, 6144]), ("wmo", [BW, D]), ("who", [BW, D]), ("wsg", [BW, 2 * D]), ("wo", [D, D]), ("wq", [D, D]),
                        ("pu", [16384, 2048]), ("pv", [16384, D]), ("psk", [128, 16, 128]), ("ln", [128, 4 * D])]:
            if notab and nm in ("pu", "pv"):
                continue
            if 't' not in self.phases and nm in ("wg", "wmo", "who", "wsg", "wo"):
                continue
            d[nm] = self.ext(nm, shp)
        self.W = [d]
        import os as _os
        nblk = int(_os.environ.get("B_NBLK", str(NBLK)))
        self.x_tok = self.ext("x_tok", [nblk * TPC, D])
        xTs = self.ext("xTs", [D, nblk * TPC])
        self.actin = [self.ext(nm, [BW, nblk * TPC], BF16) for nm in ("hn", "hy", "s5")] if 't' in self.phases else []
        self.out = k.dram("out", [nblk * TPC, D], F32, kind="ExternalOutput")
        self.xT_sh = k.dram("xT_sh", [D, TPC], BF16)
        if self.debug:
            self.dbgX1 = k.dram("dbgX1", [nblk * TPC, D], F32, kind="ExternalOutput")
        self.X = [k.sb(es, f"X{i}", [128, D]) for i in range(8)]
        g = {}
        for nm, rows, cols in [("wg", D, 6144), ("wmo", BW, D), ("who", BW, D), ("wsg", BW, 2 * D), ("wo", D, D), ("wq", D, D),
                               ("pu", 16384, 2048), ("pv", 16384, D)]:
            if nm not in d:
                continue
            full = k.dram(f"{nm}F", [rows, cols], BF16)
            nch = 8 if rows >= 8192 else 1
            rr = rows // nch
            for i in range(nch):
                k.dma('pool', full[i * rr:(i + 1) * rr, :], d[nm][i * rr:(i + 1) * rr, :], R=[d[nm]], W=[full])
            g[nm] = full
        self.GW = [g]
        for blk in range(nblk):
            self.blk = blk
            b0 = blk * TPC
            for i in range(8):
                k.dma('sp', self.X[i][:, :], self.x_tok[b0 + i * 128:b0 + (i + 1) * 128, :], R=[self.x_tok], W=[self.X[i]])
            k.dma('pool', self.xT_sh[:, :], xTs[:, b0:b0 + TPC], R=[xTs], W=[self.xT_sh])
            if 't' in self.phases:
                self.phase4_token(0)
            if self.debug:
                for i in range(8):
                    k.dma('sp', self.dbgX1[b0 + i * 128:b0 + (i + 1) * 128, :], self.X[i][:, :], R=[self.X[i]], W=[self.dbgX1])
            if 'p' in self.phases:
                self.phase5_peer(0)
            for i in range(8):
                k.dma('sp', self.out[b0 + i * 128:b0 + (i + 1) * 128, :], self.X[i][:, :], R=[self.X[i]], W=[self.out])
        k._wait('sp', self.out.w)
        if self.debug:
            k._wait('sp', self.dbgX1.w)

    def rstd(self, m2):
        k = self.k
        k.op('act', lambda e: e.activation(out=m2[:, 1:2], in_=m2[:, 1:2], func=AF.Ln, bias=self.epst[:, 0:1]), R=[m2, self.epst], W=[m2])
        k.op('act', lambda e: e.activation(out=m2[:, 1:2], in_=m2[:, 1:2], func=AF.Exp, scale=-0.5), R=[m2], W=[m2])

    def sctmp(self, es, n):
        k = self.k
        return (k.sb(es, "sc_ki", [128, n], mybir.dt.int32), k.sb(es, "sc_kf", [128, n]), k.sb(es, "sc_y", [128, n]))

    def sinr(self, tmp, srcT, src, n, outT, out, shift, p0=0, p1=128):
        k = self.k
        ki_, kf_, y_ = tmp
        ki, kf, y = ki_.t[p0:p1, 0:n], kf_.t[p0:p1, 0:n], y_.t[p0:p1, 0:n]
        k.op('dve', lambda e: e.tensor_scalar(out=y, in0=src, scalar1=shift, scalar2=None, op0=ALU.add), R=[srcT], W=[y_])
        k.op('dve', lambda e: e.tensor_single_scalar(out=ki, in_=y, scalar=1.0 / TWO_PI, op=ALU.mult), R=[y_], W=[ki_])
        k.op('dve', lambda e: e.tensor_copy(out=kf, in_=ki), R=[ki_], W=[kf_])
        k.op('dve', lambda e: e.scalar_tensor_tensor(out=y, in0=kf, scalar=-TWO_PI, in1=y, op0=ALU.mult, op1=ALU.add), R=[kf_, y_], W=[y_])
        k.op('dve', lambda e: e.tensor_scalar(out=y, in0=y, scalar1=-3.1415925, scalar2=3.1415925, op0=ALU.max, op1=ALU.min), R=[y_], W=[y_])
        k.op('act', lambda e: e.activation(out=out, in_=y, func=AF.Sin), R=[y_], W=[outT])

    def sincos(self, tmp, angT, ang, n, sinT, sin_out, cosT, cos_out):
        k = self.k
        ki_, kf_, y_ = tmp

        class V:
            def __init__(s_, t):
                s_.t = t

            def __getitem__(s_, idx):
                return s_.t.t[:, 0:n]
        ki, kf, y = V(ki_), V(kf_), V(y_)
        for shift, oT, out in ((0.0, sinT, sin_out), (0.5 * math.pi, cosT, cos_out)):
            k.op('dve', lambda e: e.tensor_scalar(out=y[:, :], in0=ang, scalar1=shift, scalar2=None, op0=ALU.add), R=[angT], W=[y_])
            k.op('dve', lambda e: e.tensor_single_scalar(out=ki[:, :], in_=y[:, :], scalar=1.0 / TWO_PI, op=ALU.mult), R=[y_], W=[ki_])
            k.op('dve', lambda e: e.tensor_copy(out=kf[:, :], in_=ki[:, :]), R=[ki_], W=[kf_])
            k.op('dve', lambda e: e.scalar_tensor_tensor(out=y[:, :], in0=kf[:, :], scalar=-TWO_PI, in1=y[:, :], op0=ALU.mult, op1=ALU.add), R=[kf_, y_], W=[y_])
            k.op('dve', lambda e: e.tensor_scalar(out=y[:, :], in0=y[:, :], scalar1=-3.1415925, scalar2=3.1415925, op0=ALU.max, op1=ALU.min), R=[y_], W=[y_])
            k.op('act', lambda e: e.activation(out=out, in_=y[:, :], func=AF.Sin), R=[y_], W=[oT])

    def phase3_s5(self, l):
        k = self.k
        W = self.W[l]
        PL = 1024
        NP = S // PL
        TT = lambda *a, **kw: None
        with ExitStack() as es:
            p3 = k.sb(es, "s_p3", [128, 8, 3]); sB = k.sb(es, "s_B", [128, 8, 2, 16]); sC = k.sb(es, "s_C", [128, 8, 2, 16]); sk = k.sb(es, "s_k", [128, 1])
            k.dma('sp', p3[:, :, :], W["s5p"][:, :, :], R=[W["s5p"]], W=[p3])
            k.dma('sp', sB[:, :, :, :], W["s5B"][:, :, :, :], R=[W["s5B"]], W=[sB])
            k.dma('sp', sC[:, :, :, :], W["s5C"][:, :, :, :], R=[W["s5C"]], W=[sC])
            k.dma('sp', sk[:, :], W["s5k"][:, :], R=[W["s5k"]], W=[sk])
            uT = k.sb(es, "s_uT", [128, S], BF16)
            k.dma('pool', uT[:, :], self.PT[896:1024, :], R=[self.PT], W=[uT])
            acc = k.sb(es, "s_acc", [128, S])
            P = {n: k.sb(es, f"s_{n}", [128, 8]) for n in ["step", "th", "rho", "r", "sn", "cs", "are", "aim", "nre", "nim", "den", "fre", "fim",
                                                            "t1", "t2", "thL", "snL", "csL", "nth"]}
            lre, lim, lst = p3[:, :, 0], p3[:, :, 1], p3[:, :, 2]

            def tt(o, a, b, op, R, Wt):
                k.op('dve', lambda e: e.tensor_tensor(out=o, in0=a, in1=b, op=op), R=R, W=Wt)
            k.op('act', lambda e: e.activation(out=P["step"][:, :], in_=lst, func=AF.Exp), R=[p3], W=[P["step"]])
            tt(P["th"][:, :], lim, P["step"][:, :], ALU.mult, [p3, P["step"]], [P["th"]])
            tt(P["rho"][:, :], lre, P["step"][:, :], ALU.mult, [p3, P["step"]], [P["rho"]])
            k.op('act', lambda e: e.activation(out=P["r"][:, :], in_=P["rho"][:, :], func=AF.Exp), R=[P["rho"]], W=[P["r"]])
            scs = self.sctmp(es, PL)
            self.sincos(scs, P["th"], P["th"][:, :], 8, P["sn"], P["sn"][:, :], P["cs"], P["cs"][:, :])
            k.op('dve', lambda e: e.tensor_single_scalar(out=P["thL"][:, :], in_=P["th"][:, :], scalar=float(PL), op=ALU.mult), R=[P["th"]], W=[P["thL"]])
            self.sincos(scs, P["thL"], P["thL"][:, :], 8, P["snL"], P["snL"][:, :], P["csL"], P["csL"][:, :])
            tt(P["are"][:, :], P["r"][:, :], P["cs"][:, :], ALU.mult, [P["r"], P["cs"]], [P["are"]])
            tt(P["aim"][:, :], P["r"][:, :], P["sn"][:, :], ALU.mult, [P["r"], P["sn"]], [P["aim"]])
            k.op('dve', lambda e: e.tensor_single_scalar(out=P["t1"][:, :], in_=P["are"][:, :], scalar=-1.0, op=ALU.add), R=[P["are"]], W=[P["t1"]])
            tt(P["nre"][:, :], P["t1"][:, :], lre, ALU.mult, [P["t1"], p3], [P["nre"]])
            tt(P["t2"][:, :], P["aim"][:, :], lim, ALU.mult, [P["aim"], p3], [P["t2"]])
            tt(P["nre"][:, :], P["nre"][:, :], P["t2"][:, :], ALU.add, [P["nre"], P["t2"]], [P["nre"]])
            tt(P["nim"][:, :], P["aim"][:, :], lre, ALU.mult, [P["aim"], p3], [P["nim"]])
            tt(P["t2"][:, :], P["t1"][:, :], lim, ALU.mult, [P["t1"], p3], [P["t2"]])
            tt(P["nim"][:, :], P["nim"][:, :], P["t2"][:, :], ALU.subtract, [P["nim"], P["t2"]], [P["nim"]])
            tt(P["den"][:, :], lre, lre, ALU.mult, [p3], [P["den"]])
            tt(P["t2"][:, :], lim, lim, ALU.mult, [p3], [P["t2"]])
            tt(P["den"][:, :], P["den"][:, :], P["t2"][:, :], ALU.add, [P["den"], P["t2"]], [P["den"]])
            k.op('dve', lambda e: e.reciprocal(out=P["den"][:, :], in_=P["den"][:, :]), R=[P["den"]], W=[P["den"]])
            tt(P["fre"][:, :], P["nre"][:, :], P["den"][:, :], ALU.mult, [P["nre"], P["den"]], [P["fre"]])
            tt(P["fim"][:, :], P["nim"][:, :], P["den"][:, :], ALU.mult, [P["nim"], P["den"]], [P["fim"]])
            k.op('dve', lambda e: e.tensor_single_scalar(out=P["nth"][:, :], in_=P["th"][:, :], scalar=-1.0, op=ALU.mult), R=[P["th"]], W=[P["nth"]])
            bbr = k.sb(es, "s_bbr", [128, 8, 16]); bbi = k.sb(es, "s_bbi", [128, 8, 16]); tb = k.sb(es, "s_tb", [128, 8, 16])
            fre_b = P["fre"][:, :].unsqueeze(2).to_broadcast([128, 8, 16]); fim_b = P["fim"][:, :].unsqueeze(2).to_broadcast([128, 8, 16])
            tt(bbr[:, :, :], sB[:, :, 0, :], fre_b, ALU.mult, [sB, P["fre"]], [bbr])
            tt(tb[:, :, :], sB[:, :, 1, :], fim_b, ALU.mult, [sB, P["fim"]], [tb])
            tt(bbr[:, :, :], bbr[:, :, :], tb[:, :, :], ALU.subtract, [bbr, tb], [bbr])
            tt(bbi[:, :, :], sB[:, :, 1, :], fre_b, ALU.mult, [sB, P["fre"]], [bbi])
            tt(tb[:, :, :], sB[:, :, 0, :], fim_b, ALU.mult, [sB, P["fim"]], [tb])
            tt(bbi[:, :, :], bbi[:, :, :], tb[:, :, :], ALU.add, [bbi, tb], [bbi])
            iota0 = k.sb(es, "s_iota", [128, PL]); ang = k.sb(es, "s_ang", [128, PL]); COS = k.sb(es, "s_cos", [128, PL]); SIN = k.sb(es, "s_sin", [128, PL])
            k.op('pool', lambda e: e.iota(iota0[:, :], pattern=[[1, PL]], base=0, channel_multiplier=0, allow_small_or_imprecise_dtypes=True), W=[iota0])
            pad = k.sb(es, "s_pad", [128, 128])
            LB = [k.sb(es, f"s_LB{i}", [128, 128], BF16) for i in range(2)]
            LC = [k.sb(es, f"s_LC{i}", [128, 128], BF16) for i in range(2)]
            mre = k.sb(es, "s_mre", [128, PL]); mim = k.sb(es, "s_mim", [128, PL])
            wre = k.sb(es, "s_wre", [128, PL]); wim = k.sb(es, "s_wim", [128, PL])
            xre = k.sb(es, "s_xre", [128, PL], BF16); xim = k.sb(es, "s_xim", [128, PL], BF16)
            tq = [k.sb(es, f"s_tq{i}", [128, 512]) for i in range(4)]
            tw = [k.sb(es, f"s_tw{i}", [128, PL]) for i in range(2)]
            ini = k.sb(es, "s_ini", [128, 2]); ini2 = k.sb(es, "s_ini2", [128, 2]); itmp = k.sb(es, "s_itmp", [128, 2])
            first_acc = [True] * (S // 512)
            scb = scs
            for d in range(2):
                rev = (d == 1)
                R_ = (lambda ap: ap[:, ::-1]) if rev else (lambda ap: ap)
                for pr in range(4):
                    t_ = d * 4 + pr
                    c0 = 32 * pr
                    for which, src in ((0, bbr), (1, bbi)):
                        k.op('dve', lambda e: e.memset(pad[:, :], 0.0), W=[pad])
                        k.op('dve', lambda e: e.tensor_copy(out=pad[0:64, c0:c0 + 16], in_=src[0:64, t_, :]), R=[src], W=[pad])
                        k.op('dve', lambda e: e.tensor_copy(out=pad[64:128, c0 + 16:c0 + 32], in_=src[64:128, t_, :]), R=[src], W=[pad])
                        ps = self.psum()
                        k.op('pe', lambda e: e.transpose(out=ps[:, 0:128], in_=pad[:, :], identity=self.identf[:, :]), R=[pad, self.identf], W=[ps])
                        k.op('dve', lambda e: e.tensor_copy(out=LB[which][:, :], in_=ps[:, 0:128]), R=[ps], W=[LB[which]])
                    for which, sgn in ((0, 1.0), (1, -1.0)):
                        k.op('dve', lambda e: e.memset(LC[which][:, :], 0.0), W=[LC[which]])
                        k.op('dve', lambda e: e.tensor_single_scalar(out=LC[which][0:64, c0:c0 + 16], in_=sC[0:64, t_, which, :], scalar=sgn, op=ALU.mult),
                             R=[sC], W=[LC[which]])
                        k.op('dve', lambda e: e.tensor_single_scalar(out=LC[which][64:128, c0 + 16:c0 + 32], in_=sC[64:128, t_, which, :], scalar=sgn, op=ALU.mult),
                             R=[sC], W=[LC[which]])
                    k.op('dve', lambda e: e.tensor_scalar(out=ang[:, :], in0=iota0[:, :], scalar1=P["th"][:, t_:t_ + 1], scalar2=None, op0=ALU.mult),
                         R=[iota0, P["th"]], W=[ang])
                    self.sincos(scb, ang, ang[:, :], PL, SIN, SIN[:, :], COS, COS[:, :])
                    rcol = P["r"][:, t_:t_ + 1]
                    for qi in range(NP):
                        q = (NP - 1 - qi) if rev else qi
                        base = q * PL
                        for ch in range(PL // 512):
                            cs = slice(ch * 512, (ch + 1) * 512)
                            gs_ = slice(base + ch * 512, base + (ch + 1) * 512)
                            pr_ = self.psum(); pi_ = self.psum()
                            k.op('pe', lambda e: e.matmul(pr_[:, :], lhsT=LB[0][:, :], rhs=uT[:, gs_], start=True, stop=True), R=[LB[0], uT], W=[pr_])
                            k.op('pe', lambda e: e.matmul(pi_[:, :], lhsT=LB[1][:, :], rhs=uT[:, gs_], start=True, stop=True), R=[LB[1], uT], W=[pi_])
                            if rev:
                                Cc = COS[:, PL - 1 - ch * 512 - 511:PL - ch * 512][:, ::-1]
                                Sc = SIN[:, PL - 1 - ch * 512 - 511:PL - ch * 512][:, ::-1]
                            else:
                                Cc = COS[:, cs]; Sc = SIN[:, cs]
                            tt(tq[0][:, :], pr_[:, :], Cc, ALU.mult, [pr_, COS], [tq[0]])
                            tt(tq[1][:, :], pi_[:, :], Sc, ALU.mult, [pi_, SIN], [tq[1]])
                            tt(tq[2][:, :], pi_[:, :], Cc, ALU.mult, [pi_, COS], [tq[2]])
                            tt(tq[3][:, :], pr_[:, :], Sc, ALU.mult, [pr_, SIN], [tq[3]])
                            k.op('pool', lambda e: e.tensor_tensor(out=mre[:, cs], in0=tq[0][:, :], in1=tq[1][:, :], op=ALU.add), R=[tq[0], tq[1]], W=[mre])
                            k.op('pool', lambda e: e.tensor_tensor(out=mim[:, cs], in0=tq[2][:, :], in1=tq[3][:, :], op=ALU.subtract), R=[tq[2], tq[3]], W=[mim])
                        if qi == 0:
                            k.op('dve', lambda e: e.memset(ini2[:, :], 0.0), W=[ini2])
                        else:
                            cl = P["csL"][:, t_:t_ + 1]; sl = P["snL"][:, t_:t_ + 1]
                            k.op('dve', lambda e: e.tensor_scalar(out=itmp[:, 0:1], in0=ini[:, 1:2], scalar1=sl, scalar2=-1.0, op0=ALU.mult, op1=ALU.mult), R=[ini, P["snL"]], W=[itmp])
                            k.op('dve', lambda e: e.scalar_tensor_tensor(out=ini2[:, 0:1], in0=ini[:, 0:1], scalar=cl, in1=itmp[:, 0:1], op0=ALU.mult, op1=ALU.add),
                                 R=[ini, itmp, P["csL"]], W=[ini2])
                            k.op('dve', lambda e: e.tensor_scalar(out=itmp[:, 1:2], in0=ini[:, 0:1], scalar1=sl, scalar2=None, op0=ALU.mult), R=[ini, P["snL"]], W=[itmp])
                            k.op('dve', lambda e: e.scalar_tensor_tensor(out=ini2[:, 1:2], in0=ini[:, 1:2], scalar=cl, in1=itmp[:, 1:2], op0=ALU.mult, op1=ALU.add),
                                 R=[ini, itmp, P["csL"]], W=[ini2])
                        rb = rcol.to_broadcast([128, PL])
                        k.op('dve', lambda e: e.tensor_tensor_scan(out=R_(wre[:, :]), data0=rb, data1=R_(mre[:, :]), initial=ini2[:, 0:1], op0=ALU.mult, op1=ALU.add),
                             R=[mre, ini2, P["r"]], W=[wre])
                        k.op('dve', lambda e: e.tensor_tensor_scan(out=R_(wim[:, :]), data0=rb, data1=R_(mim[:, :]), initial=ini2[:, 1:2], op0=ALU.mult, op1=ALU.add),
                             R=[mim, ini2, P["r"]], W=[wim])
                        lastc = 0 if rev else PL - 1
                        k.op('dve', lambda e: e.tensor_copy(out=ini[:, 0:1], in_=wre[:, lastc:lastc + 1]), R=[wre], W=[ini])
                        k.op('dve', lambda e: e.tensor_copy(out=ini[:, 1:2], in_=wim[:, lastc:lastc + 1]), R=[wim], W=[ini])
                        Cf_ = R_(COS[:, :]); Sf_ = R_(SIN[:, :])
                        tt(tw[0][:, :], wre[:, :], Cf_, ALU.mult, [wre, COS], [tw[0]])
                        k.op('pool', lambda e: e.tensor_tensor(out=tw[1][:, :], in0=wim[:, :], in1=Sf_, op=ALU.mult), R=[wim, SIN], W=[tw[1]])
                        tt(xre[:, :], tw[0][:, :], tw[1][:, :], ALU.subtract, [tw[0], tw[1]], [xre])
                        tt(tw[0][:, :], wre[:, :], Sf_, ALU.mult, [wre, SIN], [tw[0]])
                        k.op('pool', lambda e: e.tensor_tensor(out=tw[1][:, :], in0=wim[:, :], in1=Cf_, op=ALU.mult), R=[wim, COS], W=[tw[1]])
                        tt(xim[:, :], tw[0][:, :], tw[1][:, :], ALU.add, [tw[0], tw[1]], [xim])
                        for ch in range(PL // 512):
                            cs = slice(ch * 512, (ch + 1) * 512)
                            gi = (base + ch * 512) // 512
                            gs_ = slice(base + ch * 512, base + (ch + 1) * 512)
                            py = self.psum()
                            k.op('pe', lambda e: e.matmul(py[:, :], lhsT=LC[0][:, :], rhs=xre[:, cs], start=True, stop=False), R=[LC[0], xre], W=[py])
                            k.op('pe', lambda e: e.matmul(py[:, :], lhsT=LC[1][:, :], rhs=xim[:, cs], start=False, stop=True), R=[LC[1], xim], W=[py])
                            if first_acc[gi]:
                                first_acc[gi] = False
                                k.op('act', lambda e: e.activation(out=acc[:, gs_], in_=py[:, :], func=AF.Identity), R=[py], W=[acc])
                            else:
                                tt(acc[:, gs_], acc[:, gs_], py[:, :], ALU.add, [acc, py], [acc])
            uf = [k.sb(es, f"s_uf{i}", [128, 2048]) for i in range(2)]
            ob = [k.sb(es, f"s_ob{i}", [128, 2048], BF16) for i in range(2)]
            for pc in range(4):
                cs = slice(pc * 2048, (pc + 1) * 2048)
                k.dma('sp', uf[pc % 2][:, :], self.PT[896:1024, cs], R=[self.PT], W=[uf[pc % 2]])
                k.op('dve', lambda e: e.scalar_tensor_tensor(out=ob[pc % 2][:, :], in0=uf[pc % 2][:, :], scalar=sk[:, 0:1], in1=acc[:, cs], op0=ALU.mult, op1=ALU.add),
                     R=[uf[pc % 2], sk, acc], W=[ob[pc % 2]])
                k.dma('sp', self.S5_sh[:, cs], ob[pc % 2][:, :], R=[ob[pc % 2]], W=[self.S5_sh])
            k.barrier()

    def phase3_hyena(self, l):
        k = self.k
        W = self.W[l]
        L = S
        NB = S // 128
        TW = 16256
        HX0 = k.dram(f"h_X0_{l}", [128, S]); HZ = k.dram(f"h_Z_{l}", [128, S]); GD = k.dram(f"h_GD_{l}", [128, 16384], BF16)
        with ExitStack() as es:
            zTr = k.sb(es, "h_zTr", [128, 128, NB], BF16)
            cw = k.sb(es, "h_cw", [128, 3, 4]); hsk = k.sb(es, "h_sk", [128, 1]); hdec = k.sb(es, "h_dec", [128, 2])
            k.dma('sp', cw[:, :, :], W["hcw"][:, :, :], R=[W["hcw"]], W=[cw])
            k.dma('sp', hsk[:, :], W["hsk"][:, :], R=[W["hsk"]], W=[hsk])
            k.dma('sp', hdec[:, :], W["hdec"][:, :], R=[W["hdec"]], W=[hdec])
            with ExitStack() as esA:
                xs = k.sb(esA, "h_xs", [128, S]); u = k.sb(esA, "h_u", [128, S]); z = k.sb(esA, "h_z", [128, S])
                for j in range(3):
                    k.dma('sp', xs[:, :], self.PT[512 + j * 128:512 + (j + 1) * 128, :], R=[self.PT], W=[xs])
                    dst = z if j == 1 else u
                    k.op('act', lambda e: e.activation(out=dst[:, :], in_=xs[:, :], func=AF.Identity, scale=cw[:, j, 1:2], bias=cw[:, j, 3:4]), R=[xs, cw], W=[dst])
                    k.op('dve', lambda e: e.scalar_tensor_tensor(out=dst[:, 1:S], in0=xs[:, 0:S - 1], scalar=cw[:, j, 0:1], in1=dst[:, 1:S], op0=ALU.mult, op1=ALU.add),
                         R=[xs, cw, dst], W=[dst])
                    k.op('dve', lambda e: e.scalar_tensor_tensor(out=dst[:, 0:S - 1], in0=xs[:, 1:S], scalar=cw[:, j, 2:3], in1=dst[:, 0:S - 1], op0=ALU.mult, op1=ALU.add),
                         R=[xs, cw, dst], W=[dst])
                    if j == 0:
                        k.dma('sp', HX0[:, :], u[:, :], R=[u], W=[HX0])
                    if j == 2:
                        k.op('dve', lambda e: e.tensor_tensor(out=z[:, :], in0=z[:, :], in1=u[:, :], op=ALU.mult), R=[z, u], W=[z])
                k.dma('sp', HZ[:, :], z[:, :], R=[z], W=[HZ])
                k.op('dve', lambda e: e.tensor_copy(out=u[:, :], in_=z[:, ::-1]), R=[z], W=[u])
                for ap_ in range(0, NB, 4):
                    ps = self.psum()
                    for j in range(4):
                        a_ = ap_ + j
                        k.op('pe', lambda e: e.transpose(out=ps[:, j * 128:(j + 1) * 128], in_=u[:, a_ * 128:(a_ + 1) * 128], identity=self.identf[:, :]),
                             R=[u, self.identf], W=[ps])
                    for j in range(4):
                        a = NB - 1 - (ap_ + j)
                        self.evac(zTr[:, :, a], ps[:, j * 128:(j + 1) * 128], R=[ps], W=[zTr])
                k.barrier()
            with ExitStack() as esB:
                F = [k.sb(esB, f"h_F{d}", [128, L]) for d in range(2)]
                w1 = k.sb(esB, "h_w1", [65, 64]); w2 = k.sb(esB, "h_w2", [64, 64]); w3 = k.sb(esB, "h_w3", [64, 2, 128])
                cols = {n: k.sb(esB, f"h_{n}", [64, 1]) for n in ["hb1", "hb2", "hf0", "hf1", "fb1", "fb2"]}
                k.dma('sp', w1[:, :], W["hw1"][:, :], R=[W["hw1"]], W=[w1])
                k.dma('sp', w2[:, :], W["hw2"][:, :], R=[W["hw2"]], W=[w2])
                k.dma('sp', w3[:, :, :], W["hw3"][:, :, :], R=[W["hw3"]], W=[w3])
                for n in ["hb1", "hb2", "hf0", "hf1"]:
                    k.dma('sp', cols[n][:, :], W[n][:, :], R=[W[n]], W=[cols[n]])
                k.op('dve', lambda e: e.tensor_tensor(out=cols["fb1"][:, :], in0=cols["hb1"][:, :], in1=cols["hf0"][:, :], op=ALU.mult), R=[cols["hb1"], cols["hf0"]], W=[cols["fb1"]])
                k.op('dve', lambda e: e.tensor_tensor(out=cols["fb2"][:, :], in0=cols["hb2"][:, :], in1=cols["hf1"][:, :], op=ALU.mult), R=[cols["hb2"], cols["hf1"]], W=[cols["fb2"]])
                pidx = k.sb(esB, "h_pidx", [128, 1]); om = k.sb(esB, "h_om", [128, 1]); ndec = k.sb(esB, "h_ndec", [128, 2])
                k.op('pool', lambda e: e.iota(pidx[:, :], pattern=[[0, 1]], base=0, channel_multiplier=1, allow_small_or_imprecise_dtypes=True), W=[pidx])
                dlt = (15.0 - 1e-4) / 15.0
                k.op('dve', lambda e: e.tensor_scalar(out=om[0:16, :], in0=pidx[0:16, :], scalar1=dlt, scalar2=1e-4, op0=ALU.mult, op1=ALU.add), R=[pidx], W=[om])
                k.op('dve', lambda e: e.tensor_scalar(out=om[32:48, :], in0=pidx[32:48, :], scalar1=dlt, scalar2=1e-4 - 32.0 * dlt, op0=ALU.mult, op1=ALU.add), R=[pidx], W=[om])
                k.op('dve', lambda e: e.tensor_single_scalar(out=om[0:48, :], in_=om[0:48, :], scalar=TWO_PI / L, op=ALU.mult), R=[om], W=[om])
                k.op('act', lambda e: e.activation(out=ndec[:, :], in_=hdec[:, :], func=AF.Abs), R=[hdec], W=[ndec])
                k.op('dve', lambda e: e.tensor_single_scalar(out=ndec[:, :], in_=ndec[:, :], scalar=-1.0 / (L - 1), op=ALU.mult), R=[ndec], W=[ndec])
                CH = 512
                iot = k.sb(esB, "h_iot", [128, CH]); feat = k.sb(esB, "h_feat", [65, CH]); ang = k.sb(esB, "h_ang", [48, CH])
                a1 = k.sb(esB, "h_a1", [64, CH]); h1 = k.sb(esB, "h_h1", [64, CH]); h2 = k.sb(esB, "h_h2", [64, CH]); win = k.sb(esB, "h_win", [128, CH])
                asum = k.sb(esB, "h_asum", [128, 2, L // CH]); tot = k.sb(esB, "h_tot", [128, 2])
                tmp = self.sctmp(esB, CH)
                k.op('dve', lambda e: e.memset(feat[:, :], 0.0), W=[feat])
                for ci in range(L // CH):
                    j0 = ci * CH
                    k.op('pool', lambda e: e.iota(iot[:, :], pattern=[[1, CH]], base=j0, channel_multiplier=0, allow_small_or_imprecise_dtypes=True), W=[iot])
                    k.op('dve', lambda e: e.tensor_scalar(out=ang[:, :], in0=iot[0:48, :], scalar1=om[0:48, 0:1], scalar2=None, op0=ALU.mult), R=[iot, om], W=[ang])
                    self.sinr(tmp, ang, ang[0:16, :], CH, feat, feat[0:16, :], 0.5 * math.pi, 0, 16)
                    self.sinr(tmp, ang, ang[32:48, :], CH, feat, feat[32:48, :], math.pi, 32, 48)
                    k.op('dve', lambda e: e.tensor_single_scalar(out=feat[64:65, :], in_=iot[64:65, :], scalar=1.0 / (L - 1), op=ALU.mult), R=[iot], W=[feat])
                    ps = self.psum()
                    k.op('pe', lambda e: e.matmul(ps[0:64, 0:CH], lhsT=w1[:, :], rhs=feat[:, :], start=True, stop=True), R=[w1, feat], W=[ps])
                    k.op('act', lambda e: e.activation(out=a1[:, :], in_=ps[0:64, 0:CH], func=AF.Identity, scale=cols["hf0"][:, 0:1], bias=cols["fb1"][:, 0:1]),
                         R=[ps, cols["hf0"], cols["fb1"]], W=[a1])
                    self.sinr(tmp, a1, a1[:, :], CH, h1, h1[:, :], 0.0, 0, 64)
                    ps = self.psum()
                    k.op('pe', lambda e: e.matmul(ps[0:64, 0:CH], lhsT=w2[:, :], rhs=h1[:, :], start=True, stop=True), R=[w2, h1], W=[ps])
                    k.op('act', lambda e: e.activation(out=a1[:, :], in_=ps[0:64, 0:CH], func=AF.Identity, scale=cols["hf1"][:, 0:1], bias=cols["fb2"][:, 0:1]),
                         R=[ps, cols["hf1"], cols["fb2"]], W=[a1])
                    self.sinr(tmp, a1, a1[:, :], CH, h2, h2[:, :], 0.0, 0, 64)
                    for d in range(2):
                        ps = self.psum()
                        k.op('pe', lambda e: e.matmul(ps[:, 0:CH], lhsT=w3[:, d, :], rhs=h2[:, :], start=True, stop=True), R=[w3, h2], W=[ps])
                        k.op('act', lambda e: e.activation(out=win[:, :], in_=iot[:, :], func=AF.Exp, scale=ndec[:, d:d + 1]), R=[iot, ndec], W=[win])
                        k.op('dve', lambda e: e.tensor_tensor(out=F[d][:, j0:j0 + CH], in0=ps[:, 0:CH], in1=win[:, :], op=ALU.mult), R=[ps, win], W=[F[d]])
                        k.op('dve', lambda e: e.tensor_reduce(out=asum[:, d, ci:ci + 1], in_=F[d][:, j0:j0 + CH], axis=AX.X, op=ALU.add, apply_absolute_value=True),
                             R=[F[d]], W=[asum])
                k.op('dve', lambda e: e.tensor_reduce(out=tot[:, :], in_=asum[:, :, :], axis=AX.X, op=ALU.add), R=[asum], W=[tot])
                k.op('dve', lambda e: e.tensor_single_scalar(out=tot[:, :], in_=tot[:, :], scalar=1e-6, op=ALU.add), R=[tot], W=[tot])
                k.op('dve', lambda e: e.reciprocal(out=tot[:, :], in_=tot[:, :]), R=[tot], W=[tot])
                gb16 = [k.sb(esB, f"h_gb{i}", [128, 2048], BF16) for i in range(2)]
                n_ = 0
                for pc in range(4):
                    g = gb16[n_ % 2]; n_ += 1
                    k.op('dve', lambda e: e.tensor_scalar(out=g[:, :], in0=F[0][:, pc * 2048:(pc + 1) * 2048], scalar1=tot[:, 0:1], scalar2=None, op0=ALU.mult),
                         R=[F[0], tot], W=[g])
                    k.dma('sp', GD[:, 8191 + pc * 2048:8191 + (pc + 1) * 2048], g[:, :], R=[g], W=[GD])
                for pc in range(4):
                    g = gb16[n_ % 2]; n_ += 1
                    i0 = pc * 2048
                    n = 2048 if pc < 3 else 2047
                    lo = 8191 - i0 - n + 1
                    src = F[1][:, lo:lo + n]
                    k.op('dve', lambda e: e.tensor_scalar(out=g[:, 0:n], in0=src[:, ::-1], scalar1=tot[:, 1:2], scalar2=None, op0=ALU.mult), R=[F[1], tot], W=[g])
                    k.dma('sp', GD[:, i0:i0 + n], g[:, 0:n], R=[g], W=[GD])
                k.barrier()
            with ExitStack() as esC:
                Tt = [k.sb(esC, f"h_Tt{i}", [128, TW], BF16) for i in range(2)]
                Yt = k.sb(esC, "h_Yt", [128, NB, 128])
                for c8 in range(16):
                    ps = self.psum()
                    for cj in range(8):
                        c = c8 * 8 + cj
                        tt_ = Tt[c % 2]
                        src = bass.AP(tensor=GD.t, offset=c * 16384, ap=[[1, 128], [1, TW]])
                        k.dma('sp', tt_[:, :], src, R=[GD], W=[tt_])
                        o0 = cj * 64
                        dl = [0] + [x for d_ in range(1, NB) for x in (d_, -d_)]
                        for n_i, dlt_ in enumerate(dl):
                            y0 = 128 * dlt_ + 8064
                            if dlt_ >= 0:
                                rhs = zTr[:, c, 0:NB - dlt_]; out = ps[:, o0 + dlt_:o0 + NB]
                            else:
                                rhs = zTr[:, c, -dlt_:NB]; out = ps[:, o0:o0 + NB + dlt_]
                            k.op('pe', lambda e: e.matmul(out, lhsT=tt_[:, y0:y0 + 128], rhs=rhs, start=(n_i == 0), stop=(n_i == len(dl) - 1), skip_group_check=True),
                                 R=[tt_, zTr], W=[ps])
                    self.evac(Yt[:, :, c8 * 8:(c8 + 1) * 8], ps[:, :].rearrange("p (c a) -> p a c", c=8), R=[ps], W=[Yt])
                zf = [k.sb(esC, f"h_zf{i}", [128, 512]) for i in range(2)]
                xf = [k.sb(esC, f"h_xf{i}", [128, 512]) for i in range(2)]
                ob = [k.sb(esC, f"h_ob{i}", [128, 512], BF16) for i in range(2)]
                for a4 in range(0, NB, 4):
                    i2 = (a4 // 4) % 2
                    cs = slice(a4 * 128, (a4 + 4) * 128)
                    k.dma('sp', zf[i2][:, :], HZ[:, cs], R=[HZ], W=[zf[i2]])
                    k.dma('sp', xf[i2][:, :], HX0[:, cs], R=[HX0], W=[xf[i2]])
                    ps = self.psum()
                    for j in range(4):
                        k.op('pe', lambda e: e.transpose(out=ps[:, j * 128:(j + 1) * 128], in_=Yt[:, a4 + j, :], identity=self.identf[:, :]), R=[Yt, self.identf], W=[ps])
                    k.op('dve', lambda e: e.scalar_tensor_tensor(out=zf[i2][:, :], in0=zf[i2][:, :], scalar=hsk[:, 0:1], in1=ps[:, :], op0=ALU.mult, op1=ALU.add),
                         R=[zf[i2], hsk, ps], W=[zf[i2]])
                    k.op('dve', lambda e: e.tensor_tensor(out=ob[i2][:, :], in0=zf[i2][:, :], in1=xf[i2][:, :], op=ALU.mult), R=[zf[i2], xf[i2]], W=[ob[i2]])
                    k.dma('sp', self.HY_sh[:, cs], ob[i2][:, :], R=[ob[i2]], W=[self.HY_sh])
                k.barrier()

    def layernorm(self, es, Y, grow, brow, out):
        k = self.k
        s6 = k.sb(es, "ln_s6", [128, 4, 6]); m2 = k.sb(es, "ln_m2", [128, 2])
        for c4 in range(4):
            k.op('dve', lambda e: e.bn_stats(out=s6[:, c4, :], in_=Y[:, c4 * 512:(c4 + 1) * 512]), R=[Y], W=[s6])
        k.op('dve', lambda e: e.bn_aggr(out=m2[:, :], in_=s6[:, :, :].rearrange("p a b -> p (a b)")), R=[s6], W=[m2])
        self.rstd(m2)
        k.op('dve', lambda e: e.tensor_scalar(out=Y[:, :], in0=Y[:, :], scalar1=m2[:, 0:1], scalar2=m2[:, 1:2], op0=ALU.subtract, op1=ALU.mult),
             R=[Y, m2], W=[Y])
        k.op('pool', lambda e: e.tensor_tensor(out=Y[:, :], in0=Y[:, :], in1=grow[:, :], op=ALU.mult), R=[Y, grow], W=[Y])
        k.op('dve', lambda e: e.tensor_tensor(out=out[:, :], in0=Y[:, :], in1=brow[:, :], op=ALU.add), R=[Y, brow], W=[out])

    def phase4_token(self, l):
        k = self.k
        W = self.W[l]; G = self.GW[l]
        with ExitStack() as es:
            lnt = [k.sb(es, f"t_ln{i}", [128, D]) for i in range(4)]
            for i in range(4):
                k.dma('sp', lnt[i][:, :], W["ln"][:, i * D:(i + 1) * D], R=[W["ln"]], W=[lnt[i]])
            xt = k.sb(es, "t_xt", [128, 16, 128], BF16)
            act3 = [k.sb(es, f"t_a{i}", [128, 8, 128], BF16) for i in range(3)]
            wgt = [k.sb(es, f"t_wg{i}", [128, 16, 512], BF16) for i in range(2)]
            w8 = [k.sb(es, f"t_w8{i}", [128, 8, 512], BF16) for i in range(2)]
            gs = k.sb(es, "t_gs", [128, 512]); tmp = k.sb(es, "t_tmp", [128, 512]); sg = k.sb(es, "t_sg", [128, 512])
            merged = k.sb(es, "t_merged", [128, D]); Y = k.sb(es, "t_Y", [128, D])
            mT = k.sb(es, "t_mT", [128, 16, 128], BF16)
            nw = [0, 0]

            def gate(i, b, cc):
                wt = wgt[nw[0] % 2]; nw[0] += 1
                c0 = b * D + cc * 512
                k.dma('sp', wt[:, :, :], G["wg"][:, c0:c0 + 512].rearrange("(a p) c -> p a c", p=128), R=[G["wg"]], W=[wt])
                ps = self.psum()
                for dk in range(16):
                    k.op('pe', lambda e: e.matmul(ps[:, :], lhsT=xt[:, dk, :], rhs=wt[:, dk, :], start=(dk == 0), stop=(dk == 15)), R=[xt, wt], W=[ps])
                k.op('act', lambda e: e.activation(out=gs[:, :], in_=ps[:, :], func=AF.Sigmoid), R=[ps], W=[gs])

            def proj8(a, wfull, c0):
                wt = w8[nw[1] % 2]; nw[1] += 1
                k.dma('sp', wt[:, :, :], wfull[:, c0:c0 + 512].rearrange("(a p) c -> p a c", p=128), R=[wfull], W=[wt])
                ps = self.psum()
                for dk in range(8):
                    k.op('pe', lambda e: e.matmul(ps[:, :], lhsT=a[:, dk, :], rhs=wt[:, dk, :], start=(dk == 0), stop=(dk == 7)), R=[a, wt], W=[ps])
                return ps

            for i in range(8):
                tk = slice(self.blk * TPC + i * 128, self.blk * TPC + (i + 1) * 128)
                k.dma('sp', xt[:, :, :], self.xT_sh[:, i * 128:(i + 1) * 128].rearrange("(a p) t -> p a t", p=128), R=[self.xT_sh], W=[xt])
                for j_ in range(3):
                    k.dma('sp', act3[j_][:, :, :], self.actin[j_][:, tk].rearrange("(a p) t -> p a t", p=128), R=[self.actin[j_]], W=[act3[j_]])
                for cc in range(4):
                    csl = slice(cc * 512, (cc + 1) * 512)
                    gate(i, 0, cc)
                    ps = proj8(act3[0], G["wmo"], cc * 512)
                    k.op('dve', lambda e: e.tensor_tensor(out=merged[:, csl], in0=ps[:, :], in1=gs[:, :], op=ALU.mult), R=[ps, gs], W=[merged])
                    gate(i, 1, cc)
                    ps = proj8(act3[1], G["who"], cc * 512)
                    k.op('dve', lambda e: e.tensor_tensor(out=tmp[:, :], in0=ps[:, :], in1=gs[:, :], op=ALU.mult), R=[ps, gs], W=[tmp])
                    k.op('pool', lambda e: e.tensor_tensor(out=merged[:, csl], in0=merged[:, csl], in1=tmp[:, :], op=ALU.add), R=[merged, tmp], W=[merged])
                    gate(i, 2, cc)
                    psg = proj8(act3[2], G["wsg"], D + cc * 512)
                    k.op('act', lambda e: e.activation(out=sg[:, :], in_=psg[:, :], func=AF.Sigmoid), R=[psg], W=[sg])
                    ps = proj8(act3[2], G["wsg"], cc * 512)
                    k.op('dve', lambda e: e.tensor_tensor(out=sg[:, :], in0=ps[:, :], in1=sg[:, :], op=ALU.mult), R=[ps, sg], W=[sg])
                    k.op('dve', lambda e: e.tensor_tensor(out=tmp[:, :], in0=sg[:, :], in1=gs[:, :], op=ALU.mult), R=[sg, gs], W=[tmp])
                    k.op('pool', lambda e: e.tensor_tensor(out=merged[:, csl], in0=merged[:, csl], in1=tmp[:, :], op=ALU.add), R=[merged, tmp], W=[merged])
                for g4 in range(4):
                    ps = self.psum()
                    for j in range(4):
                        dk = g4 * 4 + j
                        k.op('pe', lambda e: e.transpose(out=ps[:, j * 128:(j + 1) * 128], in_=merged[:, dk * 128:(dk + 1) * 128], identity=self.identf[:, :]),
                             R=[merged, self.identf], W=[ps])
                    self.evac(mT[:, g4 * 4:(g4 + 1) * 4, :], ps[:, :].rearrange("p (a b) -> p a b", a=4), R=[ps], W=[mT])
                for cc in range(4):
                    csl = slice(cc * 512, (cc + 1) * 512)
                    wt = wgt[nw[0] % 2]; nw[0] += 1
                    k.dma('sp', wt[:, :, :], G["wo"][:, csl].rearrange("(a p) c -> p a c", p=128), R=[G["wo"]], W=[wt])
                    ps = self.psum()
                    for dk in range(16):
                        k.op('pe', lambda e: e.matmul(ps[:, :], lhsT=mT[:, dk, :], rhs=wt[:, dk, :], start=(dk == 0), stop=(dk == 15)), R=[mT, wt], W=[ps])
                    k.op('dve', lambda e: e.scalar_tensor_tensor(out=Y[:, csl], in0=self.X[i][:, csl], scalar=ALPHA, in1=ps[:, :], op0=ALU.mult, op1=ALU.add),
                         R=[self.X[i], ps], W=[Y])
                with ExitStack() as es2:
                    self.layernorm(es2, Y, lnt[0], lnt[1], self.X[i])
            k.barrier()

    def phase5_peer(self, l):
        k = self.k
        W = self.W[l]; G = self.GW[l]
        NEG = -1e30
        C1 = math.sqrt(0.044715)
        C2 = 2.0 * math.sqrt(2.0 / math.pi)
        with ExitStack() as es:
            psk = k.sb(es, "p_sk", [128, 16, 128])
            k.dma('sp', psk[:, :, :], W["psk"][:, :, :], R=[W["psk"]], W=[psk])
            x1T = k.sb(es, "p_x1T", [128, 16, 512], BF16)
            skhi = k.sb(es, "p_skhi", [128, 16, 128], BF16); sklo = k.sb(es, "p_sklo", [128, 16, 128], BF16)
            qhi = k.sb(es, "p_qhi", [128, 512], BF16); qlo = k.sb(es, "p_qlo", [128, 512], BF16)
            k.op('dve', lambda e: e.tensor_copy(out=skhi[:, :, :], in_=psk[:, :, :]), R=[psk], W=[skhi])
            k.op('dve', lambda e: e.tensor_tensor(out=sklo[:, :, :], in0=psk[:, :, :], in1=skhi[:, :, :], op=ALU.subtract), R=[psk, skhi], W=[sklo])
            s1 = [k.sb(es, f"p_s1_{i}", [128, 8, 128]) for i in range(4)]
            s2 = [k.sb(es, f"p_s2_{i}", [128, 8, 128]) for i in range(4)]
            tau = [k.sb(es, f"p_tau{i}", [128, 8]) for i in range(4)]
            nkap = [k.sb(es, f"p_nkap{i}", [128, 8]) for i in range(4)]
            wqt = [k.sb(es, f"p_wq{i}", [128, 16, 128], BF16) for i in range(1)]
            qT = [k.sb(es, f"p_qT{i}", [128, 512]) for i in range(1)]
            uT = [k.sb(es, f"p_uT{i}", [128, 16, 512], BF16) for i in range(1)]
            Vt = k.sb(es, "p_V", [128, 4, D], BF16)
            HT = [k.sb(es, f"p_HT{i}", [128, 512], BF16) for i in range(4)]
            gh = [k.sb(es, f"p_gh{i}", [128, 4, 128], BF16) for i in range(8)]
            sm = [k.sb(es, f"p_sm{i}", [128, 4, 128]) for i in range(2)]
            ex = [k.sb(es, f"p_ex{i}", [128, 4, 128]) for i in range(2)]
            g1 = k.sb(es, "p_g1", [128, 512]); g2 = k.sb(es, "p_g2", [128, 512]); g3 = k.sb(es, "p_g3", [128, 512])
            t16 = [k.sb(es, f"p_t16_{i}", [128, 24]) for i in range(2)]
            wk = k.sb(es, "p_wk", [128, 128]); wk2 = k.sb(es, "p_wk2", [128, 128])
            cand = k.sb(es, "p_cand", [128, 24, 24]); cwk = k.sb(es, "p_cwk", [128, 576]); cwk2 = k.sb(es, "p_cwk2", [128, 576])
            b24 = k.sb(es, "p_b24", [128, 24]); e16 = k.sb(es, "p_e16", [128, 16]); st = k.sb(es, "p_st", [128, 4])
            nq = 0
            import os
            STAGE = int(os.environ.get("PEER_STAGE", "9")); NEG_ = int(os.environ.get("PEER_NEG", "32")); NHF = int(os.environ.get("PEER_NHF", "2"))
            for hf in range(NHF):
                Xh = self.X[hf * 4:(hf + 1) * 4]
                for ti in range(4):
                    for g4 in range(4):
                        ps = self.psum()
                        for j in range(4):
                            dk = g4 * 4 + j
                            k.op('pe', lambda e: e.transpose(out=ps[:, j * 128:(j + 1) * 128], in_=Xh[ti][:, dk * 128:(dk + 1) * 128], identity=self.identf[:, :]),
                                 R=[Xh[ti], self.identf], W=[ps])
                        self.evac(x1T[:, g4 * 4:(g4 + 1) * 4, ti * 128:(ti + 1) * 128], ps[:, :].rearrange("p (a b) -> p a b", a=4), R=[ps], W=[x1T])
                    k.op('pool', lambda e: e.tensor_single_scalar(out=Xh[ti][:, :], in_=Xh[ti][:, :], scalar=ALPHA, op=ALU.mult), R=[Xh[ti]], W=[Xh[ti]])
                for hc in range(16 if STAGE >= 1 else 0):
                    h, c = hc // 2, hc % 2
                    wt = wqt[0]; q_ = qT[0]; nq += 1
                    WQ = G[os.environ.get("PEER_WQ", "wq")]
                    k.dma('sp', wt[:, :, :], WQ[:, hc * 128:(hc + 1) * 128].rearrange("(a p) c -> p a c", p=128), R=[WQ], W=[wt])
                    ps = self.psum()
                    for dk in range(16):
                        k.op('pe', lambda e: e.matmul(ps[:, :], lhsT=wt[:, dk, :], rhs=x1T[:, dk, :], start=(dk == 0), stop=(dk == 15)), R=[wt, x1T], W=[ps])
                    self.evac(q_[:, :], ps[:, :], R=[ps], W=[q_])
                    if os.environ.get("PEER_NOSC"):
                        continue
                    k.op('dve', lambda e: e.tensor_copy(out=qhi[:, :], in_=q_[:, :]), R=[q_], W=[qhi])
                    k.op('dve', lambda e: e.tensor_tensor(out=qlo[:, :], in0=q_[:, :], in1=qhi[:, :], op=ALU.subtract), R=[q_, qhi], W=[qlo])
                    PX = int(os.environ.get("PEER_X", "0"))
                    if PX == 2:
                        continue
                    ps2 = self.psum()
                    for ti in range(4):
                        tsl = slice(ti * 128, (ti + 1) * 128)
                        k.op('pe', lambda e: e.matmul(ps2[:, tsl], lhsT=qhi[:, tsl], rhs=skhi[:, hc, :], start=True, stop=False), R=[qhi, skhi], W=[ps2])
                        k.op('pe', lambda e: e.matmul(ps2[:, tsl], lhsT=qhi[:, tsl], rhs=sklo[:, hc, :], start=False, stop=False), R=[qhi, sklo], W=[ps2])
                        k.op('pe', lambda e: e.matmul(ps2[:, tsl], lhsT=qlo[:, tsl], rhs=skhi[:, hc, :], start=False, stop=True), R=[qlo, skhi], W=[ps2])
                    for ti in range(4 if PX != 1 else 0):
                        dst = (s1 if c == 0 else s2)[ti]
                        k.op('dve', lambda e: e.tensor_copy(out=dst[:, h, :], in_=ps2[:, ti * 128:(ti + 1) * 128]), R=[ps2], W=[dst])
                for ti in range(4 if STAGE >= 2 else 0):
                    for h in range(8):
                        for c, sc in ((0, s1[ti]), (1, s2[ti])):
                            t_ = t16[c]
                            k.op('dve', lambda e: e.max(out=t_[:, 0:8], in_=sc[:, h, :]), R=[sc], W=[t_])
                            k.op('dve', lambda e: e.match_replace(out=wk[:, :], in_to_replace=t_[:, 0:8], in_values=sc[:, h, :], imm_value=NEG), R=[sc, t_], W=[wk])
                            k.op('dve', lambda e: e.max(out=t_[:, 8:16], in_=wk[:, :]), R=[wk], W=[t_])
                            k.op('dve', lambda e: e.match_replace(out=wk2[:, :], in_to_replace=t_[:, 8:16], in_values=wk[:, :], imm_value=NEG), R=[wk, t_], W=[wk2])
                            k.op('dve', lambda e: e.max(out=t_[:, 16:24], in_=wk2[:, :]), R=[wk2], W=[t_])
                        k.op('dve', lambda e: e.tensor_tensor(out=cand[:, :, :], in0=t16[0][:, :].unsqueeze(2).to_broadcast([128, 24, 24]),
                                                              in1=t16[1][:, :].unsqueeze(1).to_broadcast([128, 24, 24]), op=ALU.add), R=[t16[0], t16[1]], W=[cand])
                        cf = cand[:, :, :].rearrange("p a b -> p (a b)")
                        k.op('dve', lambda e: e.max(out=b24[:, 0:8], in_=cf), R=[cand], W=[b24])
                        k.op('dve', lambda e: e.match_replace(out=cwk[:, :], in_to_replace=b24[:, 0:8], in_values=cf, imm_value=NEG), R=[cand, b24], W=[cwk])
                        k.op('dve', lambda e: e.max(out=b24[:, 8:16], in_=cwk[:, :]), R=[cwk], W=[b24])
                        k.op('dve', lambda e: e.match_replace(out=cwk2[:, :], in_to_replace=b24[:, 8:16], in_values=cwk[:, :], imm_value=NEG), R=[cwk, b24], W=[cwk2])
                        k.op('dve', lambda e: e.max(out=b24[:, 16:24], in_=cwk2[:, :]), R=[cwk2], W=[b24])
                        k.op('dve', lambda e: e.tensor_tensor(out=tau[ti][:, h:h + 1], in0=b24[:, 15:16], in1=b24[:, 16:17], op=ALU.add), R=[b24], W=[tau[ti]])
                        k.op('dve', lambda e: e.tensor_single_scalar(out=st[:, 0:1], in_=b24[:, 0:1], scalar=-1.0, op=ALU.mult), R=[b24], W=[st])
                        k.op('act', lambda e: e.activation(out=e16[:, :], in_=b24[:, 0:16], func=AF.Exp, bias=st[:, 0:1]), R=[b24, st], W=[e16])
                        k.op('dve', lambda e: e.tensor_reduce(out=st[:, 1:2], in_=e16[:, :], axis=AX.X, op=ALU.add), R=[e16], W=[st])
                        k.op('act', lambda e: e.activation(out=st[:, 2:3], in_=st[:, 1:2], func=AF.Ln), R=[st], W=[st])
                        k.op('dve', lambda e: e.tensor_tensor(out=nkap[ti][:, h:h + 1], in0=st[:, 0:1], in1=st[:, 2:3], op=ALU.subtract), R=[st], W=[nkap[ti]])
                    k.op('dve', lambda e: e.tensor_single_scalar(out=tau[ti][:, :], in_=tau[ti][:, :], scalar=0.5, op=ALU.mult), R=[tau[ti]], W=[tau[ti]])
                for eg in range(NEG_ if STAGE >= 3 else 0):
                    u_ = uT[0]
                    r_, e0 = (eg * 512) // 2048, (eg * 512) % 2048
                    k.dma('sp', u_[:, :, :], G["pu"][r_ * D:(r_ + 1) * D, e0:e0 + 512].rearrange("(a p) e -> p a e", p=128), R=[G["pu"]], W=[u_])
                    k.dma('sp', Vt[:, :, :], G["pv"][eg * 512:(eg + 1) * 512, :].rearrange("(a p) d -> p a d", p=128), R=[G["pv"]], W=[Vt])
                    pg = [self.psum() for _ in range(4)]
                    for ti in range(4):
                        for h in range(8):
                            sm_, ex_, g_ = sm[h % 2], ex[h % 2], gh[h]
                            k.op('pool', lambda e: e.tensor_tensor(out=sm_[:, :, :], in0=s1[ti][:, h, eg * 4:(eg + 1) * 4].unsqueeze(2).to_broadcast([128, 4, 128]),
                                                                   in1=s2[ti][:, h, :].unsqueeze(1).to_broadcast([128, 4, 128]), op=ALU.add),
                                 R=[s1[ti], s2[ti]], W=[sm_])
                            k.op('act', lambda e: e.activation(out=ex_[:, :, :], in_=sm_[:, :, :], func=AF.Exp, bias=nkap[ti][:, h:h + 1]), R=[sm_, nkap[ti]], W=[ex_])
                            k.op('dve', lambda e: e.scalar_tensor_tensor(out=g_[:, :, :], in0=sm_[:, :, :], scalar=tau[ti][:, h:h + 1], in1=ex_[:, :, :],
                                                                         op0=ALU.is_gt, op1=ALU.mult), R=[sm_, ex_, tau[ti]], W=[g_])
                        for il in range(4):
                            for h in range(8):
                                k.op('pe', lambda e: e.matmul(pg[il][:, ti * 128:(ti + 1) * 128], lhsT=gh[h][:, il, :], rhs=self.identb[:, :], start=(h == 0), stop=(h == 7)),
                                     R=[gh[h], self.identb], W=[pg[il]])
                    for il in range(4):
                        pa = self.psum()
                        for dk in range(16):
                            k.op('pe', lambda e: e.matmul(pa[:, :], lhsT=u_[:, dk, il * 128:(il + 1) * 128], rhs=x1T[:, dk, :], start=(dk == 0), stop=(dk == 15)),
                                 R=[u_, x1T], W=[pa])
                        k.op('act', lambda e: e.activation(out=g1[:, :], in_=pa[:, :], func=AF.Square, scale=C1), R=[pa], W=[g1])
                        k.op('dve', lambda e: e.scalar_tensor_tensor(out=g2[:, :], in0=g1[:, :], scalar=1.0, in1=pa[:, :], op0=ALU.add, op1=ALU.mult), R=[g1, pa], W=[g2])
                        k.op('act', lambda e: e.activation(out=g3[:, :], in_=g2[:, :], func=AF.Sigmoid, scale=C2), R=[g2], W=[g3])
                        k.op('dve', lambda e: e.tensor_tensor(out=g2[:, :], in0=g3[:, :], in1=pa[:, :], op=ALU.mult), R=[g3, pa], W=[g2])
                        k.op('dve', lambda e: e.tensor_tensor(out=HT[il][:, :], in0=g2[:, :], in1=pg[il][:, :], op=ALU.mult), R=[g2, pg[il]], W=[HT[il]])
                    for ti in range(4):
                        for dc in range(4):
                            po = self.psum()
                            for il in range(4):
                                k.op('pe', lambda e: e.matmul(po[:, :], lhsT=HT[il][:, ti * 128:(ti + 1) * 128], rhs=Vt[:, il, dc * 512:(dc + 1) * 512], start=(il == 0), stop=(il == 3)),
                                     R=[HT[il], Vt], W=[po])
                            k.op('dve', lambda e: e.tensor_tensor(out=Xh[ti][:, dc * 512:(dc + 1) * 512], in0=Xh[ti][:, dc * 512:(dc + 1) * 512], in1=po[:, :], op=ALU.add),
                                 R=[Xh[ti], po], W=[Xh[ti]])
            k.barrier()
        with ExitStack() as es:
            lnt = [k.sb(es, f"p_ln{i}", [128, D]) for i in range(2)]
            Y = k.sb(es, "p_Y", [128, D])
            for i in range(2):
                k.dma('sp', lnt[i][:, :], W["ln"][:, (2 + i) * D:(3 + i) * D], R=[W["ln"]], W=[lnt[i]])
            for ti in range(8):
                k.op('act', lambda e: e.activation(out=Y[:, :], in_=self.X[ti][:, :], func=AF.Identity), R=[self.X[ti]], W=[Y])
                with ExitStack() as es2:
                    self.layernorm(es2, Y, lnt[0], lnt[1], self.X[ti])
            k.barrier()

    def psum(self):
        p = self.PS[self.psn % 8]
        self.psn += 1
        return p

    def evac(self, out, in_, R, W):
        self.pq += 1
        import os
        fe = os.environ.get("EVAC_ENG")
        if (self.pq % 2 and fe != "dve") or fe == "act":
            self.k.op('act', lambda e: e.activation(out=out, in_=in_, func=AF.Identity), R=R, W=W)
        else:
            self.k.op('dve', lambda e: e.tensor_copy(out=out, in_=in_), R=R, W=W)

    def phase2_inproj(self, l):
        k = self.k
        W = self.W[l]
        with ExitStack() as es:
            WF = k.sb(es, "WF", [128, 16, NWF], BF16)
            WT = k.sb(es, "WT", [128, 16, NWT], BF16)
            XR = [k.sb(es, f"XR{i}", [128, 16, TPC], BF16) for i in range(2)]
            SF = [k.sb(es, f"SF{i}", [128, 1024]) for i in range(2)]
            ST = [k.sb(es, f"ST{i}", [128, NWT]) for i in range(2)]
            k.dma('pool', WF[:, :, :], W["wf"][:, :].rearrange("(a p) c -> p a c", p=128), R=[W["wf"]], W=[WF])
            k.dma('pool', WT[:, :, :], W["wt"][:, :].rearrange("(a p) c -> p a c", p=128), R=[W["wt"]], W=[WT])
            nsf = nst = 0
            for r in range(NCORES):
                xr = XR[r % 2]
                k.dma('pool', xr[:, :, :], self.xT_ext[:, r * TPC:(r + 1) * TPC].rearrange("(a p) t -> p a t", p=128), R=[self.xT_ext], W=[xr])
                for g in range(9):
                    c0 = g * 128
                    m = 128 if g < 8 else 4
                    sf = SF[nsf % 2]
                    nsf += 1
                    for hf in range(2):
                        ps = self.psum()
                        for dk in range(16):
                            k.op('pe', lambda e: e.matmul(ps[0:m, :], lhsT=WF[:, dk, c0:c0 + m], rhs=xr[:, dk, hf * 512:(hf + 1) * 512],
                                                          start=(dk == 0), stop=(dk == 15)), R=[WF, xr], W=[ps])
                        self.evac(sf[0:m, hf * 512:(hf + 1) * 512], ps[0:m, :], R=[ps], W=[sf])
                    k.dma('sp', self.PT[c0:c0 + m, r * TPC:(r + 1) * TPC], sf[0:m, :], R=[sf], W=[self.PT])
                for tt in range(8):
                    st = ST[nst % 2]
                    nst += 1
                    for (n0, n1) in ((0, 512), (512, 768)):
                        ps = self.psum()
                        for dk in range(16):
                            k.op('pe', lambda e: e.matmul(ps[:, 0:n1 - n0], lhsT=xr[:, dk, tt * 128:(tt + 1) * 128], rhs=WT[:, dk, n0:n1],
                                                          start=(dk == 0), stop=(dk == 15)), R=[WT, xr], W=[ps])
                        self.evac(st[:, n0:n1], ps[:, 0:n1 - n0], R=[ps], W=[st])
                    t0 = r * TPC + tt * 128
                    k.dma('sp', self.KV[t0:t0 + 128, :], st[:, :], R=[st], W=[self.KV])
            k.barrier()


    def phase3_mlstm(self, l):
        k = self.k
        W = self.W[l]
        NCH = S // 128
        with ExitStack() as es:
            kT = k.sb(es, "m_kT", [128, 2, S], BF16)
            qT = k.sb(es, "m_qT", [128, 2, S], BF16)
            gb = k.sb(es, "m_gb", [128, 4]); ngb = k.sb(es, "m_ngb", [128, 4])
            mgain = k.sb(es, "m_gain", [128, 256])
            trif = k.sb(es, "m_trif", [128, 128]); trib = k.sb(es, "m_trib", [128, 128])
            G4 = k.sb(es, "m_G4", [64, 4, 128]); l1s = k.sb(es, "m_l1s", [64, 128]); bks = k.sb(es, "m_bks", [64, 128])
            ones64 = k.sb(es, "m_ones64", [64, 128])
            bkcol = [k.sb(es, f"m_bkcol{d}", [128, NCH]) for d in range(2)]
            dec = [k.sb(es, f"m_dec{d}", [128, NCH]) for d in range(2)]
            msk = k.sb(es, "m_msk", [128, 2048])
            lfp = [k.sb(es, f"m_lfp{i}", [128, 2048]) for i in range(2)]
            qst = [k.sb(es, f"m_qst{i}", [128, 2048]) for i in range(2)]
            vst = [k.sb(es, f"m_vst{i}", [128, 512]) for i in range(3)]
            ktb = [k.sb(es, f"m_ktb{i}", [128, 256], BF16) for i in range(3)]
            vtl = [k.sb(es, f"m_vtl{i}", [128, 257], BF16) for i in range(3)]
            sm = [k.sb(es, f"m_sm{i}", [128, 128], BF16) for i in range(2)]
            Cf = k.sb(es, "m_Cf", [128, 2, 257]); Cb = k.sb(es, "m_Cb", [128, 2, 257], BF16); Ct = k.sb(es, "m_Ct", [128, 2, 257])
            rr = [k.sb(es, f"m_rr{i}", [128, 1]) for i in range(2)]
            hst = [k.sb(es, f"m_hst{i}", [128, 256]) for i in range(2)]
            HD = [k.dram(f"m_HD{l}_{d}", [S, 256]) for d in range(2)]
            k.dma('sp', gb[:, :], W["gb"][:, :], R=[W["gb"]], W=[gb])
            k.dma('sp', mgain[:, :], W["mg"][:, :], R=[W["mg"]], W=[mgain])
            k.op('dve', lambda e: e.tensor_single_scalar(out=ngb[:, :], in_=gb[:, :], scalar=-1.0, op=ALU.mult), R=[gb], W=[ngb])
            k.op('pool', lambda e: e.affine_select(out=trif[:, :], in_=self.onesf[:, :], pattern=[[1, 128]], compare_op=ALU.is_ge, fill=0.0,
                                                   base=0, channel_multiplier=-1), R=[self.onesf], W=[trif])
            k.op('pool', lambda e: e.affine_select(out=trib[:, :], in_=self.onesf[:, :], pattern=[[-1, 128]], compare_op=ALU.is_ge, fill=0.0,
                                                   base=0, channel_multiplier=1), R=[self.onesf], W=[trib])
            k.op('pool', lambda e: e.iota(msk[:, :], pattern=[[0, 16], [1, 128]], base=0, channel_multiplier=0,
                                          allow_small_or_imprecise_dtypes=True), W=[msk])
            k.op('dve', lambda e: e.tensor_single_scalar(out=msk[:, :], in_=msk[:, :], scalar=0.0, op=ALU.is_gt), R=[msk], W=[msk])
            k.op('dve', lambda e: e.memset(ones64[:, :], 1.0), W=[ones64])
            for dk in range(2):
                k.dma('pool', kT[:, dk, :], self.PT[256 + dk * 128:256 + (dk + 1) * 128, :], R=[self.PT], W=[kT])
            k.dma('sp', G4[:, :, :], self.PT[1024:1028, :].rearrange("g (j s) -> j g s", s=128), R=[self.PT], W=[G4])
            for d in range(2):
                rev = (d == 1)
                R_ = (lambda ap: ap[:, ::-1]) if rev else (lambda ap: ap)
                k.op('act', lambda e: e.activation(out=l1s[:, :], in_=G4[:, 2 * d + 1, :], func=AF.Exp, scale=-1.0, bias=ngb[0:64, 2 * d + 1:2 * d + 2]),
                     R=[G4, ngb], W=[l1s])
                k.op('act', lambda e: e.activation(out=l1s[:, :], in_=l1s[:, :], func=AF.Ln, bias=self.onesf[0:64, 0:1]), R=[l1s, self.onesf], W=[l1s])
                k.op('dve', lambda e: e.tensor_tensor_scan(out=R_(bks[:, :]), data0=ones64[:, :], data1=R_(l1s[:, :]), initial=0.0,
                                                           op0=ALU.mult, op1=ALU.add), R=[ones64, l1s], W=[bks])
                k.op('dve', lambda e: e.tensor_tensor(out=bks[:, :], in0=bks[:, :], in1=G4[:, 2 * d, :], op=ALU.add), R=[bks, G4], W=[bks])
                k.op('act', lambda e: e.activation(out=bks[:, :], in_=bks[:, :], func=AF.Exp, bias=gb[0:64, 2 * d:2 * d + 1]), R=[bks, gb], W=[bks])
                ps = self.psum()
                k.op('pe', lambda e: e.transpose(out=ps[:, 0:64], in_=bks[:, :], identity=self.identf[0:64, 0:64]), R=[bks, self.identf], W=[ps])
                k.op('dve', lambda e: e.tensor_single_scalar(out=bkcol[d][:, :], in_=ps[:, 0:64], scalar=0.0625, op=ALU.mult), R=[ps], W=[bkcol[d]])
                for pc in range(4):
                    t0 = pc * 2048
                    lf = lfp[pc % 2]
                    k.dma('sp', lf[:, :], self.PT[1024 + 2 * d + 1:1024 + 2 * d + 2, t0:t0 + 2048].partition_broadcast(128), R=[self.PT], W=[lf])
                    k.op('act', lambda e: e.activation(out=lf[:, :], in_=lf[:, :], func=AF.Exp, scale=-1.0, bias=ngb[:, 2 * d + 1:2 * d + 2]),
                         R=[lf, ngb], W=[lf])
                    k.op('act', lambda e: e.activation(out=lf[:, :], in_=lf[:, :], func=AF.Ln, bias=self.onesf[:, 0:1]), R=[lf, self.onesf], W=[lf])
                    k.op('dve', lambda e: e.tensor_tensor_scan(out=R_(lf[:, :]), data0=msk[:, :], data1=R_(lf[:, :]), initial=0.0,
                                                               op0=ALU.mult, op1=ALU.add), R=[msk, lf], W=[lf])
                    k.op('act', lambda e: e.activation(out=lf[:, :], in_=lf[:, :], func=AF.Exp, scale=-1.0), R=[lf], W=[lf])
                    o0 = 0 if rev else 127
                    k.op('dve', lambda e: e.tensor_copy(out=dec[d][:, pc * 16:(pc + 1) * 16], in_=lf[:, o0::128]), R=[lf], W=[dec[d]])
                    for dk in range(2):
                        qs = qst[dk]
                        k.dma('sp', qs[:, :], self.PT[dk * 128:(dk + 1) * 128, t0:t0 + 2048], R=[self.PT], W=[qs])
                        k.op('pool' if dk else 'dve', lambda e: e.tensor_tensor(out=qT[:, dk, t0:t0 + 2048], in0=qs[:, :], in1=lf[:, :], op=ALU.mult),
                             R=[qs, lf], W=[qT])
                k.op('dve', lambda e: e.memset(Cf[:, :, :], 0.0), W=[Cf])
                k.op('dve', lambda e: e.memset(Cb[:, :, :], 0.0), W=[Cb])
                tri = trib if rev else trif
                for ji in range(NCH):
                    j = (NCH - 1 - ji) if rev else ji
                    t0 = j * 128
                    vs, kb, vt, smm, r1, hs = vst[ji % 3], ktb[ji % 3], vtl[ji % 3], sm[ji % 2], rr[ji % 2], hst[ji % 2]
                    k.dma('sp', vs[:, :], self.KV[t0:t0 + 128, 0:512], R=[self.KV], W=[vs])
                    k.op('act', lambda e: e.activation(out=kb[:, :], in_=vs[:, 256:512], func=AF.Identity), R=[vs], W=[kb])
                    k.op('dve', lambda e: e.tensor_scalar(out=vt[:, 0:256], in0=vs[:, 0:256], scalar1=bkcol[d][:, j:j + 1], scalar2=None, op0=ALU.mult),
                         R=[vs, bkcol[d]], W=[vt])
                    k.op('act', lambda e: e.activation(out=vt[:, 256:257], in_=bkcol[d][:, j:j + 1], func=AF.Identity), R=[bkcol[d]], W=[vt])
                    ps_s = self.psum()
                    for dk in range(2):
                        k.op('pe', lambda e: e.matmul(ps_s[:, 0:128], lhsT=kT[:, dk, t0:t0 + 128], rhs=qT[:, dk, t0:t0 + 128], start=(dk == 0), stop=(dk == 1)),
                             R=[kT, qT], W=[ps_s])
                    k.op('dve', lambda e: e.tensor_tensor(out=smm[:, :], in0=ps_s[:, 0:128], in1=tri[:, :], op=ALU.mult), R=[ps_s, tri], W=[smm])
                    ps_n = self.psum()
                    k.op('pe', lambda e: e.matmul(ps_n[:, 0:257], lhsT=smm[:, :], rhs=vt[:, :], start=True, stop=False), R=[smm, vt], W=[ps_n])
                    for dk in range(2):
                        k.op('pe', lambda e: e.matmul(ps_n[:, 0:257], lhsT=qT[:, dk, t0:t0 + 128], rhs=Cb[:, dk, :], start=False, stop=(dk == 1)),
                             R=[qT, Cb], W=[ps_n])
                    for dk in range(2):
                        ps_u = self.psum()
                        k.op('pe', lambda e: e.matmul(ps_u[:, 0:257], lhsT=kb[:, dk * 128:(dk + 1) * 128], rhs=vt[:, :], start=True, stop=True),
                             R=[kb, vt], W=[ps_u])
                        k.op('dve', lambda e: e.tensor_tensor(out=Ct[:, dk, :], in0=ps_u[:, 0:257], in1=Cf[:, dk, :], op=ALU.add), R=[ps_u, Cf], W=[Ct])
                    k.op('act', lambda e: e.activation(out=Cf[:, :, :], in_=Ct[:, :, :], func=AF.Copy, scale=dec[d][:, j:j + 1]), R=[Ct, dec[d]], W=[Cf])
                    k.op('act', lambda e: e.activation(out=Cb[:, :, :], in_=Ct[:, :, :], func=AF.Copy, scale=dec[d][:, j:j + 1]), R=[Ct, dec[d]], W=[Cb])
                    k.op('act', lambda e: e.activation(out=r1[:, :], in_=ps_n[:, 256:257], func=AF.Abs), R=[ps_n], W=[r1])
                    k.op('dve', lambda e: e.tensor_single_scalar(out=r1[:, :], in_=r1[:, :], scalar=1.0, op=ALU.max), R=[r1], W=[r1])
                    k.op('dve', lambda e: e.reciprocal(out=r1[:, :], in_=r1[:, :]), R=[r1], W=[r1])
                    k.op('act', lambda e: e.activation(out=hs[:, :], in_=ps_n[:, 0:256], func=AF.Copy, scale=r1[:, 0:1]), R=[ps_n, r1], W=[hs])
                    k.dma('sp', HD[d][t0:t0 + 128, :], hs[:, :], R=[hs], W=[HD[d]])
            hA = [k.sb(es, f"m_hA{i}", [128, 256]) for i in range(2)]
            hB = [k.sb(es, f"m_hB{i}", [128, 256]) for i in range(2)]
            oo = [k.sb(es, f"m_oo{i}", [128, 256]) for i in range(2)]
            st6 = [k.sb(es, f"m_st6{i}", [128, 6]) for i in range(2)]
            mv = [k.sb(es, f"m_mv{i}", [128, 2]) for i in range(2)]
            hT = [k.sb(es, f"m_hT{i}", [128, 2, 128], BF16) for i in range(2)]
            for j in range(NCH):
                t0 = j * 128
                a, b, o, s6, m2, ht = hA[j % 2], hB[j % 2], oo[j % 2], st6[j % 2], mv[j % 2], hT[j % 2]
                k.dma('sp', a[:, :], HD[0][t0:t0 + 128, :], R=[HD[0]], W=[a])
                k.dma('sp', b[:, :], HD[1][t0:t0 + 128, :], R=[HD[1]], W=[b])
                k.dma('sp', o[:, :], self.KV[t0:t0 + 128, 512:768], R=[self.KV], W=[o])
                k.op('act', lambda e: e.activation(out=o[:, :], in_=o[:, :], func=AF.Sigmoid), R=[o], W=[o])
                k.op('dve', lambda e: e.tensor_tensor(out=a[:, :], in0=a[:, :], in1=b[:, :], op=ALU.add), R=[a, b], W=[a])
                k.op('dve', lambda e: e.tensor_tensor(out=a[:, :], in0=a[:, :], in1=o[:, :], op=ALU.mult), R=[a, o], W=[a])
                k.op('dve', lambda e: e.bn_stats(out=s6[:, :], in_=a[:, :]), R=[a], W=[s6])
                k.op('dve', lambda e: e.bn_aggr(out=m2[:, :], in_=s6[:, :]), R=[s6], W=[m2])
                self.rstd(m2)
                k.op('dve', lambda e: e.tensor_scalar(out=a[:, :], in0=a[:, :], scalar1=m2[:, 0:1], scalar2=m2[:, 1:2], op0=ALU.subtract, op1=ALU.mult),
                     R=[a, m2], W=[a])
                k.op('dve', lambda e: e.tensor_tensor(out=a[:, :], in0=a[:, :], in1=mgain[:, :], op=ALU.mult), R=[a, mgain], W=[a])
                ps = self.psum()
                for dk in range(2):
                    k.op('pe', lambda e: e.transpose(out=ps[:, dk * 128:(dk + 1) * 128], in_=a[:, dk * 128:(dk + 1) * 128], identity=self.identf[:, :]),
                         R=[a, self.identf], W=[ps])
                self.evac(ht[:, :, :], ps[:, 0:256].rearrange("p (a b) -> p a b", a=2), R=[ps], W=[ht])
                k.dma('sp', self.HN_sh[:, t0:t0 + 128].rearrange("(a p) t -> p a t", p=128), ht[:, :, :], R=[ht], W=[self.HN_sh])
            k.barrier()


_PROGS = {}


def _prog(mode):
    if mode not in _PROGS:
        p = Prog(mode)
        _PROGS[mode] = p.build()
    return _PROGS[mode]


def kernel(**inputs):
    inp = {k_: np.asarray(v) for k_, v in inputs.items()}
    x = np.ascontiguousarray(inp["x"][0]).astype(np.float32)
    cores = list(range(NCORES))
    for l in range(DEPTH):
        xT = np.ascontiguousarray(x.T)
        resA = run_bass_kernel_spmd(_prog('A'), [_inputs_A(inp, l, c, xT) for c in cores], core_ids=cores).results
        hn = np.concatenate([resA[c]["HN_sh"] for c in range(4)], axis=0)
        hy = np.concatenate([resA[c]["HY_sh"] for c in cores], axis=0)
        s5 = np.concatenate([resA[c]["S5_sh"] for c in cores], axis=0)
        del resA
        shared = _shared_B(inp, l)
        in_maps = []
        bcores = list(range(NCB))
        tpb = NBLK * TPC
        for c in bcores:
            ts = slice(c * tpb, (c + 1) * tpb)
            m = dict(shared)
            m["x_tok"] = np.ascontiguousarray(x[ts])
            m["xTs"] = np.ascontiguousarray(xT[:, ts])
            m["hn"] = np.ascontiguousarray(hn[:, ts]); m["hy"] = np.ascontiguousarray(hy[:, ts]); m["s5"] = np.ascontiguousarray(s5[:, ts])
            in_maps.append(m)
        resB = run_bass_kernel_spmd(_prog('B'), in_maps, core_ids=bcores).results
        x = np.concatenate([resB[c]["out"] for c in bcores], axis=0)
        del resB, in_maps, shared
    return np.ascontiguousarray(x[None]).astype(np.float32)
```
